# Optimizing a Trainium2 kernel written in Bass

```python
import math
import jax
import jax.numpy as jnp
from jax import lax
import numpy as np

D_MODEL = 1024
BATCH = 32
SEQ = 256
DEPTH = 4
DEC_BATCH = 8
DEC_SEQ = 2048
PAST_LEN = 256

GRID_W = 64
H_A = 4
HEAD_DIM_A = 128
A_WIDTH = H_A * HEAD_DIM_A
SHORT_CONV = 5
CHUNK = 64
S5_GROUP = 16
S5_STATE = 64
B_WIDTH = 512
S5_GROUPS = B_WIDTH // S5_GROUP
H_C = 8
QK_NOPE = 64
QK_ROPE = 32
V_HEAD = 64
Q_LORA = 384
KV_LORA = 256
C_WIDTH = H_C * V_HEAD
ROPE_BASE = 10000.0
Q_BLOCK = 128
N_BRANCH = 3
BRANCH_WIDTH = 512
D_FF = 2816
FFN_CONV = 3
NORM_EPS = 1e-6
IN_SPLITS = (3 * A_WIDTH, A_WIDTH, 2 * H_A, 2 * H_A, B_WIDTH, Q_LORA, KV_LORA, QK_ROPE, N_BRANCH * D_MODEL)
IN_WIDTH = 4 * A_WIDTH + 4 * H_A + B_WIDTH + Q_LORA + KV_LORA + QK_ROPE + N_BRANCH * D_MODEL

kernel_name = 'hybrid_diffusion_prefix_trunk_step'


def rmsnorm(x, gain):
    xf = x.astype(jnp.float32)
    y = xf * lax.rsqrt(jnp.mean(xf * xf, axis=-1, keepdims=True) + NORM_EPS)
    return y * gain.astype(jnp.float32)


def l2norm(x):
    return x * lax.rsqrt(jnp.sum(x * x, axis=-1, keepdims=True) + NORM_EPS)


def dwconv_centred(x, w):
    width = w.shape[0]
    pad = width // 2
    length = x.shape[1]
    xp = jnp.pad(x, ((0, 0), (pad, pad), (0, 0)))
    return sum(xp[:, i:i + length] * w[i] for i in range(width))


def split_columns(t, widths):
    parts, start = [], 0
    for wd in widths:
        parts.append(t[..., start:start + wd])
        start += wd
    return parts


def modulation(cvec, w_mod, b_mod):
    m = jax.nn.silu(cvec.astype(jnp.float32)) @ w_mod.astype(jnp.float32) + b_mod.astype(jnp.float32)
    return [t[:, None, :] for t in jnp.split(m, 6, axis=-1)]


def axial_rope(length):
    rows = length // GRID_W
    row = jnp.repeat(jnp.arange(rows, dtype=jnp.float32), GRID_W)
    col = (jnp.arange(length) % GRID_W).astype(jnp.float32)
    n_freq = QK_ROPE // 4
    inv_freq = 1.0 / (ROPE_BASE ** (jnp.arange(n_freq, dtype=jnp.float32) / n_freq))
    ang = jnp.concatenate([row[:, None] * inv_freq, col[:, None] * inv_freq], axis=-1)
    return jnp.cos(ang), jnp.sin(ang)


def apply_rope(x, cos, sin):
    half = x.shape[-1] // 2
    x1, x2 = x[..., :half], x[..., half:]
    return jnp.concatenate([x1 * cos - x2 * sin, x2 * cos + x1 * sin], axis=-1)


def chunk_gated_delta(q, k, v, g, beta, s0):
    bsz, length, nh, dk = q.shape
    dv = v.shape[-1]
    n = length // CHUNK

    def chunks(t):
        return t.reshape(bsz, n, CHUNK, nh, t.shape[-1]).transpose(1, 0, 3, 2, 4)

    qc, kc, vc = chunks(q * dk ** -0.5), chunks(k), chunks(v)
    gc = jnp.cumsum(chunks(g[..., None])[..., 0], axis=-1)
    bc = chunks(beta[..., None])
    lower = jnp.tril(jnp.ones((CHUNK, CHUNK), dtype=bool))
    diff = gc[..., :, None] - gc[..., None, :]
    decay = jnp.where(lower, jnp.exp(jnp.where(lower, diff, 0.0)), 0.0)
    kb = kc * bc
    a_strict = jnp.einsum('nbhcd,nbhed->nbhce', kb, kc) * decay
    rhs = jnp.concatenate([vc * bc, kb * jnp.exp(gc)[..., None]], axis=-1)
    sol = lax.linalg.triangular_solve(a_strict, rhs, left_side=True, lower=True, unit_diagonal=True)
    u, w = sol[..., :dv], sol[..., dv:]
    qk = jnp.einsum('nbhcd,nbhed->nbhce', qc, kc) * decay

    def step(state, inp):
        q_i, k_i, u_i, w_i, g_i, qk_i = inp
        v_new = u_i - jnp.einsum('bhcd,bhdv->bhcv', w_i, state)
        o_i = (jnp.einsum('bhcd,bhdv->bhcv', q_i * jnp.exp(g_i)[..., None], state)
               + jnp.einsum('bhce,bhev->bhcv', qk_i, v_new))
        g_last = g_i[..., -1:]
        state = (state * jnp.exp(g_last)[..., None]
                 + jnp.einsum('bhcd,bhcv->bhdv', k_i * jnp.exp(g_last - g_i)[..., None], v_new))
        return state, o_i

    s_final, o = lax.scan(step, s0, (qc, kc, u, w, gc, qk))
    o = o.transpose(1, 0, 3, 2, 4).reshape(bsz, length, nh, dv)
    return o, s_final


def gated_deltanet(qkv, z, beta_logit, alpha_logit, conv_w, a_log, dt_bias, g_out, s0):
    f32 = jnp.float32
    bsz, length, _ = qkv.shape
    qkv = jax.nn.silu(dwconv_centred(qkv.astype(f32), conv_w.astype(f32)))
    q, k, v = jnp.split(qkv, 3, axis=-1)

    def heads(t):
        return t.reshape(bsz, length, H_A, HEAD_DIM_A)

    q, k, v = l2norm(heads(q)), l2norm(heads(k)), heads(v)
    beta = jax.nn.sigmoid(beta_logit.astype(f32))
    g = -jnp.exp(a_log.astype(f32)) * jax.nn.softplus(alpha_logit.astype(f32) + dt_bias.astype(f32))

    def flip(t):
        return jnp.flip(t, axis=1)

    o_f, s_f = chunk_gated_delta(q, k, v, g[:, :, 0], beta[:, :, 0], s0[:, 0])
    o_b, s_b = chunk_gated_delta(flip(q), flip(k), flip(v), flip(g[:, :, 1]), flip(beta[:, :, 1]), s0[:, 1])
    o = rmsnorm(o_f + flip(o_b), g_out) * jax.nn.silu(heads(z.astype(f32)))
    return o.reshape(bsz, length, A_WIDTH), jnp.stack([s_f, s_b], axis=1)


def s5_diag_scan(u, lam_bar, b_bar, h0):
    bu = jnp.einsum('blgc,gpc->blgp', u.astype(jnp.complex64), b_bar)
    bu = bu.at[:, 0].add(lam_bar * h0)
    a = jnp.broadcast_to(lam_bar, bu.shape)

    def combine(e1, e2):
        a1, b1 = e1
        a2, b2 = e2
        return a1 * a2, a2 * b1 + b2

    _, h = lax.associative_scan(combine, (a, bu), axis=1)
    return h


def s5_mixer(u, lam_re, lam_im, log_dt, b_re, b_im, c_re, c_im, d_skip, w_glu, b_glu, h0):
    f32 = jnp.float32
    bsz, length, _ = u.shape
    uf = u.astype(f32).reshape(bsz, length, S5_GROUPS, S5_GROUP)
    lam = lax.complex(lam_re.astype(f32), lam_im.astype(f32))
    lam_bar = jnp.exp(lam * jnp.exp(log_dt.astype(f32))[..., None])
    b_bar = ((lam_bar - 1.0) / lam)[..., None] * lax.complex(b_re.astype(f32), b_im.astype(f32))

    def flip(t):
        return jnp.flip(t, axis=1)

    h_f = s5_diag_scan(uf, lam_bar[0], b_bar[0], h0[:, 0])
    h_b = flip(s5_diag_scan(flip(uf), lam_bar[1], b_bar[1], h0[:, 1]))
    c_mat = lax.complex(c_re.astype(f32), c_im.astype(f32))
    y = (jnp.real(jnp.einsum('blgp,gcp->blgc', h_f + h_b, c_mat))
         + d_skip.astype(f32).reshape(S5_GROUPS, S5_GROUP) * uf)
    y = jax.nn.gelu(y.reshape(bsz, length, B_WIDTH))
    y = y * jax.nn.sigmoid(y @ w_glu.astype(f32) + b_glu.astype(f32))
    final = jnp.stack([h_f[:, -1], h_b[:, 0]], axis=1)
    return y, final


def mla_keys(ckv, k_rope, w_kv_b):
    bsz, length, _ = ckv.shape
    kv = (ckv @ w_kv_b).reshape(bsz, length, H_C, QK_NOPE + V_HEAD)
    k_nope, v = kv[..., :QK_NOPE], kv[..., QK_NOPE:]
    k_r = jnp.broadcast_to(k_rope[:, :, None, :], (bsz, length, H_C, QK_ROPE))
    return jnp.concatenate([k_nope, k_r], axis=-1), v


def blocked_attention(q, k, v):
    bsz, lq, nh, dqk = q.shape
    nb = lq // Q_BLOCK
    qb = q.reshape(bsz, nb, Q_BLOCK, nh, dqk).transpose(1, 0, 2, 3, 4)
    scale = dqk ** -0.5

    def one_block(q_blk):
        s = jnp.einsum('bqhd,bkhd->bhqk', q_blk, k) * scale
        p = jax.nn.softmax(s, axis=-1)
        return jnp.einsum('bhqk,bkhv->bqhv', p, v)

    o = lax.map(one_block, qb)
    return o.transpose(1, 0, 2, 3, 4).reshape(bsz, lq, nh, v.shape[-1])


def mla_mixer(qa, kva, krope, g_q_a, w_q_b, g_kv_a, w_kv_b, cache):
    f32 = jnp.float32
    bsz, length, _ = qa.shape
    q = (rmsnorm(qa, g_q_a) @ w_q_b.astype(f32)).reshape(bsz, length, H_C, QK_NOPE + QK_ROPE)
    ckv = rmsnorm(kva, g_kv_a)
    krope = krope.astype(f32)
    w_kv_b = w_kv_b.astype(f32)
    if cache is None:
        k, v = mla_keys(ckv, krope, w_kv_b)
    else:
        cache_ckv, cache_krope = cache
        cos, sin = axial_rope(length)
        q = jnp.concatenate([q[..., :QK_NOPE], apply_rope(q[..., QK_NOPE:], cos[:, None], sin[:, None])], axis=-1)
        k_lat, v_lat = mla_keys(ckv, apply_rope(krope, cos, sin), w_kv_b)
        k_ctx, v_ctx = mla_keys(cache_ckv.astype(f32), cache_krope.astype(f32), w_kv_b)
        k = jnp.concatenate([k_ctx, k_lat], axis=1)
        v = jnp.concatenate([v_ctx, v_lat], axis=1)
    o = blocked_attention(q, k, v)
    return o.reshape(bsz, length, C_WIDTH), ckv, krope


def token_mixers(h, lp, ctx):
    f32 = jnp.float32
    bsz, length, _ = h.shape
    proj = h @ lp['w_in'].astype(f32)
    a_qkv, a_z, a_beta, a_alpha, b_u, c_qa, c_kva, c_kr, gate_logit = split_columns(proj, IN_SPLITS)
    if ctx is None:
        s0 = jnp.zeros((bsz, 2, H_A, HEAD_DIM_A, HEAD_DIM_A), f32)
        h0 = jnp.zeros((bsz, 2, S5_GROUPS, S5_STATE), jnp.complex64)
        mla_cache = None
    else:
        st_delta, st_re, st_im, c_ckv, c_krope = ctx
        s0 = st_delta.astype(f32)
        h0 = lax.complex(st_re.astype(f32), st_im.astype(f32))
        mla_cache = (c_ckv, c_krope)
    o_a, s_delta = gated_deltanet(a_qkv, a_z, a_beta.reshape(bsz, length, 2, H_A),
                                  a_alpha.reshape(bsz, length, 2, H_A), lp['conv_qkv'], lp['a_log'],
                                  lp['dt_bias'], lp['g_delta_out'], s0)
    o_b, s_s5 = s5_mixer(b_u, lp['s5_lam_re'], lp['s5_lam_im'], lp['s5_log_dt'], lp['s5_b_re'], lp['s5_b_im'],
                         lp['s5_c_re'], lp['s5_c_im'], lp['s5_d'], lp['w_glu'], lp['b_glu'], h0)
    o_c, ckv, krope = mla_mixer(c_qa, c_kva, c_kr, lp['g_q_a'], lp['w_q_b'], lp['g_kv_a'], lp['w_kv_b'], mla_cache)
    branches = jnp.stack([o_a, o_b, o_c], axis=2)
    per_branch = jnp.einsum('blnw,nwd->blnd', branches, lp['w_branch'].astype(f32))
    gates = jax.nn.sigmoid(gate_logit.astype(f32)).reshape(bsz, length, N_BRANCH, D_MODEL)
    out = jnp.sum(gates * per_branch, axis=2) @ lp['w_out'].astype(f32)
    if ctx is None:
        return out, (s_delta, jnp.real(s_s5), jnp.imag(s_s5), ckv, krope)
    return out, None


def conv_mlp(h, w_up, conv_w, conv_b, w_down):
    f32 = jnp.float32
    up = dwconv_centred(h @ w_up.astype(f32), conv_w.astype(f32)) + conv_b.astype(f32)
    gate, val = jnp.split(up, 2, axis=-1)
    return (jax.nn.silu(gate) * val) @ w_down.astype(f32)


def trunk_layer(x, mods, lp, ctx):
    shift_m, scale_m, gate_m, shift_f, scale_f, gate_f = mods
    h = rmsnorm(x, lp['g_norm_mix']) * (1.0 + scale_m) + shift_m
    mix, ctx_tensors = token_mixers(h, lp, ctx)
    x = x + gate_m * mix
    h = rmsnorm(x, lp['g_norm_ffn']) * (1.0 + scale_f) + shift_f
    x = x + gate_f * conv_mlp(h, lp['w_ffn_up'], lp['conv_ffn'], lp['b_conv_ffn'], lp['w_ffn_down'])
    return x, ctx_tensors


def setup_inputs(seed: int = 0) -> dict:
    key = jax.random.key(seed)
    keys = iter(jax.random.split(key, 48))
    f32 = jnp.float32

    def nrm(shape, scale):
        return scale * jax.random.normal(next(keys), shape, f32)

    def unif(shape, lo, hi):
        return jax.random.uniform(next(keys), shape, f32, lo, hi)

    G, P, CG = S5_GROUPS, S5_STATE, S5_GROUP
    dt_a = jnp.exp(unif((DEPTH, 2, H_A), math.log(1e-3), math.log(1e-1)))
    return {
        'x_prompt': nrm((BATCH, SEQ, D_MODEL), 1.0),
        'x_sample': nrm((DEC_BATCH, DEC_SEQ, D_MODEL), 1.0),
        'state_delta': nrm((DEC_BATCH, DEPTH, 2, H_A, HEAD_DIM_A, HEAD_DIM_A), 0.1),
        'state_s5_re': nrm((DEC_BATCH, DEPTH, 2, G, P), 0.05),
        'state_s5_im': nrm((DEC_BATCH, DEPTH, 2, G, P), 0.05),
        'cache_ckv': nrm((DEC_BATCH, DEPTH, PAST_LEN, KV_LORA), 1.0),
        'cache_krope': nrm((DEC_BATCH, DEPTH, PAST_LEN, QK_ROPE), 1.0),
        'c': nrm((DEC_BATCH, D_MODEL), 1.0),
        'c_ctx': nrm((D_MODEL,), 1.0),
        'w_mod': nrm((DEPTH, D_MODEL, 6 * D_MODEL), 0.5 * D_MODEL ** -0.5),
        'b_mod': nrm((DEPTH, 6 * D_MODEL), 0.02),
        'g_norm_mix': 1.0 + nrm((DEPTH, D_MODEL), 0.02),
        'g_norm_ffn': 1.0 + nrm((DEPTH, D_MODEL), 0.02),
        'w_in': nrm((DEPTH, D_MODEL, IN_WIDTH), D_MODEL ** -0.5),
        'conv_qkv': nrm((DEPTH, SHORT_CONV, 3 * A_WIDTH), SHORT_CONV ** -0.5),
        'a_log': jnp.log(unif((DEPTH, 2, H_A), 1.0, 16.0)),
        'dt_bias': dt_a + jnp.log(-jnp.expm1(-dt_a)),
        'g_delta_out': 1.0 + nrm((DEPTH, HEAD_DIM_A), 0.02),
        's5_lam_re': -0.5 + nrm((DEPTH, 2, G, P), 0.01),
        's5_lam_im': math.pi * jnp.arange(P, dtype=f32) + nrm((DEPTH, 2, G, P), 0.01),
        's5_log_dt': unif((DEPTH, 2, G), math.log(1e-3), math.log(1e-1)),
        's5_b_re': nrm((DEPTH, G, P, CG), (2 * CG) ** -0.5),
        's5_b_im': nrm((DEPTH, G, P, CG), (2 * CG) ** -0.5),
        's5_c_re': nrm((DEPTH, G, CG, P), P ** -0.5),
        's5_c_im': nrm((DEPTH, G, CG, P), P ** -0.5),
        's5_d': nrm((DEPTH, B_WIDTH), 1.0),
        'w_glu': nrm((DEPTH, B_WIDTH, B_WIDTH), B_WIDTH ** -0.5),
        'b_glu': nrm((DEPTH, B_WIDTH), 0.02),
        'g_q_a': 1.0 + nrm((DEPTH, Q_LORA), 0.02),
        'w_q_b': nrm((DEPTH, Q_LORA, H_C * (QK_NOPE + QK_ROPE)), Q_LORA ** -0.5),
        'g_kv_a': 1.0 + nrm((DEPTH, KV_LORA), 0.02),
        'w_kv_b': nrm((DEPTH, KV_LORA, H_C * (QK_NOPE + V_HEAD)), KV_LORA ** -0.5),
        'w_branch': nrm((DEPTH, N_BRANCH, BRANCH_WIDTH, D_MODEL), BRANCH_WIDTH ** -0.5),
        'w_out': nrm((DEPTH, D_MODEL, D_MODEL), D_MODEL ** -0.5),
        'w_ffn_up': nrm((DEPTH, D_MODEL, 2 * D_FF), D_MODEL ** -0.5),
        'conv_ffn': nrm((DEPTH, FFN_CONV, 2 * D_FF), FFN_CONV ** -0.5),
        'b_conv_ffn': nrm((DEPTH, 2 * D_FF), 0.02),
        'w_ffn_down': nrm((DEPTH, D_FF, D_MODEL), D_FF ** -0.5),
        'g_final': 1.0 + nrm((D_MODEL,), 0.02),
    }


def reference(x_prompt, x_sample, state_delta, state_s5_re, state_s5_im, cache_ckv, cache_krope, c, c_ctx,
              w_mod, b_mod, g_norm_mix, g_norm_ffn, w_in, conv_qkv, a_log, dt_bias, g_delta_out,
              s5_lam_re, s5_lam_im, s5_log_dt, s5_b_re, s5_b_im, s5_c_re, s5_c_im, s5_d, w_glu, b_glu,
              g_q_a, w_q_b, g_kv_a, w_kv_b, w_branch, w_out, w_ffn_up, conv_ffn, b_conv_ffn, w_ffn_down,
              g_final):
    xp = x_prompt.astype(jnp.float32)
    xs = x_sample.astype(jnp.float32)
    deltas, s5_res, s5_ims, ckvs, kropes = [], [], [], [], []
    for l in range(DEPTH):
        lp = {
            'g_norm_mix': g_norm_mix[l], 'g_norm_ffn': g_norm_ffn[l], 'w_in': w_in[l],
            'conv_qkv': conv_qkv[l], 'a_log': a_log[l], 'dt_bias': dt_bias[l], 'g_delta_out': g_delta_out[l],
            's5_lam_re': s5_lam_re[l], 's5_lam_im': s5_lam_im[l], 's5_log_dt': s5_log_dt[l],
            's5_b_re': s5_b_re[l], 's5_b_im': s5_b_im[l], 's5_c_re': s5_c_re[l], 's5_c_im': s5_c_im[l],
            's5_d': s5_d[l], 'w_glu': w_glu[l], 'b_glu': b_glu[l],
            'g_q_a': g_q_a[l], 'w_q_b': w_q_b[l], 'g_kv_a': g_kv_a[l], 'w_kv_b': w_kv_b[l],
            'w_branch': w_branch[l], 'w_out': w_out[l],
            'w_ffn_up': w_ffn_up[l], 'conv_ffn': conv_ffn[l], 'b_conv_ffn': b_conv_ffn[l],
            'w_ffn_down': w_ffn_down[l],
        }
        mods_p = modulation(c_ctx[None, :], w_mod[l], b_mod[l])
        xp, ctx_out = trunk_layer(xp, mods_p, lp, None)
        deltas.append(ctx_out[0])
        s5_res.append(ctx_out[1])
        s5_ims.append(ctx_out[2])
        ckvs.append(ctx_out[3])
        kropes.append(ctx_out[4])
        mods_s = modulation(c, w_mod[l], b_mod[l])
        cache_l = (state_delta[:, l], state_s5_re[:, l], state_s5_im[:, l], cache_ckv[:, l], cache_krope[:, l])
        xs, _ = trunk_layer(xs, mods_s, lp, cache_l)
    y_prompt = rmsnorm(xp, g_final)
    y_sample = rmsnorm(xs, g_final)
    new_state_delta = jnp.stack(deltas, axis=1)
    new_state_s5_re = jnp.stack(s5_res, axis=1)
    new_state_s5_im = jnp.stack(s5_ims, axis=1)
    new_cache_ckv = jnp.stack(ckvs, axis=1)
    new_cache_krope = jnp.stack(kropes, axis=1)
    return (y_prompt, y_sample, new_state_delta, new_state_s5_re, new_state_s5_im, new_cache_ckv, new_cache_krope)
```

```python
import math
import numpy as np
import concourse.bass as bass
import concourse.mybir as mybir
from concourse.bass_utils import run_bass_kernel_spmd

F32 = mybir.dt.float32
BF16 = mybir.dt.bfloat16
I32 = mybir.dt.int32
AF = mybir.ActivationFunctionType
ALU = mybir.AluOpType
AX = mybir.AxisListType

DEPTH = 4
NPS, LP, LS = 4, 256, 2048
NT = NPS * LP + LS
SEQS = [(i * LP, LP, 0) for i in range(NPS)] + [(NPS * LP, LS, 1)]
D = 1024
EPS = 1e-6
D_FF = 2816
NFO = 51
PW = NFO * 128
R_QKV, R_Z, R_U, R_QA, R_KVA, R_GATE, R_MA, R_MB = 0, 1536, 2048, 2560, 2944, 3200, 6272, 6400

DMA_QUEUES = ("sync", "gpsimd", "scalar")
COMPUTE = ("tensor", "vector", "scalar", "gpsimd")
N_DMA_SEMS = 12
ARENA_WORDS = 52600


def dsize(dt):
    return 2 if dt == BF16 else 4


class Buf:
    __slots__ = ("name", "w", "r", "t", "psum", "g")

    def __init__(self, name, t=None, psum=False):
        self.name = name
        self.w = {}
        self.r = {}
        self.g = []
        self.t = t
        self.psum = psum

    def __getitem__(self, idx):
        return V(self, self.t[idx])

    @property
    def v(self):
        return V(self, self.t)


class V:
    __slots__ = ("buf", "ap")

    def __init__(self, buf, ap):
        self.buf = buf
        self.ap = ap

    def __getitem__(self, idx):
        return V(self.buf, self.ap[idx])

    def rearrange(self, pat, **kw):
        return V(self.buf, self.ap.rearrange(pat, **kw))

    def bitcast(self, dt):
        return V(self.buf, self.ap.bitcast(dt))

    def bc(self, shape):
        return V(self.buf, self.ap.broadcast_to(list(shape)))

    def unsq(self, ax):
        return V(self.buf, self.ap.unsqueeze(ax))

    def pbc(self, n):
        return V(self.buf, self.ap.partition_broadcast(n))

    @property
    def shape(self):
        return self.ap.shape


class Op:
    __slots__ = ("eng", "fn", "deps", "dma", "idx", "sig", "sem", "val", "prev_val", "slot")

    def __init__(self, eng, fn, dma):
        self.eng = eng
        self.fn = fn
        self.dma = dma
        self.deps = set()
        self.sig = False
        self.sem = None
        self.val = 0
        self.prev_val = 0


def _ap(x):
    return x.ap if isinstance(x, V) else x


class FW:
    def __init__(self, nc):
        self.nc = nc
        self.ops = []
        self.streams = {e: [] for e in ("tensor", "vector", "scalar", "gpsimd", "sync")}
        self.arena = nc.alloc_sbuf_tensor("arena", [128, ARENA_WORDS], F32)
        self.top = 0
        self.psb = [Buf("psb%d" % i, nc.alloc_psum_tensor("psb%d" % i, [128, 512], F32), psum=True) for i in range(8)]
        self.psi = 0
        self.ps_set = list(range(8))
        self.dma_since = []
        self.pending = {e: [] for e in self.streams}
        self.rr = 0
        self.drr = {q: 0 for q in DMA_QUEUES}

    def alloc(self, name, shape, dtype=F32):
        P = shape[0]
        free = 1
        for s in shape[1:]:
            free *= s
        words = (free * dsize(dtype) + 3) // 4
        words = (words + 7) // 8 * 8
        off = self.top
        self.top += words
        assert self.top <= ARENA_WORDS, "SBUF arena overflow %s %d" % (name, self.top)
        ap = self.arena[0:P, off:off + words]
        if dtype != F32:
            ap = ap.bitcast(dtype)
        ap = ap[:, 0:free]
        if len(shape) >= 3:
            names = ["d%d" % i for i in range(len(shape) - 1)]
            pat = "p (%s) -> p %s" % (" ".join(names), " ".join(names))
            kw = {names[i]: shape[i + 1] for i in range(1, len(names))}
            ap = ap.rearrange(pat, **kw)
        return Buf(name, ap)

    def mark(self):
        return self.top

    def release(self, m):
        self.top = m
        self.barrier()

    def ps(self):
        st = self.ps_set
        b = self.psb[st[self.psi % len(st)]]
        self.psi += 1
        return b

    def dram(self, name, shape, dtype, kind="Internal"):
        t = self.nc.dram_tensor(name, list(shape), dtype, kind=kind)
        return Buf(name, t.ap())

    def barrier(self):
        bar = []
        for e, st in self.streams.items():
            for o in reversed(st):
                if not o.dma:
                    bar.append(o.idx)
                    break
        bar.extend(self.dma_since)
        self.dma_since = []
        for e in self.pending:
            self.pending[e] = list(bar)

    def op(self, eng, fn, reads=(), writes=(), dma=False):
        o = Op(eng, fn, dma)
        o.idx = len(self.ops)
        self.ops.append(o)
        self.streams[eng].append(o)
        if dma:
            o.slot = self.drr[eng] % N_DMA_SEMS
            self.drr[eng] += 1
            key = (eng, o.slot)
        else:
            key = eng
        if self.pending[eng]:
            o.deps.update(self.pending[eng])
            self.pending[eng] = []
        for b in reads:
            o.deps.update(b.w.values())
        for b in writes:
            if b.r:
                b.g = list(b.r.values()) + list(b.w.values())
                b.w = {}
                b.r = {}
            o.deps.update(b.g)
            if b.psum:
                o.deps.update(b.w.values())
            elif key in b.w:
                o.deps.add(b.w[key])
        o.deps.discard(o.idx)
        for b in writes:
            b.w[key] = o.idx
        for b in reads:
            if b not in writes:
                b.r[key] = o.idx
        if dma:
            self.dma_since.append(o.idx)
        return o.idx

    def _rw(self, outs, ins):
        w = [x.buf for x in outs if isinstance(x, V)]
        r = [x.buf for x in ins if isinstance(x, V)]
        w = w + [b for b in r if b.psum]
        return r, w

    def dma(self, q, out, in_, slow=False):
        r, w = self._rw([out], [in_])
        o, i = _ap(out), _ap(in_)
        if slow:
            return self.op(q, lambda e: e.dma_start(out=o, in_=i, allow_slow_non_contiguous=True), r, w, dma=True)
        return self.op(q, lambda e: e.dma_start(out=o, in_=i), r, w, dma=True)

    def matmul(self, out, lhsT, rhs, start=True, stop=True, **kw):
        r, w = self._rw([out], [lhsT, rhs])
        if not start:
            r = r + [out.buf]
        o, a, b = _ap(out), _ap(lhsT), _ap(rhs)
        return self.op("tensor", lambda e: e.matmul(o, lhsT=a, rhs=b, start=start, stop=stop, **kw), r, w)

    def transpose(self, out, in_, ident):
        r, w = self._rw([out], [in_, ident])
        o, a, b = _ap(out), _ap(in_), _ap(ident)
        return self.op("tensor", lambda e: e.transpose(o, a, b), r, w)

    def act(self, out, in_, func, bias=None, scale=None, accum_out=None, eng="scalar"):
        r, w = self._rw([out, accum_out], [in_, bias, scale])
        kw = {}
        if bias is not None:
            kw["bias"] = _ap(bias)
        if scale is not None:
            kw["scale"] = _ap(scale)
        if accum_out is not None:
            kw["accum_out"] = _ap(accum_out)
        o, i = _ap(out), _ap(in_)
        return self.op("scalar", lambda e: e.activation(out=o, in_=i, func=func, **kw), r, w)

    def tt(self, eng, out, in0, in1, op):
        r, w = self._rw([out], [in0, in1])
        o, a, b = _ap(out), _ap(in0), _ap(in1)
        return self.op(eng, lambda e: e.tensor_tensor(out=o, in0=a, in1=b, op=op), r, w)

    def ts(self, eng, out, in0, s1, s2=None, op0=ALU.mult, op1=None):
        eng = "vector"
        r, w = self._rw([out], [in0, s1, s2])
        o, a, x1, x2 = _ap(out), _ap(in0), _ap(s1), _ap(s2)
        if op1 is None:
            return self.op(eng, lambda e: e.tensor_scalar(out=o, in0=a, scalar1=x1, scalar2=None, op0=op0), r, w)
        return self.op(eng, lambda e: e.tensor_scalar(out=o, in0=a, scalar1=x1, scalar2=x2, op0=op0, op1=op1), r, w)

    def stt(self, eng, out, in0, scalar, in1, op0, op1):
        eng = "vector"
        r, w = self._rw([out], [in0, scalar, in1])
        o, a, s, b = _ap(out), _ap(in0), _ap(scalar), _ap(in1)
        return self.op(eng, lambda e: e.scalar_tensor_tensor(out=o, in0=a, scalar=s, in1=b, op0=op0, op1=op1), r, w)

    def copy(self, eng, out, in_):
        r, w = self._rw([out], [in_])
        o, a = _ap(out), _ap(in_)
        if eng == "scalar":
            return self.op(eng, lambda e: e.activation(out=o, in_=a, func=AF.Copy), r, w)
        return self.op(eng, lambda e: e.tensor_copy(out=o, in_=a), r, w)

    def memset(self, eng, out, val):
        r, w = self._rw([out], [])
        r = [out.buf]
        o = _ap(out)
        i = self.op(eng, lambda e: e.memset(o, val), r, w)
        out.buf.r[("ms", eng)] = i
        return i

    def recip(self, out, in_, eng="vector"):
        r, w = self._rw([out], [in_])
        o, a = _ap(out), _ap(in_)
        return self.op(eng, lambda e: e.reciprocal(out=o, in_=a), r, w)

    def rmax(self, out, in_, eng="vector"):
        r, w = self._rw([out], [in_])
        o, a = _ap(out), _ap(in_)
        return self.op(eng, lambda e: e.reduce_max(out=o, in_=a, axis=AX.X), r, w)

    def scan(self, out, d0, d1, initial, op0=ALU.mult, op1=ALU.add):
        r, w = self._rw([out], [d0, d1, initial])
        o, a, b, i = _ap(out), _ap(d0), _ap(d1), _ap(initial)
        return self.op("vector", lambda e: e.tensor_tensor_scan(out=o, data0=a, data1=b, initial=i, op0=op0, op1=op1), r, w)

    def any2(self):
        self.rr += 1
        return ("vector", "gpsimd")[self.rr % 2]

    def any3(self):
        self.rr += 1
        return ("vector", "gpsimd", "scalar")[self.rr % 3]

    def evac(self):
        self.rr += 1
        return ("vector", "scalar")[self.rr % 2]

    def emit(self):
        nc = self.nc
        ops = self.ops
        for o in ops:
            for d in list(o.deps):
                do = ops[d]
                if (not do.dma) and (not o.dma) and do.eng == o.eng and o.eng == "tensor":
                    o.deps.discard(d)
                    continue
                do.sig = True
        sems = {e: nc.alloc_semaphore("s_" + e) for e in COMPUTE}
        dsems = {q: [nc.alloc_semaphore("d_%s_%d" % (q, i)) for i in range(N_DMA_SEMS)] for q in DMA_QUEUES}
        cnt = {e: 0 for e in COMPUTE}
        dcnt = {q: [0] * N_DMA_SEMS for q in DMA_QUEUES}
        for o in ops:
            if o.dma:
                j = o.slot
                o.sem = dsems[o.eng][j]
                o.prev_val = 16 * dcnt[o.eng][j]
                dcnt[o.eng][j] += 1
                o.val = 16 * dcnt[o.eng][j]
            elif o.sig:
                cnt[o.eng] += 1
                o.sem = sems[o.eng]
                o.val = cnt[o.eng]

        def run_stream(ename, eng, final=False):
            seen = {}
            for o in self.streams[ename]:
                waits = {}
                for d in o.deps:
                    do = ops[d]
                    k = id(do.sem)
                    if seen.get(k, 0) >= do.val:
                        continue
                    if k not in waits or waits[k][1] < do.val:
                        waits[k] = (do.sem, do.val)
                if o.dma and o.prev_val > 0:
                    k = id(o.sem)
                    if seen.get(k, 0) < o.prev_val and (k not in waits or waits[k][1] < o.prev_val):
                        waits[k] = (o.sem, o.prev_val)
                for k, (s, v) in waits.items():
                    eng.wait_ge(s, v)
                    seen[k] = v
                ins = o.fn(eng)
                if o.dma:
                    ins.then_inc(o.sem, 16)
                elif o.sig:
                    ins.then_inc(o.sem, 1)
            if final:
                for q in DMA_QUEUES:
                    for j in range(N_DMA_SEMS):
                        if dcnt[q][j] > 0:
                            eng.wait_ge(dsems[q][j], 16 * dcnt[q][j])
                for e in COMPUTE:
                    if cnt[e] > 0:
                        eng.wait_ge(sems[e], cnt[e])

        with nc.Block() as block:
            @block.sync
            def _(e):
                run_stream("sync", e, final=True)

            @block.tensor
            def _(e):
                run_stream("tensor", e)

            @block.vector
            def _(e):
                run_stream("vector", e)

            @block.scalar
            def _(e):
                run_stream("scalar", e)

            @block.gpsimd
            def _(e):
                run_stream("gpsimd", e)


class Ctx:
    pass


def cast_load(fw, C, dst, src, shape, q=None):
    st = C.stg[C.stg_i % len(C.stg)]
    C.stg_i += 1
    n = 1
    for s in shape[1:]:
        n *= s
    sv = st[:, 0:n]
    if len(shape) == 3:
        sv = sv.rearrange("p (a b) -> p a b", b=shape[2])
    fw.dma(("sync", "gpsimd")[C.stg_i % 2] if q is None else q, sv, src)
    fw.copy(fw.any3(), dst, sv)


def load_colvec(fw, C, dst, src_vec, J, tmpb):
    fw.dma("sync", tmpb[0:J, :], src_vec.rearrange("(j p) -> j p", p=128))
    ps = fw.ps()
    fw.transpose(ps[:, 0:J], tmpb[0:J, :], C.ident[0:J, 0:J])
    fw.copy("vector", dst, ps[:, 0:J])


def rms_rstd(fw, C, chunks, nfeat, N, sq, rstd, tmp):
    ps = fw.ps()
    nchunk = len(chunks)
    for c, xc in enumerate(chunks):
        fw.act(sq[:, c, 0:N], xc, AF.Square)
    for c in range(nchunk):
        fw.matmul(ps[:, 0:N], C.ones_bf[:, :], sq[:, c, 0:N], start=(c == 0), stop=(c == nchunk - 1))
    fw.act(tmp[:, 0:N], ps[:, 0:N], AF.Sqrt, scale=1.0 / nfeat, bias=C.eps_col[:, 0:1])
    fw.recip(rstd[:, 0:N], tmp[:, 0:N])
    return rstd


def stage_in(fw, C):
    m = fw.mark()
    xin = [fw.alloc("xin%d" % i, [128, 1024]) for i in range(2)]
    xo = [fw.alloc("xo%d" % i, [128, 8, 128]) for i in range(2)]
    xTv = C.xT.v.rearrange("(c p) t -> p c t", p=128)
    for tt in range(NT // 128):
        a = xin[tt % 2]
        o = xo[tt % 2]
        fw.dma("sync", a[:, :], C.x_tok[tt * 128:(tt + 1) * 128, :])
        for half in range(2):
            ps = fw.ps()
            for c in range(4):
                fw.transpose(ps[:, c * 128:(c + 1) * 128], a[:, (half * 4 + c) * 128:(half * 4 + c + 1) * 128], C.ident[:, :])
            fw.copy(fw.evac(), o[:, half * 4:(half + 1) * 4, :], ps[:, :].rearrange("p (c t) -> p c t", t=128))
        fw.dma("gpsimd", xTv[:, :, tt * 128:(tt + 1) * 128], o[:, :, :])
    fw.release(m)


def stage_mods(fw, C, l):
    m = fw.mark()
    wt = [fw.alloc("wmod%d" % i, [128, 8, 128]) for i in range(3)]
    raw = fw.alloc("modraw", [128, 48, 2])
    bm = fw.alloc("bmod", [128, 48])
    g1 = fw.alloc("gmix", [128, 8])
    g2 = fw.alloc("gffn", [128, 8])
    wv = C.w_mod[l].rearrange("(k p) f -> p k f", p=128)
    tb = [fw.alloc("tb%d" % i, [128, 128]) for i in range(3)]
    load_colvec(fw, C, bm[:, :], C.b_mod[l], 48, tb[0])
    load_colvec(fw, C, g1[:, :], C.g_norm_mix[l], 8, tb[1])
    load_colvec(fw, C, g2[:, :], C.g_norm_ffn[l], 8, tb[2])
    ps = fw.ps()
    for fo in range(48):
        w = wt[fo % 3]
        fw.dma(("sync", "gpsimd")[fo % 2], w[:, :, :], wv[:, :, fo * 128:(fo + 1) * 128])
        for k in range(8):
            fw.matmul(ps[:, fo * 2:fo * 2 + 2], w[:, k, :], C.cT[:, k, :], start=(k == 0), stop=(k == 7))
    fw.tt("vector", raw[:, :, :], ps[:, 0:96].rearrange("p (j n) -> p j n", n=2), bm[:, :].unsq(2).bc([128, 48, 2]), ALU.add)
    for (A, B, G, g, base) in ((C.mA_m, C.mB_m, C.mG_m, g1, 0), (C.mA_f, C.mB_f, C.mG_f, g2, 24)):
        fw.ts("vector", A[:, :, :], raw[:, base + 8:base + 16, :], 1.0, None, op0=ALU.add)
        fw.tt("vector", A[:, :, :], A[:, :, :], g[:, :].unsq(2).bc([128, 8, 2]), ALU.mult)
        fw.copy("vector", B[:, :, :], raw[:, base:base + 8, :])
        fw.copy("vector", G[:, :, :], raw[:, base + 16:base + 24, :])
    fw.release(m)


def load_win(fw, C, l, Win):
    wv = C.w_in[l].rearrange("(k p) f -> p k f", p=128)
    segs = [(R_QKV, 0, 1536), (R_Z, 1536, 512), (R_U, 2064, 512), (R_QA, 2576, 384), (R_KVA, 2960, 256), (R_GATE, 3248, 3072)]
    fw.memset("gpsimd", Win[:, :, R_MA:R_MA + 256], 0.0)
    for (dc, sc, wd) in segs:
        for o in range(0, wd, 256):
            w = min(256, wd - o)
            cast_load(fw, C, Win[:, :, dc + o:dc + o + w], wv[:, :, sc + o:sc + o + w], [128, 8, w])
    small = [(R_MA + 0, 2056, 8), (R_MA + 32, 2048, 8), (R_MA + 64, 3216, 32), (R_MB + 64, 3232, 16), (R_MB + 80, 3216, 16)]
    for (dc, sc, wd) in small:
        cast_load(fw, C, Win[:, :, dc:dc + wd], wv[:, :, sc:sc + wd], [128, 8, wd])


def stage_inproj(fw, C, l):
    m = fw.mark()
    Win = fw.alloc("Win", [128, 8, PW], BF16)
    m2 = fw.mark()
    C.stg = [fw.alloc("stg%d" % i, [128, 2048]) for i in range(2)]
    C.stg_i = 0
    load_win(fw, C, l, Win)
    fw.release(m2)
    xt = [fw.alloc("xt%d" % i, [128, 8, 512]) for i in range(2)]
    sq = fw.alloc("sq", [128, 8, 512], BF16)
    h = [fw.alloc("h%d" % i, [128, 8, 512], BF16) for i in range(2)]
    rstd = fw.alloc("rstd", [128, 512])
    tmp = [fw.alloc("tmp%d" % i, [128, 512]) for i in range(2)]
    so = [fw.alloc("so%d" % i, [128, 4, 512]) for i in range(3)]
    xTv = C.xT.v.rearrange("(c p) t -> p c t", p=128)
    pv = C.projT.v.rearrange("(j p) t -> p j t", p=128)
    soi = 0
    for t in range(NT // 512):
        n = 0 if t < 2 else 1
        x = xt[t % 2]
        hh = h[t % 2]
        fw.dma("sync", x[:, :, :], xTv[:, :, t * 512:(t + 1) * 512])
        rms_rstd(fw, C, [x[:, c, :] for c in range(8)], D, 512, sq, rstd, tmp[0])
        for c in range(8):
            tm = tmp[c % 2]
            fw.tt(("vector", "gpsimd")[c % 2], tm[:, :], x[:, c, :], rstd[:, :], ALU.mult)
            fw.act(hh[:, c, :], tm[:, :], AF.Identity, scale=C.mA_m[:, c, n:n + 1], bias=C.mB_m[:, c, n:n + 1])
        for fo in range(NFO):
            ps = fw.ps()
            for k in range(8):
                fw.matmul(ps[:, :], Win[:, k, fo * 128:(fo + 1) * 128], hh[:, k, :], start=(k == 0), stop=(k == 7))
            s = so[soi % 3]
            fw.copy(fw.evac(), s[:, fo % 4, :], ps[:, :])
            if fo % 4 == 3 or fo == NFO - 1:
                f0 = fo - fo % 4
                nn = fo % 4 + 1
                fw.dma("gpsimd", pv[:, f0:f0 + nn, t * 512:(t + 1) * 512], s[:, 0:nn, :])
                soi += 1
    fw.release(m)


TS5 = 16
NCH = NT // TS5
TWO_PI = 2.0 * math.pi


def load_T(fw, C, dst, src2d, J, tmpb):
    fw.dma("sync", tmpb[0:J, :], src2d)
    ps = fw.ps()
    fw.transpose(ps[:, 0:J], tmpb[0:J, :], C.ident[0:J, 0:J])
    fw.copy("vector", dst, ps[:, 0:J])


def cmul(fw, eng, outr, outi, ar, ai, br, bi, t1, t2, nai=None):
    fw.tt(eng, t1, ar, br, ALU.mult)
    fw.tt(eng, t2, ai, bi, ALU.mult)
    fw.tt(eng, outr, t1, t2, ALU.subtract)
    fw.tt(eng, t1, ar, bi, ALU.mult)
    fw.tt(eng, t2, ai, br, ALU.mult)
    fw.tt(eng, outi, t1, t2, ALU.add)


def stage_s5(fw, C, l):
    m = fw.mark()
    pv = C.projT.v.rearrange("(j p) t -> p j t", p=128)
    T = TS5
    U = fw.alloc("U", [128, 4, T, NCH], BF16)
    BD = fw.alloc("BD", [128, 31, 4, 128], BF16)
    HB = fw.alloc("HB", [128, 2, 2, 16, NCH], BF16)
    Er = fw.alloc("Er", [128, 2, 17, 16])
    Ei = fw.alloc("Ei", [128, 2, 17, 16])
    Cbd = fw.alloc("Cbd", [128, 2, 16, 32])
    Wglu = fw.alloc("Wglu", [128, 4, 512], BF16)
    dsk = fw.alloc("dsk", [128, 4])
    bgl = fw.alloc("bgl", [128, 4])
    fin = fw.alloc("fin", [128, 2, 2, NPS, 16])
    m2 = fw.mark()
    HA = fw.alloc("HA", [128, 2, 16, NCH + 1])
    HBk = fw.alloc("HBk", [128, 2, 16, NCH + 1])
    Bb = fw.alloc("Bb", [128, 2, 3, 16, 32])
    C.stg = [fw.alloc("stg%d" % i, [128, 2048]) for i in range(1)]
    C.stg_i = 0
    tb = [fw.alloc("tb%d" % i, [128, 128]) for i in range(4)]
    m3 = fw.mark()
    wv = C.w_glu[l].rearrange("(k p) f -> p k f", p=128)
    cast_load(fw, C, Wglu[:, :, :], wv, [128, 4, 512])
    load_colvec(fw, C, dsk[:, :], C.s5_d[l], 4, tb[0])
    load_colvec(fw, C, bgl[:, :], C.b_glu[l], 4, tb[1])
    lam = fw.alloc("lam", [128, 2, 2, 16])
    dtb = fw.alloc("dtb", [128, 2, 16])
    small = fw.alloc("small", [16, 2])
    for d in range(2):
        load_T(fw, C, lam[:, d, 0, :], C.s5_lam_re[l, d].rearrange("(j g) p -> j (g p)", g=2), 16, tb[2])
        load_T(fw, C, lam[:, d, 1, :], C.s5_lam_im[l, d].rearrange("(j g) p -> j (g p)", g=2), 16, tb[3])
        fw.dma("sync", small[:, :], C.s5_log_dt[l, d].rearrange("(j g) -> j g", g=2))
        fw.copy("vector", tb[2][0:16, :].rearrange("j (g p) -> j g p", g=2), small[:, :].unsq(2).bc([16, 2, 64]))
        ps = fw.ps()
        fw.transpose(ps[:, 0:16], tb[2][0:16, :], C.ident[0:16, 0:16])
        fw.act(dtb[:, d, :], ps[:, 0:16], AF.Exp)
    rho = fw.alloc("rho", [128, 2, 16])
    th = fw.alloc("th", [128, 2, 16])
    fw.tt("vector", rho[:, :, :], lam[:, :, 0, :], dtb[:, :, :], ALU.mult)
    fw.tt("vector", th[:, :, :], lam[:, :, 1, :], dtb[:, :, :], ALU.mult)
    ang = fw.alloc("ang", [128, 17, 16])
    kf = fw.alloc("kf", [128, 17, 16])
    ki = fw.alloc("ki", [128, 17, 16], I32)
    msk = fw.alloc("msk", [128, 17, 16])
    mag = fw.alloc("mag", [128, 17, 16])
    sn = fw.alloc("sn", [128, 17, 16])
    cs = fw.alloc("cs", [128, 17, 16])
    svb = C.svec[:, :].unsq(2).bc([128, 17, 16])
    for d in range(2):
        fw.tt("vector", ang[:, :, :], th[:, d, :].unsq(1).bc([128, 17, 16]), svb, ALU.mult)
        fw.ts("vector", kf[:, :, :], ang[:, :, :], 1.0 / TWO_PI, None, op0=ALU.mult)
        fw.copy("vector", ki[:, :, :], kf[:, :, :])
        fw.copy("vector", kf[:, :, :], ki[:, :, :])
        fw.stt("vector", ang[:, :, :], kf[:, :, :], -TWO_PI, ang[:, :, :], ALU.mult, ALU.add)
        fw.ts("vector", msk[:, :, :], ang[:, :, :], math.pi, None, op0=ALU.is_gt)
        fw.stt("vector", ang[:, :, :], msk[:, :, :], -TWO_PI, ang[:, :, :], ALU.mult, ALU.add)
        fw.ts("vector", msk[:, :, :], ang[:, :, :], -math.pi, None, op0=ALU.is_lt)
        fw.stt("vector", ang[:, :, :], msk[:, :, :], TWO_PI, ang[:, :, :], ALU.mult, ALU.add)
        fw.act(sn[:, :, :], ang[:, :, :], AF.Sin)
        fw.ts("vector", msk[:, :, :], ang[:, :, :], -1.0, None, op0=ALU.mult)
        fw.tt("vector", msk[:, :, :], msk[:, :, :], ang[:, :, :], ALU.max)
        fw.act(cs[:, :, :], msk[:, :, :], AF.Sin, scale=-1.0, bias=C.halfpi[:, 0:1])
        fw.tt("vector", mag[:, :, :], rho[:, d, :].unsq(1).bc([128, 17, 16]), svb, ALU.mult)
        fw.act(mag[:, :, :], mag[:, :, :], AF.Exp)
        fw.tt("vector", Er[:, d, :, :], mag[:, :, :], cs[:, :, :], ALU.mult)
        fw.tt("vector", Ei[:, d, :, :], mag[:, :, :], sn[:, :, :], ALU.mult)
    cf = fw.alloc("cf", [128, 2, 2, 16])
    t1 = fw.alloc("t1", [128, 2, 16])
    t2 = fw.alloc("t2", [128, 2, 16])
    den = fw.alloc("den", [128, 2, 16])
    nr = fw.alloc("nr", [128, 2, 16])
    fw.tt("vector", t1[:, :, :], lam[:, :, 0, :], lam[:, :, 0, :], ALU.mult)
    fw.tt("vector", t2[:, :, :], lam[:, :, 1, :], lam[:, :, 1, :], ALU.mult)
    fw.tt("vector", den[:, :, :], t1[:, :, :], t2[:, :, :], ALU.add)
    fw.recip(den[:, :, :], den[:, :, :])
    fw.ts("vector", nr[:, :, :], Er[:, :, 1, :], -1.0, None, op0=ALU.add)
    fw.tt("vector", t1[:, :, :], nr[:, :, :], lam[:, :, 0, :], ALU.mult)
    fw.tt("vector", t2[:, :, :], Ei[:, :, 1, :], lam[:, :, 1, :], ALU.mult)
    fw.tt("vector", t1[:, :, :], t1[:, :, :], t2[:, :, :], ALU.add)
    fw.tt("vector", cf[:, :, 0, :], t1[:, :, :], den[:, :, :], ALU.mult)
    fw.tt("vector", t1[:, :, :], Ei[:, :, 1, :], lam[:, :, 0, :], ALU.mult)
    fw.tt("vector", t2[:, :, :], nr[:, :, :], lam[:, :, 1, :], ALU.mult)
    fw.tt("vector", t1[:, :, :], t1[:, :, :], t2[:, :, :], ALU.subtract)
    fw.tt("vector", cf[:, :, 1, :], t1[:, :, :], den[:, :, :], ALU.mult)
    Bn = fw.alloc("Bn", [128, 2, 16, 16])
    fw.dma("sync", Bn[:, 0, :, :], C.s5_b_re[l].rearrange("(j g) p c -> (g p) j c", g=2))
    fw.dma("sync", Bn[:, 1, :, :], C.s5_b_im[l].rearrange("(j g) p c -> (g p) j c", g=2))
    bb = fw.alloc("bb", [128, 2, 16, 16])
    u1 = fw.alloc("u1", [128, 16, 16])
    u2 = fw.alloc("u2", [128, 16, 16])
    mS = C.maskS[:, :].unsq(1).unsq(3).bc([128, 16, 2, 16])
    for d in range(2):
        cr = cf[:, d, 0, :].unsq(2).bc([128, 16, 16])
        ci = cf[:, d, 1, :].unsq(2).bc([128, 16, 16])
        cmul(fw, "vector", bb[:, 0, :, :], bb[:, 1, :, :], cr, ci, Bn[:, 0, :, :], Bn[:, 1, :, :], u1[:, :, :], u2[:, :, :])
        for ri in range(2):
            fw.tt("vector", Bb[:, d, ri, :, :].rearrange("p j (g c) -> p j g c", g=2), bb[:, ri, :, :].unsq(2).bc([128, 16, 2, 16]), mS, ALU.mult)
        fw.ts("vector", Bb[:, d, 2, :, :], Bb[:, d, 1, :, :], -1.0, None, op0=ALU.mult)
    cn = fw.alloc("cn", [128, 64])
    cx = fw.alloc("cx", [128, 2, 64])
    for ri, src in enumerate((C.s5_c_re, C.s5_c_im)):
        cv = src[l].rearrange("g c p -> (g c) p")
        for gh in range(4):
            fw.dma("sync", cn[:, :], cv[gh * 128:(gh + 1) * 128, :])
            fw.tt("vector", cx[:, :, :], cn[:, :].unsq(1).bc([128, 2, 64]), C.maskR[:, :].unsq(2).bc([128, 2, 64]), ALU.mult)
            ps = fw.ps()
            fw.transpose(ps[:, 0:128], cx[:, :, :].rearrange("p g q -> p (g q)"), C.ident[:, :])
            fw.copy("vector", Cbd[:, ri, gh * 4:(gh + 1) * 4, :].rearrange("p j x -> p (j x)"), ps[:, 0:128])
    uf = fw.alloc("uf", [128, 512])
    for gh in range(4):
        for t in range(NT // 512):
            fw.dma("sync", uf[:, :], pv[:, R_U // 128 + gh, t * 512:(t + 1) * 512])
            fw.copy(fw.any2(), U[:, gh, :, t * 32:(t + 1) * 32], uf[:, :].rearrange("p (k i) -> p i k", i=T))
    Gt = fw.alloc("Gt", [128, 2, 2, 16, 32])
    Gb = fw.alloc("Gb", [128, 2, 2, 16, 32], BF16)
    Bbb = fw.alloc("Bbb", [128, 2, 3, 16, 32], BF16)
    fw.copy("vector", Bbb[:, :, :, :, :], Bb[:, :, :, :, :])
    g1 = fw.alloc("g1", [128, 16, 32])
    g2 = fw.alloc("g2", [128, 16, 32])
    blk = fw.alloc("blk", [128, 64])
    fw.memset("gpsimd", BD[:, :, :, :], 0.0)
    mQ = C.maskQ[:, :].unsq(2).bc([128, 4, 32])

    def make_G(dst, d, s, j0, j1, t1_, t2_, neg_im):
        er = Er[:, d, s, j0:j1].unsq(2).bc([128, j1 - j0, 32])
        ei = Ei[:, d, s, j0:j1].unsq(2).bc([128, j1 - j0, 32])
        cmul(fw, "vector", dst[0], dst[1], Cbd[:, 0, j0:j1, :], Cbd[:, 1, j0:j1, :], er, ei, t1_, t2_)

    for dd in range(16):
        for d in range(2):
            make_G((Gt[:, d, 0, :, :], Gt[:, d, 1, :, :]), d, dd, 0, 16, g1[:, :, :], g2[:, :, :], False)
        fw.copy("gpsimd", Gb[:, :, :, :, :], Gt[:, :, :, :, :])
        for gh in range(4):
            ps = fw.ps()
            for jl in (3, 0, 1, 2):
                j = gh * 4 + jl
                if jl == 3:
                    o_ = ps[64:128, :]
                    lh = lambda d, bsel: Bbb[:, d, bsel, j - 1:j + 1, :].rearrange("p a b -> p (a b)")
                else:
                    o_ = ps[32 * jl:32 * jl + 32, :]
                    lh = lambda d, bsel: Bbb[:, d, bsel, j, :]
                if dd == 0:
                    seq = [(0, 0, 0), (0, 2, 1), (1, 0, 0), (1, 2, 1)]
                    for n_, (d, bsel, ri) in enumerate(seq):
                        fw.matmul(o_[:, 0:32], lh(d, bsel), Gb[:, d, ri, j, :], start=(n_ == 0), stop=(n_ == 3))
                else:
                    for d in range(2):
                        fw.matmul(o_[:, 32 * d:32 * d + 32], lh(d, 0), Gb[:, d, 0, j, :], start=True, stop=False)
                        fw.matmul(o_[:, 32 * d:32 * d + 32], lh(d, 2), Gb[:, d, 1, j, :], start=False, stop=True)
            fw.copy("vector", blk[:, :], ps[:, 0:64])
            if dd == 0:
                fw.tt("vector", BD[:, 15, gh, :].rearrange("p (a b) -> p a b", b=32), blk[:, 0:32].unsq(1).bc([128, 4, 32]), mQ, ALU.mult)
            else:
                fw.tt("vector", BD[:, 15 + dd, gh, :].rearrange("p (a b) -> p a b", b=32), blk[:, 0:32].unsq(1).bc([128, 4, 32]), mQ, ALU.mult)
                fw.tt("vector", BD[:, 15 - dd, gh, :].rearrange("p (a b) -> p a b", b=32), blk[:, 32:64].unsq(1).bc([128, 4, 32]), mQ, ALU.mult)
    fw.release(m3)
    LX = fw.alloc("LX", [128, 16, 2, 128], BF16)
    LX3 = fw.alloc("LX3", [128, 16, 2, 128], BF16)
    wt = [fw.alloc("wt%d" % i, [128, 2, 4, 32]) for i in range(2)]
    w1 = fw.alloc("w1", [128, 4, 32])
    w2 = fw.alloc("w2", [128, 4, 32])
    for d in range(2):
        Hdst = HA if d == 0 else HBk
        slot0 = 1 if d == 0 else 0
        for gh in range(4):
            for i in range(16):
                s_ = 15 - i if d == 0 else i
                w = wt[i % 2]
                er = Er[:, d, s_, gh * 4:(gh + 1) * 4].unsq(2).bc([128, 4, 32])
                ei = Ei[:, d, s_, gh * 4:(gh + 1) * 4].unsq(2).bc([128, 4, 32])
                cmul(fw, ("vector", "gpsimd")[i % 2], w[:, 0, :, :], w[:, 1, :, :], Bb[:, d, 0, gh * 4:(gh + 1) * 4, :], Bb[:, d, 1, gh * 4:(gh + 1) * 4, :], er, ei, w1[:, :, :], w2[:, :, :])
                ps = fw.ps()
                for ri in range(2):
                    fw.transpose(ps[:, ri * 128:(ri + 1) * 128], w[:, ri, :, :].rearrange("p a b -> p (a b)"), C.ident[:, :])
                fw.copy(fw.evac(), LX[:, i, :, :], ps[:, 0:256].rearrange("p (r x) -> p r x", r=2))
                fw.ts("gpsimd", LX3[64:128, i, :, :], LX[64:128, i, :, :], C.rm3[64:128, 0:1], None, op0=ALU.mult)
            for jl in range(4):
                for ri in range(2):
                    ps = fw.ps()
                    for i in range(16):
                        if jl == 3:
                            fw.matmul(ps[:, 0:NCH], LX3[64:128, i, ri, :], U[64:128, gh, i, :], start=(i == 0), stop=(i == 15))
                        else:
                            fw.matmul(ps[:, 0:NCH], LX[32 * jl:32 * jl + 32, i, ri, :], U[32 * jl:32 * jl + 32, gh, i, :], start=(i == 0), stop=(i == 15))
                    fw.copy(fw.evac(), Hdst[:, ri, gh * 4 + jl, slot0:slot0 + NCH], ps[:, 0:NCH])
    h0 = fw.alloc("h0", [128, 2, 2, 16])
    for d in range(2):
        load_T(fw, C, h0[:, d, 0, :], C.s5re[l, d].rearrange("(j g) p -> j (g p)", g=2), 16, tb[0])
        load_T(fw, C, h0[:, d, 1, :], C.s5im[l, d].rearrange("(j g) p -> j (g p)", g=2), 16, tb[1])
    sa = [fw.alloc("sa%d" % i, [128, 2, 16]) for i in range(2)]
    sb = [fw.alloc("sb%d" % i, [128, 2, 16]) for i in range(2)]
    mu3 = fw.alloc("mu3", [128, 2, 3, 16])
    for d in range(2):
        fw.copy("vector", mu3[:, d, 0, :], Er[:, d, 16, :])
        fw.copy("vector", mu3[:, d, 1, :], Ei[:, d, 16, :])
        fw.ts("vector", mu3[:, d, 2, :], Ei[:, d, 16, :], -1.0, None, op0=ALU.mult)
    for d in range(2):
        eng = ("vector", "gpsimd")[d]
        H = HA if d == 0 else HBk
        a, b = sa[d], sb[d]
        mur = mu3[:, d, 0, :].unsq(1).bc([128, 2, 16])
        order = list(enumerate(SEQS)) if d == 0 else list(enumerate(SEQS))[::-1]
        for si, (off, L, smp) in order:
            k0, k1 = off // T, (off + L) // T
            if d == 0:
                init, steps = k0, [(k, k + 1) for k in range(k0, k1)]
            else:
                init, steps = k1, [(k, k - 1) for k in range(k1, k0, -1)]
            if smp:
                fw.copy(eng, H[:, :, :, init], h0[:, d, :, :])
            else:
                fw.memset(eng, H[:, :, :, init], 0.0)
            for n_, (src, dst) in enumerate(steps):
                last = (n_ == len(steps) - 1)
                fw.tt(eng, a[:, :, :], H[:, :, :, src], mur, ALU.mult)
                fw.tt(eng, b[:, 0, :], H[:, 1, :, src], mu3[:, d, 2, :], ALU.mult)
                fw.tt(eng, b[:, 1, :], H[:, 0, :, src], mu3[:, d, 1, :], ALU.mult)
                fw.tt(eng, a[:, :, :], a[:, :, :], b[:, :, :], ALU.add)
                if last:
                    if not smp:
                        fw.tt(eng, fin[:, d, :, si, :], a[:, :, :], H[:, :, :, dst], ALU.add)
                else:
                    fw.tt(eng, H[:, :, :, dst], a[:, :, :], H[:, :, :, dst], ALU.add)
    fo_ = fw.alloc("fo", [16, 128])
    for d in range(2):
        for ri, dst in enumerate((C.o_s5re, C.o_s5im)):
            for si in range(NPS):
                ps = fw.ps()
                fw.transpose(ps[0:16, 0:128], fin[:, d, ri, si, :], C.ident[:, :])
                fw.copy("vector", fo_[:, :], ps[0:16, 0:128])
                fw.dma("gpsimd", dst[si, l, d].rearrange("(j g) p -> j (g p)", g=2), fo_[:, :])
    fw.copy("vector", HB[:, 0, :, :, :], HA[:, :, :, 0:NCH])
    fw.copy("gpsimd", HB[:, 1, :, :, :], HBk[:, :, :, 1:NCH + 1])
    fw.release(m2)
    yg = fw.alloc("yg", [128, 4, NT], BF16)
    Y = fw.alloc("Y", [128, NT])
    uf = fw.alloc("uf2", [128, NT])
    Gm = [fw.alloc("Gm%d" % i, [128, 2, 2, 4, 32]) for i in range(2)]
    Gmb = [fw.alloc("Gmb%d" % i, [128, 2, 2, 5, 32], BF16) for i in range(2)]
    for g_ in Gmb:
        fw.memset("vector", g_[:, :, :, :, :], 0.0)
    g1 = fw.alloc("g1m", [128, 4, 32])
    g2 = fw.alloc("g2m", [128, 4, 32])
    y2 = fw.alloc("y2", [128, NT])
    y3 = fw.alloc("y3", [128, NT])
    for gh in range(4):
        fw.dma("sync", uf[:, :], pv[:, R_U // 128 + gh, :])
        for j in range(16):
            G_, Gb_ = Gm[j % 2], Gmb[j % 2]
            for d in range(2):
                s_ = j + 1 if d == 0 else 16 - j
                er = Er[:, d, s_, gh * 4:(gh + 1) * 4].unsq(2).bc([128, 4, 32])
                ei = Ei[:, d, s_, gh * 4:(gh + 1) * 4].unsq(2).bc([128, 4, 32])
                cmul(fw, ("vector", "gpsimd")[d], G_[:, d, 0, :, :], G_[:, d, 1, :, :], Cbd[:, 0, gh * 4:(gh + 1) * 4, :], Cbd[:, 1, gh * 4:(gh + 1) * 4, :], er, ei, g1[:, :, :] if d == 0 else y2[:, 0:128].rearrange("p (a b) -> p a b", b=32), g2[:, :, :] if d == 0 else y3[:, 0:128].rearrange("p (a b) -> p a b", b=32))
                fw.ts(("vector", "gpsimd")[d], G_[:, d, 1, :, :], G_[:, d, 1, :, :], -1.0, None, op0=ALU.mult)
            fw.copy("vector", Gb_[:, :, :, 0:3, :], G_[:, :, :, 0:3, :])
            fw.copy("vector", Gb_[:, :, :, 4, :], G_[:, :, :, 3, :])
            ps = fw.ps()
            for i in range(16):
                fw.matmul(ps[:, 0:NCH], BD[:, 15 + j - i, gh, :], U[:, gh, i, :], start=(i == 0), stop=False)
            for jl in range(4):
                n_ = 0
                for d in range(2):
                    for ri in range(2):
                        n_ += 1
                        if jl == 3:
                            fw.matmul(ps[64:128, 0:NCH], Gb_[:, d, ri, 3:5, :].rearrange("p a b -> p (a b)"), HB[:, d, ri, gh * 4 + jl, :], start=False, stop=(n_ == 4))
                        else:
                            fw.matmul(ps[32 * jl:32 * jl + 32, 0:NCH], Gb_[:, d, ri, jl, :], HB[:, d, ri, gh * 4 + jl, :], start=False, stop=(n_ == 4 and jl != 2))
            fw.copy(fw.evac(), Y[:, :].rearrange("p (k i) -> p i k", i=T)[:, j, :], ps[:, 0:NCH])
        fw.stt("vector", Y[:, :], uf[:, :], dsk[:, gh:gh + 1], Y[:, :], ALU.mult, ALU.add)
        fw.tt("gpsimd", y2[:, :], Y[:, :], Y[:, :], ALU.mult)
        fw.ts("vector", y2[:, :], y2[:, :], 0.044715, 1.0, op0=ALU.mult, op1=ALU.add)
        fw.tt("gpsimd", y2[:, :], y2[:, :], Y[:, :], ALU.mult)
        fw.act(y3[:, :], y2[:, :], AF.Sigmoid, scale=2.0 * 0.7978845608028654)
        fw.tt("vector", yg[:, gh, :], Y[:, :], y3[:, :], ALU.mult)
    ob = [fw.alloc("ob%d" % i, [128, 4, 512], BF16) for i in range(2)]
    sg = [fw.alloc("sgl%d" % i, [128, 512]) for i in range(2)]
    bT = C.brT.v[1].rearrange("(c p) t -> p c t", p=128)
    for t in range(NT // 512):
        o = ob[t % 2]
        for oc_ in range(4):
            ps = fw.ps()
            for k in range(4):
                fw.matmul(ps[:, :], Wglu[:, k, oc_ * 128:(oc_ + 1) * 128], yg[:, k, t * 512:(t + 1) * 512], start=(k == 0), stop=(k == 3))
            sgt = sg[oc_ % 2]
            fw.act(sgt[:, :], ps[:, :], AF.Sigmoid, bias=bgl[:, oc_:oc_ + 1])
            fw.tt(fw.any2(), o[:, oc_, :], yg[:, oc_, t * 512:(t + 1) * 512], sgt[:, :], ALU.mult)
        fw.dma("gpsimd", bT[:, :, t * 512:(t + 1) * 512], o[:, :, :])
    fw.release(m)


NB = NT // 128
GSC = 128 ** -0.5
import os
GDN_STOP = int(os.environ.get('GDN_STOP', '9'))
BULK_STOP = int(os.environ.get('BULK_STOP', '9'))


def stage_gdn(fw, C, l):
    m = fw.mark()
    pv = C.projT.v.rearrange("(j p) t -> p j t", p=128)
    qkT = fw.alloc("qkT", [128, 12, NT], BF16)
    osum = fw.alloc("osum", [128, 4, NT])
    Rall = fw.alloc("Rall", [128, NT])
    Col = fw.alloc("Col", [128, NB, 32])
    gout = fw.alloc("gout", [128, 1])
    masks = fw.alloc("gmasks", [128, 4, 128])
    sel = fw.alloc("gsel", [128, 16, 128])
    fw.memset("vector", Rall[:, :], 0.0)
    fw.dma("sync", masks[:, :, :], C.c_gmask)
    fw.dma("sync", sel[:, :, :], C.c_gsel)
    fw.dma("sync", gout[:, :], C.g_delta_out[l].rearrange("(p o) -> p o", o=1))
    m2 = fw.mark()
    cw = fw.alloc("cw", [128, 5, 12])
    tb = [fw.alloc("tb%d" % i, [128, 128]) for i in range(5)]
    for i in range(5):
        load_colvec(fw, C, cw[:, i, :], C.conv_qkv[l, i], 12, tb[i])
    xc = [fw.alloc("xc%d" % i, [128, NT]) for i in range(1)]
    acc = [fw.alloc("acc%d" % i, [128, NT]) for i in range(1)]
    sq = fw.alloc("sq", [128, 1, 512], BF16)
    rstd = fw.alloc("rstd", [128, 512])
    tmp = fw.alloc("tmp", [128, 512])
    for c in range(12):
        x, a = xc[0], acc[0]
        fw.dma("sync", x[:, :], pv[:, c, :])
        fw.act(a[:, :], x[:, :], AF.Identity, scale=cw[:, 2, c:c + 1])
        for i in (0, 1, 3, 4):
            sh = i - 2
            for (off, L, smp) in SEQS:
                lo, hi = max(0, -sh), min(L, L - sh)
                eng = ("vector", "gpsimd")[(i + (off // LP)) % 2]
                fw.stt(eng, a[:, off + lo:off + hi], x[:, off + lo + sh:off + hi + sh], cw[:, i, c:c + 1], a[:, off + lo:off + hi], ALU.mult, ALU.add)
        fw.act(a[:, :], a[:, :], AF.Silu)
        if c < 8:
            for t in range(NT // 512):
                tok = slice(t * 512, (t + 1) * 512)
                ps = fw.ps()
                fw.act(sq[:, 0, :], a[:, tok], AF.Square)
                fw.matmul(ps[:, :], C.ones_bf[:, :], sq[:, 0, :])
                fw.act(tmp[:, :], ps[:, :], AF.Sqrt, scale=1.0, bias=C.eps_col[:, 0:1])
                fw.recip(rstd[:, :], tmp[:, :])
                if c < 4:
                    fw.stt("vector", qkT[:, c, tok], a[:, tok], GSC, rstd[:, :], ALU.mult, ALU.mult)
                else:
                    fw.tt("vector", qkT[:, c, tok], a[:, tok], rstd[:, :], ALU.mult)
        else:
            fw.copy("gpsimd", qkT[:, c, :], a[:, :])
    fw.release(m2)
    if GDN_STOP <= 1:
        fw.release(m)
        return
    m2 = fw.mark()
    Rall2 = fw.alloc("Rall2", [128, NT])
    al = fw.alloc("al", [128, NT])
    P = fw.alloc("P", [128, NT])
    tot = fw.alloc("tot", [128, NT // 64])
    rst = fw.alloc("rst", [128, NT])
    colp = fw.alloc("colp", [128, 4])
    fw.dma("sync", al[0:8, :], C.projT[R_MA:R_MA + 8, :])
    fw.dma("sync", Rall[32:40, :], C.projT[R_MA + 32:R_MA + 40, :])
    fw.dma("sync", rst[0:8, :], C.c_rst)
    fw.dma("sync", colp[0:8, 0:1], C.dt_bias[l].rearrange("d (h o) -> (d h) o", o=1))
    fw.dma("sync", colp[0:8, 1:2], C.a_log[l].rearrange("d (h o) -> (d h) o", o=1))
    fw.dma("sync", colp[0:8, 2:3], C.c_mdir)
    fw.memset("vector", colp[0:8, 3:4], 1.0)
    fw.act(colp[0:8, 1:2], colp[0:8, 1:2], AF.Exp)
    fw.ts("vector", colp[0:8, 1:2], colp[0:8, 1:2], -1.0, None, op0=ALU.mult)
    fw.act(al[0:8, :], al[0:8, :], AF.Exp, bias=colp[0:8, 0:1])
    fw.act(al[0:8, :], al[0:8, :], AF.Ln, bias=colp[0:8, 3:4])
    fw.ts("vector", al[0:8, :], al[0:8, :], colp[0:8, 1:2], None, op0=ALU.mult)
    fw.act(Rall[32:40, :], Rall[32:40, :], AF.Sigmoid)
    fw.scan(P[0:8, :], rst[0:8, :], al[0:8, :], 0.0)
    P3 = P[0:8, :].rearrange("r (k c) -> r k c", c=64)
    fw.copy("vector", tot[0:8, :], P3[:, :, 63])
    totb = tot[0:8, :].unsq(2).bc([8, NT // 64, 64])
    S3 = Rall2[0:8, :].rearrange("r (k c) -> r k c", c=64)
    a3 = al[0:8, :].rearrange("r (k c) -> r k c", c=64)
    G3 = Rall[0:8, :].rearrange("r (k c) -> r k c", c=64)
    fw.tt("vector", S3, totb, P3, ALU.subtract)
    fw.tt("vector", S3, S3, a3, ALU.add)
    fw.tt("vector", S3, S3, P3, ALU.subtract)
    fw.stt("vector", Rall[0:8, :], Rall2[0:8, :], colp[0:8, 2:3], P[0:8, :], ALU.mult, ALU.add)
    fw.tt("vector", S3, totb, G3, ALU.subtract)
    fw.act(Rall2[0:8, :], Rall2[0:8, :], AF.Exp)
    for b in range(NB):
        ps = fw.ps()
        fw.transpose(ps[:, 0:128], Rall[:, b * 128:(b + 1) * 128], C.ident[:, :])
        fw.transpose(ps[:, 128:256], Rall2[:, b * 128:(b + 1) * 128], C.ident[:, :])
        fw.copy("vector", Col[:, b, 0:8], ps[:, 0:8])
        fw.copy("vector", Col[:, b, 8:16], ps[:, 32:40])
        fw.copy("vector", Col[:, b, 24:32], ps[:, 128:136])
        fw.act(Col[:, b, 16:24], ps[:, 0:8], AF.Exp)
        fw.tt("vector", Col[:, b, 16:24], Col[:, b, 16:24], Col[:, b, 8:16], ALU.mult)
        fw.ts("vector", Col[:, b, 16:24], Col[:, b, 16:24], -1.0, None, op0=ALU.mult)
    fw.release(m2)
    NU = 2
    def mk(nm, dt=F32):
        return [[fw.alloc("%s_%d_%d" % (nm, h, u), [128, 128], dt) for u in range(NU)] for h in range(4)]
    TTb, QKb, qtb, ktb, bvb = mk("TT", F32), mk("QK", BF16), mk("qt", BF16), mk("kt", BF16), mk("bv", F32)
    ED = fw.alloc("ED", [128, 4, NU, 2])
    def mkh(nm, n, dt=F32):
        return [[fw.alloc("%s_%d_%d" % (nm, h, i), [128, 128], dt) for i in range(n)] for h in range(4)]
    E_h, ET_h, t_h, eg_h, Af_h, Bf_h, Pf_h = mkh("E", 1), mkh("ET", 1), mkh("t", 4), mkh("eg", 1), mkh("Af", 2), mkh("Bf", 2), mkh("Pf", 2)
    Sst = [fw.alloc("S%d" % h, [128, 128]) for h in range(4)]
    Sbf = [fw.alloc("Sb%d" % h, [128, 128], BF16) for h in range(4)]
    rhsb = [fw.alloc("rhs%d" % h, [128, 128]) for h in range(4)]
    vnb = [fw.alloc("vn%d" % h, [128, 128], BF16) for h in range(4)]

    def bulk(d, h, b, u):
        r = d * 4 + h
        blk = slice(b * 128, (b + 1) * 128)
        mS, mT, mI = (masks[:, 0, :], masks[:, 1, :], masks[:, 3, :]) if d == 0 else (masks[:, 1, :], masks[:, 0, :], masks[:, 2, :])
        gcol, bcol, dlcol = Col[:, b, r:r + 1], Col[:, b, 8 + r:9 + r], Col[:, b, 24 + r:25 + r]
        E, ET, eg, t_, Af, Bf, Pf = E_h[h][0], ET_h[h][0], eg_h[h][0], t_h[h], Af_h[h], Bf_h[h], Pf_h[h]
        bank = fw.psb[h]
        fw.matmul(bank[:, 0:128], sel[:, r, :], Rall[:, blk])
        fw.matmul(bank[:, 128:256], sel[:, 8 + r, :], Rall[:, blk])
        yield
        fw.ts("vector", E[:, :], bank[:, 0:128], gcol, 0.0, op0=ALU.subtract, op1=ALU.max)
        fw.ts("vector", ET[:, :], bank[:, 0:128], gcol, 0.0, op0=ALU.subtract, op1=ALU.min)
        yield
        fw.act(eg[:, :], bank[:, 0:128], AF.Exp)
        yield
        fw.tt("vector", t_[3][:, :], bank[:, 128:256], mT, ALU.mult)
        fw.matmul(bank[:, 256:384], qkT[:, 4 + h, blk], qkT[:, 4 + h, blk])
        fw.matmul(bank[:, 384:512], qkT[:, 4 + h, blk], qkT[:, h, blk])
        yield
        fw.act(E[:, :], E[:, :], AF.Exp, scale=-1.0)
        fw.act(ET[:, :], ET[:, :], AF.Exp)
        for hf_ in range(2):
            c_ = 64 * hf_ + (63 if d == 0 else 0)
            fw.copy("gpsimd", ED[:, h, u, hf_:hf_ + 1], eg[:, c_:c_ + 1])
        fw.tt("gpsimd", qtb[h][u][:, :], qkT[:, h, blk], eg[:, :], ALU.mult)
        yield
        fw.tt("vector", t_[0][:, :], bank[:, 256:384], E[:, :], ALU.mult)
        fw.tt("vector", t_[1][:, :], bank[:, 256:384], ET[:, :], ALU.mult)
        fw.tt("vector", E[:, :], bank[:, 384:512], ET[:, :], ALU.mult)
        yield
        fw.stt("vector", Af[0][:, :], t_[0][:, :], bcol, mS, ALU.mult, ALU.mult)
        fw.tt("gpsimd", Bf[0][:, :], t_[1][:, :], t_[3][:, :], ALU.mult)
        fw.tt("gpsimd", QKb[h][u][:, :], E[:, :], mI, ALU.mult)
        ptb = bank[:, :].bitcast(BF16)
        fw.transpose(ptb[:, 0:128], qkT[:, 4 + h, blk], C.ident_bf[:, :])
        fw.transpose(ptb[:, 128:256], qkT[:, 8 + h, blk], C.ident_bf[:, :])
        yield
        fw.tt("gpsimd", Pf[0][:, :], C.ident[:, :], Bf[0][:, :], ALU.subtract)
        fw.ts("vector", ktb[h][u][:, :], ptb[:, 0:128], dlcol, None, op0=ALU.mult)
        fw.ts("vector", bvb[h][u][:, :], ptb[:, 128:256], bcol, None, op0=ALU.mult)
        yield
        cur = 0
        pcur = 0
        for lev in range(5):
            nxt = 1 - cur
            fw.matmul(bank[:, 0:128], Bf[cur][:, :], Af[cur][:, :])
            if lev < 4:
                fw.matmul(bank[:, 128:256], Af[cur][:, :], Bf[cur][:, :])
            yield
            fw.copy("scalar", Af[nxt][:, :], bank[:, 0:128])
            if lev < 4:
                fw.copy("vector", Bf[nxt][:, :], bank[:, 128:256])
            yield
            fw.matmul(bank[:, 256:384], Af[nxt][:, :], Pf[pcur][:, :])
            yield
            if lev < 4:
                fw.tt("vector", Pf[1 - pcur][:, :], bank[:, 256:384], Pf[pcur][:, :], ALU.add)
            else:
                fw.tt("vector", TTb[h][u][:, :], bank[:, 256:384], Pf[pcur][:, :], ALU.add)
            yield
            cur = nxt
            pcur = 1 - pcur

    def recur(d, h, b, u, first_dir):
        r = d * 4 + h
        bank = fw.psb[4 + h]
        for hf in ((0, 1) if d == 0 else (1, 0)):
            lo = 64 * hf
            rows = slice(lo, lo + 64)
            tok = slice(b * 128 + lo, b * 128 + lo + 64)
            nbeg = Col[rows, b, 16 + r:17 + r]
            fw.matmul(bank[rows, 0:128], qkT[:, 4 + h, tok], Sbf[h][:, :])
            yield
            fw.stt("vector", rhsb[h][rows, :], bank[rows, 0:128], nbeg, bvb[h][u][rows, :], ALU.mult, ALU.add)
            yield
            fw.matmul(bank[rows, 128:256], TTb[h][u][rows, lo:lo + 64], rhsb[h][rows, :])
            yield
            fw.copy("scalar", vnb[h][rows, :], bank[rows, 128:256])
            yield
            fw.matmul(bank[:, 256:320], Sbf[h][:, :], qtb[h][u][:, lo:lo + 64], start=True, stop=False)
            fw.matmul(bank[:, 256:320], vnb[h][rows, :], QKb[h][u][rows, lo:lo + 64], start=False, stop=True)
            fw.matmul(bank[:, 384:512], ktb[h][u][rows, :], vnb[h][rows, :])
            yield
            fw.stt("vector", Sst[h][:, :], Sst[h][:, :], ED[:, h, u, hf:hf + 1], bank[:, 384:512], ALU.mult, ALU.add)
            if first_dir:
                fw.copy("scalar", osum[:, h, tok], bank[:, 256:320])
            else:
                fw.tt("vector", osum[:, h, tok], osum[:, h, tok], bank[:, 256:320], ALU.add)
            yield
            fw.copy("gpsimd", Sbf[h][:, :], Sst[h][:, :])
            yield

    def run_rr(gens):
        gens = list(gens)
        while gens:
            for g in list(gens):
                try:
                    next(g)
                except StopIteration:
                    gens.remove(g)

    for si, (off, L, smp) in enumerate(SEQS):
        b0, b1 = off // 128, (off + L) // 128
        for d in range(2):
            for h in range(4):
                if smp:
                    fw.dma("sync", Sst[h][:, :], C.sd[l, d, h])
                else:
                    fw.memset("vector", Sst[h][:, :], 0.0)
                fw.copy("gpsimd", Sbf[h][:, :], Sst[h][:, :])
            blocks = list(range(b0, b1)) if d == 0 else list(range(b1 - 1, b0 - 1, -1))
            run_rr([bulk(d, h, blocks[0], 0) for h in range(4)])
            for n_, b in enumerate(blocks):
                u = n_ % NU
                gens = [recur(d, h, b, u, d == 0) for h in range(4)]
                if n_ + 1 < len(blocks):
                    gens += [bulk(d, h, blocks[n_ + 1], (n_ + 1) % NU) for h in range(4)]
                run_rr(gens)
            if not smp:
                for h in range(4):
                    fw.dma("gpsimd", C.o_sd[si, l, d, h], Sst[h][:, :])
    fw.release(m)
    m = fw.mark()
    qkT = fw.alloc("qkT", [128, 12, NT], BF16)
    osum = fw.alloc("osum", [128, 4, NT])
    Rall = fw.alloc("Rall", [128, NT])
    Col = fw.alloc("Col", [128, NB, 32])
    gout = fw.alloc("gout", [128, 1])
    zt = fw.alloc("zt", [128, NT])
    sq = fw.alloc("sq2", [128, 1, 512], BF16)
    rstd = fw.alloc("rstd2", [128, 512])
    tmp = fw.alloc("tmp2", [128, 512])
    oa = [fw.alloc("oa%d" % i, [128, 512], BF16) for i in range(2)]
    bT = C.brT.v[0].rearrange("(c p) t -> p c t", p=128)
    for h in range(4):
        fw.dma("sync", zt[:, :], pv[:, R_Z // 128 + h, :])
        fw.act(zt[:, :], zt[:, :], AF.Silu)
        for t in range(NT // 512):
            tok = slice(t * 512, (t + 1) * 512)
            ps = fw.ps()
            fw.act(sq[:, 0, :], osum[:, h, tok], AF.Square)
            fw.matmul(ps[:, :], C.ones_bf[:, :], sq[:, 0, :])
            fw.act(tmp[:, :], ps[:, :], AF.Sqrt, scale=1.0 / 128, bias=C.eps_col[:, 0:1])
            fw.recip(rstd[:, :], tmp[:, :])
            fw.stt("vector", tmp[:, :], osum[:, h, tok], gout[:, 0:1], rstd[:, :], ALU.mult, ALU.mult)
            o = oa[t % 2]
            fw.tt("gpsimd", o[:, :], tmp[:, :], zt[:, tok], ALU.mult)
            fw.dma("gpsimd", bT[:, h, tok], o[:, :])
    fw.release(m)


QSCALE = 96 ** -0.5
H_C = 8


def stage_mla(fw, C, l):
    m = fw.mark()
    pv = C.projT.v.rearrange("(j p) t -> p j t", p=128)
    Wqb = fw.alloc("Wqb", [128, 3, 768], BF16)
    Wqs = fw.alloc("Wqs", [128, 3, 256], BF16)
    Wkn = fw.alloc("Wkn", [128, 2, 512], BF16)
    Wv = fw.alloc("Wv", [128, 2, 512], BF16)
    gq = fw.alloc("gq", [128, 3])
    gkv = fw.alloc("gkv", [128, 2])
    rope = fw.alloc("rope", [128, 4, LS])
    m2 = fw.mark()
    C.stg = [fw.alloc("stg%d" % i, [128, 2304]) for i in range(2)]
    C.stg_i = 0
    tb = [fw.alloc("tb%d" % i, [128, 128]) for i in range(2)]
    load_colvec(fw, C, gq[:, :], C.g_q_a[l], 3, tb[0])
    load_colvec(fw, C, gkv[:, :], C.g_kv_a[l], 2, tb[1])
    for i in range(4):
        fw.dma("sync", rope[0:32, i, :], C.c_rope[i])
    wv = C.w_q_b[l].rearrange("(k p) f -> p k f", p=128)
    cast_load(fw, C, Wqb[:, :, :], wv, [128, 3, 768])
    st = C.stg[C.stg_i % 2]
    C.stg_i += 1
    sv = st[:, 0:768].rearrange("p (k f) -> p k f", f=256)
    for h in range(8):
        fw.dma("sync", sv[:, :, h * 32:h * 32 + 16], wv[:, :, h * 96 + 80:h * 96 + 96])
        fw.dma("sync", sv[:, :, h * 32 + 16:h * 32 + 32], wv[:, :, h * 96 + 64:h * 96 + 80])
    fw.copy("vector", Wqs[:, :, :], sv)
    wv = C.w_kv_b[l].rearrange("(k p) (h x) -> p k h x", p=128, x=128)
    st = C.stg[C.stg_i % 2]
    C.stg_i += 1
    sv = st[:, 0:2048].rearrange("p (k h x) -> p k h x", h=8, x=128)
    for k in range(2):
        fw.dma("sync", sv[:, k, :, :], wv[:, k, :, :])
    for k in range(2):
        fw.copy("vector", Wkn[:, k, :].rearrange("p (h x) -> p h x", x=64), sv[:, k, :, 0:64])
        fw.copy("gpsimd", Wv[:, k, :].rearrange("p (h x) -> p h x", x=64), sv[:, k, :, 64:128])
    fw.release(m2)
    NKMAX = LS + 256
    QT = fw.alloc("QT", [128, 8, LS], BF16)
    KT = fw.alloc("KT", [128, 8, NKMAX], BF16)
    Vaug = fw.alloc("Vaug", [128, NKMAX // 128, 8, 65], BF16)
    ckvb = fw.alloc("ckvb", [128, 2, NKMAX], BF16)
    qn = fw.alloc("qn", [128, 3, 512], BF16)
    krot = fw.alloc("krot", [128, NKMAX], BF16)
    qa = fw.alloc("qa", [128, 3, 512])
    sq = fw.alloc("sq", [128, 3, 512], BF16)
    rstd = fw.alloc("rstd", [128, 512])
    tmp = [fw.alloc("tmp%d" % i, [128, 512]) for i in range(3)]
    ckvf = fw.alloc("ckvf", [128, 2, 512])
    krA = fw.alloc("krA", [128, 512])
    krB = fw.alloc("krB", [128, 512])
    otok = [fw.alloc("otok%d" % i, [128, 288]) for i in range(2)]
    ctxt = fw.alloc("ctxt", [128, 288])
    pt = [fw.alloc("pt%d" % i, [128, 512], BF16) for i in range(3)]
    mx = fw.alloc("mx", [128, 32])
    oc = fw.alloc("oc", [128, 4, 512])
    ocT = [fw.alloc("ocT%d" % i, [128, 4, 512], BF16) for i in range(2)]
    fw.memset("vector", KT[32:64, :, :], 0.0)
    fw.memset("vector", KT[32:33, :, :], 1.0)
    fw.memset("vector", Vaug[:, :, :, 64:65], 1.0)
    oti = 0
    for si, (off, L, smp) in enumerate(SEQS):
        koff = 256 if smp else 0
        nk = L + koff
        N = min(512, L)
        fw.memset("vector", QT[32:64, :, 0:L], 0.0)
        for t0 in range(0, L, N):
            tok = slice(off + t0, off + t0 + N)
            fw.dma("sync", qa[:, :, 0:N], pv[:, R_QA // 128:R_QA // 128 + 3, tok])
            rms_rstd(fw, C, [qa[:, c, 0:N] for c in range(3)], 384, N, sq, rstd, tmp[0])
            for c in range(3):
                fw.stt("vector", qn[:, c, 0:N], qa[:, c, 0:N], gq[:, c:c + 1], rstd[:, 0:N], ALU.mult, ALU.mult)
            for h in range(8):
                psA = fw.ps()
                for k in range(3):
                    fw.matmul(psA[64:128, 0:N], Wqb[:, k, h * 96:h * 96 + 64], qn[:, k, 0:N], start=(k == 0), stop=(k == 2))
                for k in range(3):
                    fw.matmul(psA[0:32, 0:N], Wqb[:, k, h * 96 + 64:h * 96 + 96], qn[:, k, 0:N], start=(k == 0), stop=(k == 2))
                fw.act(QT[64:128, h, t0:t0 + N], psA[64:128, 0:N], AF.Copy, scale=QSCALE)
                if smp:
                    psB = fw.ps()
                    for k in range(3):
                        fw.matmul(psB[0:32, 0:N], Wqs[:, k, h * 32:h * 32 + 32], qn[:, k, 0:N], start=(k == 0), stop=(k == 2))
                    fw.tt("vector", tmp[1][0:32, 0:N], psA[0:32, 0:N], rope[0:32, 2, t0:t0 + N], ALU.mult)
                    fw.tt("vector", tmp[2][0:32, 0:N], psB[0:32, 0:N], rope[0:32, 3, t0:t0 + N], ALU.mult)
                    fw.tt("gpsimd", QT[0:32, h, t0:t0 + N], tmp[1][0:32, 0:N], tmp[2][0:32, 0:N], ALU.add)
                else:
                    fw.act(QT[0:32, h, t0:t0 + N], psA[0:32, 0:N], AF.Copy, scale=QSCALE)
            fw.dma("sync", qa[:, 0:2, 0:N], pv[:, R_KVA // 128:R_KVA // 128 + 2, tok])
            rms_rstd(fw, C, [qa[:, c, 0:N] for c in range(2)], 256, N, sq, rstd, tmp[0])
            for c in range(2):
                fw.stt("vector", ckvf[:, c, 0:N], qa[:, c, 0:N], gkv[:, c:c + 1], rstd[:, 0:N], ALU.mult, ALU.mult)
                fw.copy("gpsimd", ckvb[:, c, koff + t0:koff + t0 + N], ckvf[:, c, 0:N])
            fw.dma("sync", krA[0:32, 0:N], C.projT[R_MA + 64:R_MA + 96, tok])
            if smp:
                fw.dma("sync", krB[0:32, 0:N], C.projT[R_MB + 64:R_MB + 96, tok])
                fw.tt("vector", tmp[1][0:32, 0:N], krA[0:32, 0:N], rope[0:32, 0, t0:t0 + N], ALU.mult)
                fw.tt("vector", tmp[2][0:32, 0:N], krB[0:32, 0:N], rope[0:32, 1, t0:t0 + N], ALU.mult)
                fw.tt("gpsimd", krot[0:32, koff + t0:koff + t0 + N], tmp[1][0:32, 0:N], tmp[2][0:32, 0:N], ALU.add)
            else:
                fw.copy("gpsimd", krot[0:32, t0:t0 + N], krA[0:32, 0:N])
                for tb_ in range(N // 128):
                    ot = otok[oti % 2]
                    oti += 1
                    ps = fw.ps()
                    for c in range(2):
                        fw.transpose(ps[:, c * 128:(c + 1) * 128], ckvf[:, c, tb_ * 128:(tb_ + 1) * 128], C.ident[:, :])
                    fw.transpose(ps[:, 256:288], krA[0:32, tb_ * 128:(tb_ + 1) * 128], C.ident[0:32, 0:32])
                    fw.copy(fw.evac(), ot[:, :], ps[:, 0:288])
                    fw.dma("gpsimd", C.o_ckv[si, l, t0 + tb_ * 128:t0 + (tb_ + 1) * 128, :], ot[:, 0:256])
                    fw.dma("gpsimd", C.o_kr[si, l, t0 + tb_ * 128:t0 + (tb_ + 1) * 128, :], ot[:, 256:288])
        if smp:
            for tb_ in range(2):
                fw.dma("sync", ctxt[:, 0:256], C.cckv[l, tb_ * 128:(tb_ + 1) * 128, :])
                fw.dma("sync", ctxt[:, 256:288], C.ckr[l, tb_ * 128:(tb_ + 1) * 128, :])
                ps = fw.ps()
                for c in range(2):
                    fw.transpose(ps[:, c * 128:(c + 1) * 128], ctxt[:, c * 128:(c + 1) * 128], C.ident[:, :])
                    fw.copy(fw.evac(), ckvb[:, c, tb_ * 128:(tb_ + 1) * 128], ps[:, c * 128:(c + 1) * 128])
                fw.transpose(ps[0:32, 256:384], ctxt[:, 256:288], C.ident[:, :])
                fw.copy("vector", krot[0:32, tb_ * 128:(tb_ + 1) * 128], ps[0:32, 256:384])
        for h in range(8):
            fw.copy(fw.any2(), KT[0:32, h, 0:nk], krot[0:32, 0:nk])
            for k0 in range(0, nk, 512):
                n = min(512, nk - k0)
                ps = fw.ps()
                for k in range(2):
                    fw.matmul(ps[64:128, 0:n], Wkn[:, k, h * 64:(h + 1) * 64], ckvb[:, k, k0:k0 + n], start=(k == 0), stop=(k == 1))
                fw.copy(fw.evac(), KT[64:128, h, k0:k0 + n], ps[64:128, 0:n])
        for kc in range(nk // 128):
            ps = fw.ps()
            for k in range(2):
                fw.matmul(ps[:, :], ckvb[:, k, kc * 128:(kc + 1) * 128], Wv[:, k, :], start=(k == 0), stop=(k == 1))
            fw.copy(fw.evac(), Vaug[:, kc, :, 0:64], ps[:, :].rearrange("p (h x) -> p h x", x=64))
        nj = N // 128
        nkb = (nk + 511) // 512
        nkc = nk // 128
        for t0 in range(0, L, N):
            for h in range(8):
                fw.ps_set = [4, 5, 6, 7]
                for j in range(nj):
                    q0 = t0 + j * 128
                    for kb in range(nkb):
                        n = min(512, nk - kb * 512)
                        ps = fw.ps()
                        fw.matmul(ps[:, 0:n], QT[:, h, q0:q0 + 128], KT[:, h, kb * 512:kb * 512 + n])
                        fw.rmax(mx[:, j * 8 + kb:j * 8 + kb + 1], ps[:, 0:n])
                    if nkb > 1:
                        fw.rmax(mx[:, j * 8 + 7:j * 8 + 8], mx[:, j * 8:j * 8 + nkb])
                        mcol = mx[:, j * 8 + 7:j * 8 + 8]
                    else:
                        mcol = mx[:, j * 8:j * 8 + 1]
                    ps = fw.ps()
                    fw.matmul(ps[32:33, 0:128], mcol, C.ident[:, :])
                    fw.act(QT[32:33, h, q0:q0 + 128], ps[32:33, 0:128], AF.Copy, scale=-1.0)
                acc = [fw.psb[j] for j in range(nj)]
                for kc in range(nkc):
                    ST = fw.ps()
                    fw.matmul(ST[:, 0:N], KT[:, h, kc * 128:(kc + 1) * 128], QT[:, h, t0:t0 + N])
                    p = pt[kc % 3]
                    fw.act(p[:, 0:N], ST[:, 0:N], AF.Exp)
                    for j in range(nj):
                        fw.matmul(acc[j][:, 0:65], p[:, j * 128:(j + 1) * 128], Vaug[:, kc, h, :], start=(kc == 0), stop=(kc == nkc - 1))
                for j in range(nj):
                    fw.recip(mx[:, 8 * j + 6:8 * j + 7], acc[j][:, 64:65])
                    fw.ts("vector", oc[:, j, h * 64:(h + 1) * 64], acc[j][:, 0:64], mx[:, 8 * j + 6:8 * j + 7], None, op0=ALU.mult)
            fw.ps_set = list(range(8))
            o = ocT[(t0 // N) % 2]
            for j in range(nj):
                ps = fw.ps()
                for c in range(4):
                    fw.transpose(ps[:, c * 128:(c + 1) * 128], oc[:, j, c * 128:(c + 1) * 128], C.ident[:, :])
                fw.copy(fw.evac(), o[:, :, j * 128:(j + 1) * 128], ps[:, :].rearrange("p (c t) -> p c t", t=128))
            fw.dma("gpsimd", C.brT.v[2].rearrange("(c p) t -> p c t", p=128)[:, :, off + t0:off + t0 + N], o[:, :, 0:N])
    fw.release(m)


def stage_merge(fw, C, l):
    m = fw.mark()
    Wbr = fw.alloc("Wbr", [128, 12, 1024], BF16)
    Wout = fw.alloc("Wout", [128, 8, 1024], BF16)
    m2 = fw.mark()
    C.stg = [fw.alloc("stg%d" % i, [128, 2048]) for i in range(2)]
    C.stg_i = 0
    for br in range(3):
        wv = C.w_branch[l, br].rearrange("(k p) f -> p k f", p=128)
        for o in range(0, 1024, 512):
            cast_load(fw, C, Wbr[:, br * 4:(br + 1) * 4, o:o + 512], wv[:, :, o:o + 512], [128, 4, 512])
    wv = C.w_out[l].rearrange("(k p) f -> p k f", p=128)
    for o in range(0, 1024, 256):
        cast_load(fw, C, Wout[:, :, o:o + 256], wv[:, :, o:o + 256], [128, 8, 256])
    fw.release(m2)
    xt = [fw.alloc("xt%d" % i, [128, 8, 512]) for i in range(2)]
    xo = [fw.alloc("xo%d" % i, [128, 8, 512]) for i in range(1)]
    brt = [fw.alloc("brt%d" % i, [128, 12, 512], BF16) for i in range(2)]
    gl = [fw.alloc("gl%d" % i, [128, 8, 512]) for i in range(2)]
    merged = fw.alloc("merged", [128, 8, 512])
    mergedb = fw.alloc("mergedb", [128, 8, 512], BF16)
    sg = [fw.alloc("sg%d" % i, [128, 512]) for i in range(2)]
    tp = [fw.alloc("tp%d" % i, [128, 512]) for i in range(2)]
    xTv = C.xT.v.rearrange("(c p) t -> p c t", p=128)
    pv = C.projT.v.rearrange("(j p) t -> p j t", p=128)
    bv = C.brT.v.rearrange("b (k p) t -> b p k t", p=128)
    gi = 0
    for t in range(NT // 512):
        n = 0 if t < 2 else 1
        tok = slice(t * 512, (t + 1) * 512)
        x = xt[t % 2]
        o = xo[0]
        bt = brt[t % 2]
        fw.dma("sync", x[:, :, :], xTv[:, :, tok])
        for br in range(3):
            fw.dma("sync", bt[:, br * 4:(br + 1) * 4, :], bv[br, :, :, tok])
        for br in range(3):
            g = gl[gi % 2]
            gi += 1
            fw.dma("sync", g[:, :, :], pv[:, R_GATE // 128 + br * 8:R_GATE // 128 + br * 8 + 8, tok])
            for dc in range(8):
                ps = fw.ps()
                for k in range(4):
                    fw.matmul(ps[:, :], Wbr[:, br * 4 + k, dc * 128:(dc + 1) * 128], bt[:, br * 4 + k, :], start=(k == 0), stop=(k == 3))
                s = sg[dc % 2]
                fw.act(s[:, :], g[:, dc, :], AF.Sigmoid)
                if br == 0:
                    fw.tt("vector", merged[:, dc, :], ps[:, :], s[:, :], ALU.mult)
                else:
                    tq = tp[dc % 2]
                    fw.tt("vector", tq[:, :], ps[:, :], s[:, :], ALU.mult)
                    if br == 1:
                        fw.tt("gpsimd", merged[:, dc, :], merged[:, dc, :], tq[:, :], ALU.add)
                    else:
                        fw.tt("gpsimd", mergedb[:, dc, :], merged[:, dc, :], tq[:, :], ALU.add)
        for ec in range(8):
            ps = fw.ps()
            for d in range(8):
                fw.matmul(ps[:, :], Wout[:, d, ec * 128:(ec + 1) * 128], mergedb[:, d, :], start=(d == 0), stop=(d == 7))
            fw.stt("vector", o[:, ec, :], ps[:, :], C.mG_m[:, ec, n:n + 1], x[:, ec, :], ALU.mult, ALU.add)
        fw.dma("gpsimd", xTv[:, :, tok], o[:, :, :])
    fw.release(m)


def stage_ffn(fw, C, l):
    m = fw.mark()
    Wup = fw.alloc("Wup", [128, 8, 2 * D_FF], BF16)
    Wdn = fw.alloc("Wdn", [128, 22, 1024], BF16)
    cw = fw.alloc("cw", [128, 3, 44])
    cb = fw.alloc("cb", [128, 44])
    m2 = fw.mark()
    C.stg = [fw.alloc("stg%d" % i, [128, 2048]) for i in range(2)]
    C.stg_i = 0
    wv = C.w_ffn_up[l].rearrange("(k p) f -> p k f", p=128)
    for o in range(0, 2 * D_FF, 256):
        cast_load(fw, C, Wup[:, :, o:o + 256], wv[:, :, o:o + 256], [128, 8, 256])
    wv = C.w_ffn_down[l].rearrange("(j p) e -> p j e", p=128)
    for j in range(0, 22, 2):
        cast_load(fw, C, Wdn[:, j:j + 2, :], wv[:, j:j + 2, :], [128, 2, 1024])
    tb = [fw.alloc("tb%d" % i, [128, 128]) for i in range(4)]
    for i in range(3):
        load_colvec(fw, C, cw[:, i, :], C.conv_ffn[l, i], 44, tb[i])
    load_colvec(fw, C, cb[:, :], C.b_conv_ffn[l], 44, tb[3])
    fw.release(m2)
    W = 258
    xh = [fw.alloc("xh%d" % i, [128, 8, W]) for i in range(2)]
    sq = fw.alloc("sq", [128, 8, W], BF16)
    h = [fw.alloc("h%d" % i, [128, 8, W], BF16) for i in range(2)]
    rstd = fw.alloc("rstd", [128, W])
    tmp = [fw.alloc("tmp%d" % i, [128, W]) for i in range(2)]
    actt = fw.alloc("actt", [128, 22, 256], BF16)
    ga = [fw.alloc("ga%d" % i, [128, 256]) for i in range(2)]
    gb = [fw.alloc("gb%d" % i, [128, 256]) for i in range(2)]
    va = [fw.alloc("va%d" % i, [128, 256]) for i in range(2)]
    vb = [fw.alloc("vb%d" % i, [128, 256]) for i in range(2)]
    xo = [fw.alloc("xo%d" % i, [128, 8, 256]) for i in range(2)]
    xTv = C.xT.v.rearrange("(c p) t -> p c t", p=128)
    ti = 0
    for (off, L, n) in SEQS:
        for t0 in range(0, L, 256):
            x = xh[ti % 2]
            hh = h[ti % 2]
            o = xo[ti % 2]
            ti += 1
            lo = 1 if t0 == 0 else 0
            hi = W - 1 if t0 + 256 >= L else W
            if lo == 1:
                fw.memset("gpsimd", x[:, :, 0:1], 0.0)
            else:
                fw.copy("gpsimd", x[:, :, 0:1], xprev[:, :, W - 2:W - 1])
            if hi == W - 1:
                fw.memset("gpsimd", x[:, :, W - 1:W], 0.0)
            fw.dma("sync", x[:, :, 1:hi], xTv[:, :, off + t0:off + t0 - 1 + hi])
            xprev = x
            rms_rstd(fw, C, [x[:, c, :] for c in range(8)], D, W, sq, rstd, tmp[0])
            for c in range(8):
                tm = tmp[c % 2]
                fw.tt(("vector", "gpsimd")[c % 2], tm[:, :], x[:, c, :], rstd[:, :], ALU.mult)
                fw.act(hh[:, c, :], tm[:, :], AF.Identity, scale=C.mA_f[:, c, n:n + 1], bias=C.mB_f[:, c, n:n + 1])
            if lo == 1:
                fw.memset("gpsimd", hh[:, :, 0:1], 0.0)
            if hi == W - 1:
                fw.memset("gpsimd", hh[:, :, W - 1:W], 0.0)
            for j in range(22):
                res = []
                for (which, ta, tb) in ((0, ga[j % 2], gb[j % 2]), (1, va[j % 2], vb[j % 2])):
                    ch = which * 22 + j
                    ps = fw.ps()
                    for k in range(8):
                        fw.matmul(ps[:, 0:W], Wup[:, k, ch * 128:(ch + 1) * 128], hh[:, k, :], start=(k == 0), stop=(k == 7))
                    fw.act(ta[:, :], ps[:, 1:257], AF.Identity, scale=cw[:, 1, ch:ch + 1], bias=cb[:, ch:ch + 1])
                    fw.stt("vector", tb[:, :], ps[:, 0:256], cw[:, 0, ch:ch + 1], ta[:, :], ALU.mult, ALU.add)
                    fw.stt("vector", ta[:, :], ps[:, 2:258], cw[:, 2, ch:ch + 1], tb[:, :], ALU.mult, ALU.add)
                    res.append(ta)
                fw.act(gb[j % 2][:, :], res[0][:, :], AF.Silu)
                fw.tt("gpsimd", actt[:, j, :], gb[j % 2][:, :], res[1][:, :], ALU.mult)
            for ec in range(8):
                ps = fw.ps()
                for j in range(22):
                    fw.matmul(ps[:, 0:256], Wdn[:, j, ec * 128:(ec + 1) * 128], actt[:, j, :], start=(j == 0), stop=(j == 21))
                fw.stt("vector", o[:, ec, :], ps[:, 0:256], C.mG_f[:, ec, n:n + 1], x[:, ec, 1:257], ALU.mult, ALU.add)
            fw.dma("gpsimd", xTv[:, :, off + t0:off + t0 + 256], o[:, :, :])
    fw.release(m)


def stage_final(fw, C):
    m = fw.mark()
    gB = fw.alloc("gB", [128, 1024])
    fw.dma("sync", gB[:, :], C.g_final.unsq(0).pbc(128).rearrange("p a f -> p (a f)"))
    xt = [fw.alloc("xt%d" % i, [128, 8, 128]) for i in range(2)]
    yt = [fw.alloc("yt%d" % i, [128, 1024]) for i in range(2)]
    junk = fw.alloc("junk", [128, 512])
    ss = [fw.alloc("ss%d" % i, [128, 4]) for i in range(2)]
    xTv = C.xT.v.rearrange("(c p) t -> p c t", p=128)
    for tt in range(NT // 128):
        x = xt[tt % 2]
        y = yt[tt % 2]
        s = ss[tt % 2]
        fw.dma("sync", x[:, :, :], xTv[:, :, tt * 128:(tt + 1) * 128])
        pss = []
        for half in range(2):
            ps = fw.ps()
            pss.append(ps)
            for c in range(4):
                fw.transpose(ps[:, c * 128:(c + 1) * 128], x[:, half * 4 + c, :], C.ident[:, :])
            fw.act(junk[:, :], ps[:, :], AF.Square, accum_out=s[:, half:half + 1])
        fw.tt("vector", s[:, 2:3], s[:, 0:1], s[:, 1:2], ALU.add)
        fw.act(s[:, 3:4], s[:, 2:3], AF.Sqrt, scale=1.0 / D, bias=C.eps_col[:, 0:1])
        fw.recip(s[:, 2:3], s[:, 3:4])
        for half in range(2):
            fw.stt("vector", y[:, half * 512:(half + 1) * 512], pss[half][:, :], s[:, 2:3], gB[:, half * 512:(half + 1) * 512], ALU.mult, ALU.mult)
        fw.dma("gpsimd", C.y_tok[tt * 128:(tt + 1) * 128, :], y[:, :])
    fw.release(m)


WEIGHT_NAMES = ["w_mod", "b_mod", "g_norm_mix", "g_norm_ffn", "w_in", "conv_qkv", "a_log", "dt_bias", "g_delta_out",
                "s5_lam_re", "s5_lam_im", "s5_log_dt", "s5_b_re", "s5_b_im", "s5_c_re", "s5_c_im", "s5_d", "w_glu", "b_glu",
                "g_q_a", "w_q_b", "g_kv_a", "w_kv_b", "w_branch", "w_out", "w_ffn_up", "conv_ffn", "b_conv_ffn", "w_ffn_down",
                "g_final"]


def build(shapes, depth=DEPTH, dbg=False, skip_mixers=False, stages=("mla", "s5", "gdn", "dense")):
    nc = bass.Bass("TRN2", target_bir_lowering=False)
    fw = FW(nc)
    C = Ctx()
    for name, shp in shapes.items():
        setattr(C, name, fw.dram(name, shp, F32, kind="ExternalInput").v)
    kind = "ExternalOutput" if dbg else "Internal"
    C.y_tok = fw.dram("y_tok", [NT, D], F32, kind="ExternalOutput").v
    C.o_sd = fw.dram("o_sd", [NPS, DEPTH, 2, 4, 128, 128], F32, kind="ExternalOutput").v
    C.o_s5re = fw.dram("o_s5re", [NPS, DEPTH, 2, 32, 64], F32, kind="ExternalOutput").v
    C.o_s5im = fw.dram("o_s5im", [NPS, DEPTH, 2, 32, 64], F32, kind="ExternalOutput").v
    C.o_ckv = fw.dram("o_ckv", [NPS, DEPTH, LP, 256], F32, kind="ExternalOutput").v
    C.o_kr = fw.dram("o_kr", [NPS, DEPTH, LP, 32], F32, kind="ExternalOutput").v
    C.xT = fw.dram("xT", [D, NT], F32, kind=kind)
    C.projT = fw.dram("projT", [PW, NT], F32, kind=kind)
    C.brT = fw.dram("brT", [3, 512, NT], BF16, kind=kind)
    if dbg:
        C.xmid = fw.dram("xmid", [D, NT], F32, kind=kind)
    C.ident = fw.alloc("ident", [128, 128])
    C.ones_bf = fw.alloc("ones_bf", [128, 128], BF16)
    C.eps_col = fw.alloc("eps_col", [128, 1])
    C.cT = fw.alloc("cT", [128, 8, 2])
    for nm in ("mA_m", "mB_m", "mG_m", "mA_f", "mB_f", "mG_f"):
        setattr(C, nm, fw.alloc(nm, [128, 8, 2]))
    fw.dma("sync", C.ident[:, :], C.c_ident)
    C.halfpi = fw.alloc("halfpi", [128, 1])
    fw.memset("vector", C.halfpi[:, :], math.pi / 2)
    C.svec = fw.alloc("svec", [128, 17])
    C.maskR = fw.alloc("maskR", [128, 2])
    C.maskS = fw.alloc("maskS", [128, 2])
    C.maskQ = fw.alloc("maskQ", [128, 4])
    fw.dma("sync", C.svec[:, :], C.c_svec)
    fw.dma("sync", C.maskR[:, :], C.c_maskR)
    fw.dma("sync", C.maskS[:, :], C.c_maskS)
    fw.dma("sync", C.maskQ[:, :], C.c_maskQ)
    C.ident_bf = fw.alloc("ident_bf", [128, 128], BF16)
    fw.copy("vector", C.ident_bf[:, :], C.ident[:, :])
    C.rm3 = fw.alloc("rm3", [128, 1])
    fw.dma("sync", C.rm3[:, :], C.c_rm3)
    fw.memset("vector", C.ones_bf[:, :], 1.0)
    fw.memset("vector", C.eps_col[:, :], EPS)
    tb0 = fw.alloc("tb0", [128, 128])
    tb1 = fw.alloc("tb1", [128, 8])
    for n in range(2):
        load_colvec(fw, C, tb1[:, :], C.cc[n], 8, tb0)
        fw.copy("vector", C.cT[:, :, n], tb1[:, :])
    fw.act(C.cT[:, :, :], C.cT[:, :, :], AF.Silu)
    if skip_mixers:
        m = fw.mark()
        z = fw.alloc("z", [128, 4, 512], BF16)
        zf = fw.alloc("zf", [128, 4, 512])
        bv = C.brT.v.rearrange("b (k p) t -> b p k t", p=128)
        dv = C.dbg_br.rearrange("b (k p) t -> b p k t", p=128)
        for br in range(3):
            for t in range(NT // 512):
                fw.dma("sync", zf[:, :, :], dv[br, :, :, t * 512:(t + 1) * 512])
                fw.copy("vector", z[:, :, :], zf[:, :, :])
                fw.dma("gpsimd", bv[br, :, :, t * 512:(t + 1) * 512], z[:, :, :])
        fw.release(m)
    stage_in(fw, C)
    for l in range(depth):
        stage_mods(fw, C, l)
        stage_inproj(fw, C, l)
        if "mla" in stages:
            stage_mla(fw, C, l)
        if "s5" in stages:
            stage_s5(fw, C, l)
        if "gdn" in stages:
            stage_gdn(fw, C, l)
        if "dense" in stages:
            stage_merge(fw, C, l)
            if dbg and l == 0:
                fw.dma("sync", C.xmid.v, C.xT.v)
            stage_ffn(fw, C, l)
    stage_final(fw, C)
    fw.emit()
    return nc, fw


def host_consts():
    t = np.arange(LS)
    row = (t // 64).astype(np.float32)
    col = (t % 64).astype(np.float32)
    inv = (1.0 / (10000.0 ** (np.arange(8, dtype=np.float32) / 8))).astype(np.float32)
    ang = np.concatenate([row[:, None] * inv, col[:, None] * inv], axis=-1).astype(np.float32)
    cos, sin = np.cos(ang).astype(np.float32).T, np.sin(ang).astype(np.float32).T
    cs1 = np.concatenate([cos, cos], 0)
    cs2 = np.concatenate([-sin, sin], 0)
    rope = np.stack([cs1, cs2, cs1 * np.float32(QSCALE), cs2 * np.float32(QSCALE)]).astype(np.float32)
    p = np.arange(128)
    svec = np.tile(np.arange(17, dtype=np.float32)[None, :], (128, 1))
    maskR = (((p // 16) % 2)[:, None] == np.arange(2)[None, :]).astype(np.float32)
    maskS = ((p // 64)[:, None] == np.arange(2)[None, :]).astype(np.float32)
    maskQ = ((p // 32)[:, None] == np.arange(4)[None, :]).astype(np.float32)
    rm3 = (p >= 96).astype(np.float32)[:, None]
    cc_, ee_ = np.meshgrid(p, p, indexing="ij")
    same = (cc_ // 64) == (ee_ // 64)
    gmask = np.stack([same & (cc_ > ee_), same & (cc_ < ee_), same & (cc_ >= ee_), same & (cc_ <= ee_)], axis=1).astype(np.float32)
    gsel = np.zeros((128, 16, 128), np.float32)
    for r_ in range(8):
        gsel[r_, r_, :] = 1.0
        gsel[32 + r_, 8 + r_, :] = 1.0
    rst = np.ones((8, NT), np.float32)
    rst[:, ::64] = 0.0
    mdir = (np.arange(8) >= 4).astype(np.float32)[:, None]
    return {"c_ident": np.eye(128, dtype=np.float32), "c_rope": rope, "c_svec": svec, "c_maskR": maskR,
            "c_maskS": maskS, "c_maskQ": maskQ, "c_rm3": rm3, "c_gmask": gmask, "c_gsel": gsel, "c_rst": rst,
            "c_mdir": mdir}


def make_in_maps(inputs):
    f = lambda a: np.ascontiguousarray(np.asarray(a, dtype=np.float32))
    W = {k: f(inputs[k]) for k in WEIGHT_NAMES}
    consts = host_consts()
    xp = f(inputs["x_prompt"])
    xs = f(inputs["x_sample"])
    maps = []
    for c in range(8):
        d = dict(W)
        d.update(consts)
        d["x_tok"] = np.concatenate([xp[4 * c:4 * c + 4].reshape(NPS * LP, D), xs[c]], axis=0)
        d["cc"] = np.stack([f(inputs["c_ctx"]), f(inputs["c"])[c]], axis=0)
        d["sd"] = f(inputs["state_delta"])[c]
        d["s5re"] = f(inputs["state_s5_re"])[c]
        d["s5im"] = f(inputs["state_s5_im"])[c]
        d["cckv"] = f(inputs["cache_ckv"])[c]
        d["ckr"] = f(inputs["cache_krope"])[c]
        maps.append(d)
    return maps


def kernel(**inputs):
    maps = make_in_maps(inputs)
    shapes = {k: list(v.shape) for k, v in maps[0].items()}
    nc, fw = build(shapes)
    res = run_bass_kernel_spmd(nc, maps, core_ids=list(range(8)))
    R = res.results
    y_prompt = np.stack([R[c]["y_tok"][:NPS * LP].reshape(NPS, LP, D) for c in range(8)]).reshape(32, LP, D)
    y_sample = np.stack([R[c]["y_tok"][NPS * LP:] for c in range(8)])
    cat = lambda k: np.concatenate([np.asarray(R[c][k]) for c in range(8)], axis=0).astype(np.float32)
    return (y_prompt.astype(np.float32), y_sample.astype(np.float32), cat("o_sd"), cat("o_s5re"), cat("o_s5im"), cat("o_ckv"), cat("o_kr"))
```

```python
import math
import numpy as np
import concourse.bass as bass
import concourse.mybir as mybir
from concourse.bass_utils import run_bass_kernel_spmd

F32 = mybir.dt.float32
BF16 = mybir.dt.bfloat16
I32 = mybir.dt.int32
AF = mybir.ActivationFunctionType
ALU = mybir.AluOpType
AX = mybir.AxisListType

DEPTH = 4
NPS, LP, LS = 4, 256, 2048
NT = NPS * LP + LS
SEQS = [(i * LP, LP, 0) for i in range(NPS)] + [(NPS * LP, LS, 1)]
D = 1024
EPS = 1e-6
D_FF = 2816
NFO = 51
PW = NFO * 128
R_QKV, R_Z, R_U, R_QA, R_KVA, R_GATE, R_MA, R_MB = 0, 1536, 2048, 2560, 2944, 3200, 6272, 6400

DMA_QUEUES = ("sync", "gpsimd", "scalar")
COMPUTE = ("tensor", "vector", "scalar", "gpsimd")
N_DMA_SEMS = 12
ARENA_WORDS = 52600


def dsize(dt):
    return 2 if dt == BF16 else 4


class Buf:
    __slots__ = ("name", "w", "r", "t", "psum", "g")

    def __init__(self, name, t=None, psum=False):
        self.name = name
        self.w = {}
        self.r = {}
        self.g = []
        self.t = t
        self.psum = psum

    def __getitem__(self, idx):
        return V(self, self.t[idx])

    @property
    def v(self):
        return V(self, self.t)


class V:
    __slots__ = ("buf", "ap")

    def __init__(self, buf, ap):
        self.buf = buf
        self.ap = ap

    def __getitem__(self, idx):
        return V(self.buf, self.ap[idx])

    def rearrange(self, pat, **kw):
        return V(self.buf, self.ap.rearrange(pat, **kw))

    def bitcast(self, dt):
        return V(self.buf, self.ap.bitcast(dt))

    def bc(self, shape):
        return V(self.buf, self.ap.broadcast_to(list(shape)))

    def unsq(self, ax):
        return V(self.buf, self.ap.unsqueeze(ax))

    def pbc(self, n):
        return V(self.buf, self.ap.partition_broadcast(n))

    @property
    def shape(self):
        return self.ap.shape


class Op:
    __slots__ = ("eng", "fn", "deps", "dma", "idx", "sig", "sem", "val", "prev_val", "slot")

    def __init__(self, eng, fn, dma):
        self.eng = eng
        self.fn = fn
        self.dma = dma
        self.deps = set()
        self.sig = False
        self.sem = None
        self.val = 0
        self.prev_val = 0


def _ap(x):
    return x.ap if isinstance(x, V) else x


class FW:
    def __init__(self, nc):
        self.nc = nc
        self.ops = []
        self.streams = {e: [] for e in ("tensor", "vector", "scalar", "gpsimd", "sync")}
        self.arena = nc.alloc_sbuf_tensor("arena", [128, ARENA_WORDS], F32)
        self.top = 0
        self.psb = [Buf("psb%d" % i, nc.alloc_psum_tensor("psb%d" % i, [128, 512], F32), psum=True) for i in range(8)]
        self.psi = 0
        self.ps_set = list(range(8))
        self.dma_since = []
        self.pending = {e: [] for e in self.streams}
        self.rr = 0
        self.drr = {q: 0 for q in DMA_QUEUES}

    def alloc(self, name, shape, dtype=F32):
        P = shape[0]
        free = 1
        for s in shape[1:]:
            free *= s
        words = (free * dsize(dtype) + 3) // 4
        words = (words + 7) // 8 * 8
        off = self.top
        self.top += words
        assert self.top <= ARENA_WORDS, "SBUF arena overflow %s %d" % (name, self.top)
        ap = self.arena[0:P, off:off + words]
        if dtype != F32:
            ap = ap.bitcast(dtype)
        ap = ap[:, 0:free]
        if len(shape) >= 3:
            names = ["d%d" % i for i in range(len(shape) - 1)]
            pat = "p (%s) -> p %s" % (" ".join(names), " ".join(names))
            kw = {names[i]: shape[i + 1] for i in range(1, len(names))}
            ap = ap.rearrange(pat, **kw)
        return Buf(name, ap)

    def mark(self):
        return self.top

    def release(self, m):
        self.top = m
        self.barrier()

    def ps(self):
        st = self.ps_set
        b = self.psb[st[self.psi % len(st)]]
        self.psi += 1
        return b

    def dram(self, name, shape, dtype, kind="Internal"):
        t = self.nc.dram_tensor(name, list(shape), dtype, kind=kind)
        return Buf(name, t.ap())

    def barrier(self):
        bar = []
        for e, st in self.streams.items():
            for o in reversed(st):
                if not o.dma:
                    bar.append(o.idx)
                    break
        bar.extend(self.dma_since)
        self.dma_since = []
        for e in self.pending:
            self.pending[e] = list(bar)

    def op(self, eng, fn, reads=(), writes=(), dma=False):
        o = Op(eng, fn, dma)
        o.idx = len(self.ops)
        self.ops.append(o)
        self.streams[eng].append(o)
        if dma:
            o.slot = self.drr[eng] % N_DMA_SEMS
            self.drr[eng] += 1
            key = (eng, o.slot)
        else:
            key = eng
        if self.pending[eng]:
            o.deps.update(self.pending[eng])
            self.pending[eng] = []
        for b in reads:
            o.deps.update(b.w.values())
        for b in writes:
            if b.r:
                b.g = list(b.r.values()) + list(b.w.values())
                b.w = {}
                b.r = {}
            o.deps.update(b.g)
            if b.psum:
                o.deps.update(b.w.values())
            elif key in b.w:
                o.deps.add(b.w[key])
        o.deps.discard(o.idx)
        for b in writes:
            b.w[key] = o.idx
        for b in reads:
            if b not in writes:
                b.r[key] = o.idx
        if dma:
            self.dma_since.append(o.idx)
        return o.idx

    def _rw(self, outs, ins):
        w = [x.buf for x in outs if isinstance(x, V)]
        r = [x.buf for x in ins if isinstance(x, V)]
        w = w + [b for b in r if b.psum]
        return r, w

    def dma(self, q, out, in_, slow=False):
        r, w = self._rw([out], [in_])
        o, i = _ap(out), _ap(in_)
        if slow:
            return self.op(q, lambda e: e.dma_start(out=o, in_=i, allow_slow_non_contiguous=True), r, w, dma=True)
        return self.op(q, lambda e: e.dma_start(out=o, in_=i), r, w, dma=True)

    def matmul(self, out, lhsT, rhs, start=True, stop=True, **kw):
        r, w = self._rw([out], [lhsT, rhs])
        if not start:
            r = r + [out.buf]
        o, a, b = _ap(out), _ap(lhsT), _ap(rhs)
        return self.op("tensor", lambda e: e.matmul(o, lhsT=a, rhs=b, start=start, stop=stop, **kw), r, w)

    def transpose(self, out, in_, ident):
        r, w = self._rw([out], [in_, ident])
        o, a, b = _ap(out), _ap(in_), _ap(ident)
        return self.op("tensor", lambda e: e.transpose(o, a, b), r, w)

    def act(self, out, in_, func, bias=None, scale=None, accum_out=None, eng="scalar"):
        r, w = self._rw([out, accum_out], [in_, bias, scale])
        kw = {}
        if bias is not None:
            kw["bias"] = _ap(bias)
        if scale is not None:
            kw["scale"] = _ap(scale)
        if accum_out is not None:
            kw["accum_out"] = _ap(accum_out)
        o, i = _ap(out), _ap(in_)
        return self.op("scalar", lambda e: e.activation(out=o, in_=i, func=func, **kw), r, w)

    def tt(self, eng, out, in0, in1, op):
        r, w = self._rw([out], [in0, in1])
        o, a, b = _ap(out), _ap(in0), _ap(in1)
        return self.op(eng, lambda e: e.tensor_tensor(out=o, in0=a, in1=b, op=op), r, w)

    def ts(self, eng, out, in0, s1, s2=None, op0=ALU.mult, op1=None):
        eng = "vector"
        r, w = self._rw([out], [in0, s1, s2])
        o, a, x1, x2 = _ap(out), _ap(in0), _ap(s1), _ap(s2)
        if op1 is None:
            return self.op(eng, lambda e: e.tensor_scalar(out=o, in0=a, scalar1=x1, scalar2=None, op0=op0), r, w)
        return self.op(eng, lambda e: e.tensor_scalar(out=o, in0=a, scalar1=x1, scalar2=x2, op0=op0, op1=op1), r, w)

    def stt(self, eng, out, in0, scalar, in1, op0, op1):
        eng = "vector"
        r, w = self._rw([out], [in0, scalar, in1])
        o, a, s, b = _ap(out), _ap(in0), _ap(scalar), _ap(in1)
        return self.op(eng, lambda e: e.scalar_tensor_tensor(out=o, in0=a, scalar=s, in1=b, op0=op0, op1=op1), r, w)

    def copy(self, eng, out, in_):
        r, w = self._rw([out], [in_])
        o, a = _ap(out), _ap(in_)
        if eng == "scalar":
            return self.op(eng, lambda e: e.activation(out=o, in_=a, func=AF.Copy), r, w)
        return self.op(eng, lambda e: e.tensor_copy(out=o, in_=a), r, w)

    def memset(self, eng, out, val):
        r, w = self._rw([out], [])
        r = [out.buf]
        o = _ap(out)
        i = self.op(eng, lambda e: e.memset(o, val), r, w)
        out.buf.r[("ms", eng)] = i
        return i

    def recip(self, out, in_, eng="vector"):
        r, w = self._rw([out], [in_])
        o, a = _ap(out), _ap(in_)
        return self.op(eng, lambda e: e.reciprocal(out=o, in_=a), r, w)

    def rmax(self, out, in_, eng="vector"):
        r, w = self._rw([out], [in_])
        o, a = _ap(out), _ap(in_)
        return self.op(eng, lambda e: e.reduce_max(out=o, in_=a, axis=AX.X), r, w)

    def scan(self, out, d0, d1, initial, op0=ALU.mult, op1=ALU.add):
        r, w = self._rw([out], [d0, d1, initial])
        o, a, b, i = _ap(out), _ap(d0), _ap(d1), _ap(initial)
        return self.op("vector", lambda e: e.tensor_tensor_scan(out=o, data0=a, data1=b, initial=i, op0=op0, op1=op1), r, w)

    def any2(self):
        self.rr += 1
        return ("vector", "gpsimd")[self.rr % 2]

    def any3(self):
        self.rr += 1
        return ("vector", "gpsimd", "scalar")[self.rr % 3]

    def evac(self):
        self.rr += 1
        return ("vector", "scalar")[self.rr % 2]

    def emit(self):
        nc = self.nc
        ops = self.ops
        for o in ops:
            for d in list(o.deps):
                do = ops[d]
                if (not do.dma) and (not o.dma) and do.eng == o.eng and o.eng == "tensor":
                    o.deps.discard(d)
                    continue
                do.sig = True
        sems = {e: nc.alloc_semaphore("s_" + e) for e in COMPUTE}
        dsems = {q: [nc.alloc_semaphore("d_%s_%d" % (q, i)) for i in range(N_DMA_SEMS)] for q in DMA_QUEUES}
        cnt = {e: 0 for e in COMPUTE}
        dcnt = {q: [0] * N_DMA_SEMS for q in DMA_QUEUES}
        for o in ops:
            if o.dma:
                j = o.slot
                o.sem = dsems[o.eng][j]
                o.prev_val = 16 * dcnt[o.eng][j]
                dcnt[o.eng][j] += 1
                o.val = 16 * dcnt[o.eng][j]
            elif o.sig:
                cnt[o.eng] += 1
                o.sem = sems[o.eng]
                o.val = cnt[o.eng]

        def run_stream(ename, eng, final=False):
            seen = {}
            for o in self.streams[ename]:
                waits = {}
                for d in o.deps:
                    do = ops[d]
                    k = id(do.sem)
                    if seen.get(k, 0) >= do.val:
                        continue
                    if k not in waits or waits[k][1] < do.val:
                        waits[k] = (do.sem, do.val)
                if o.dma and o.prev_val > 0:
                    k = id(o.sem)
                    if seen.get(k, 0) < o.prev_val and (k not in waits or waits[k][1] < o.prev_val):
                        waits[k] = (o.sem, o.prev_val)
                for k, (s, v) in waits.items():
                    eng.wait_ge(s, v)
                    seen[k] = v
                ins = o.fn(eng)
                if o.dma:
                    ins.then_inc(o.sem, 16)
                elif o.sig:
                    ins.then_inc(o.sem, 1)
            if final:
                for q in DMA_QUEUES:
                    for j in range(N_DMA_SEMS):
                        if dcnt[q][j] > 0:
                            eng.wait_ge(dsems[q][j], 16 * dcnt[q][j])
                for e in COMPUTE:
                    if cnt[e] > 0:
                        eng.wait_ge(sems[e], cnt[e])

        with nc.Block() as block:
            @block.sync
            def _(e):
                run_stream("sync", e, final=True)

            @block.tensor
            def _(e):
                run_stream("tensor", e)

            @block.vector
            def _(e):
                run_stream("vector", e)

            @block.scalar
            def _(e):
                run_stream("scalar", e)

            @block.gpsimd
            def _(e):
                run_stream("gpsimd", e)


class Ctx:
    pass


def cast_load(fw, C, dst, src, shape, q=None):
    st = C.stg[C.stg_i % len(C.stg)]
    C.stg_i += 1
    n = 1
    for s in shape[1:]:
        n *= s
    sv = st[:, 0:n]
    if len(shape) == 3:
        sv = sv.rearrange("p (a b) -> p a b", b=shape[2])
    fw.dma(("sync", "gpsimd")[C.stg_i % 2] if q is None else q, sv, src)
    fw.copy(fw.any3(), dst, sv)


def load_colvec(fw, C, dst, src_vec, J, tmpb):
    fw.dma("sync", tmpb[0:J, :], src_vec.rearrange("(j p) -> j p", p=128))
    ps = fw.ps()
    fw.transpose(ps[:, 0:J], tmpb[0:J, :], C.ident[0:J, 0:J])
    fw.copy("vector", dst, ps[:, 0:J])


def rms_rstd(fw, C, chunks, nfeat, N, sq, rstd, tmp):
    ps = fw.ps()
    nchunk = len(chunks)
    for c, xc in enumerate(chunks):
        fw.act(sq[:, c, 0:N], xc, AF.Square)
    for c in range(nchunk):
        fw.matmul(ps[:, 0:N], C.ones_bf[:, :], sq[:, c, 0:N], start=(c == 0), stop=(c == nchunk - 1))
    fw.act(tmp[:, 0:N], ps[:, 0:N], AF.Sqrt, scale=1.0 / nfeat, bias=C.eps_col[:, 0:1])
    fw.recip(rstd[:, 0:N], tmp[:, 0:N])
    return rstd


def stage_in(fw, C):
    m = fw.mark()
    xin = [fw.alloc("xin%d" % i, [128, 1024]) for i in range(2)]
    xo = [fw.alloc("xo%d" % i, [128, 8, 128]) for i in range(2)]
    xTv = C.xT.v.rearrange("(c p) t -> p c t", p=128)
    for tt in range(NT // 128):
        a = xin[tt % 2]
        o = xo[tt % 2]
        fw.dma("sync", a[:, :], C.x_tok[tt * 128:(tt + 1) * 128, :])
        for half in range(2):
            ps = fw.ps()
            for c in range(4):
                fw.transpose(ps[:, c * 128:(c + 1) * 128], a[:, (half * 4 + c) * 128:(half * 4 + c + 1) * 128], C.ident[:, :])
            fw.copy(fw.evac(), o[:, half * 4:(half + 1) * 4, :], ps[:, :].rearrange("p (c t) -> p c t", t=128))
        fw.dma("gpsimd", xTv[:, :, tt * 128:(tt + 1) * 128], o[:, :, :])
    fw.release(m)


def stage_mods(fw, C, l):
    m = fw.mark()
    wt = [fw.alloc("wmod%d" % i, [128, 8, 128]) for i in range(3)]
    raw = fw.alloc("modraw", [128, 48, 2])
    bm = fw.alloc("bmod", [128, 48])
    g1 = fw.alloc("gmix", [128, 8])
    g2 = fw.alloc("gffn", [128, 8])
    wv = C.w_mod[l].rearrange("(k p) f -> p k f", p=128)
    tb = [fw.alloc("tb%d" % i, [128, 128]) for i in range(3)]
    load_colvec(fw, C, bm[:, :], C.b_mod[l], 48, tb[0])
    load_colvec(fw, C, g1[:, :], C.g_norm_mix[l], 8, tb[1])
    load_colvec(fw, C, g2[:, :], C.g_norm_ffn[l], 8, tb[2])
    ps = fw.ps()
    for fo in range(48):
        w = wt[fo % 3]
        fw.dma(("sync", "gpsimd")[fo % 2], w[:, :, :], wv[:, :, fo * 128:(fo + 1) * 128])
        for k in range(8):
            fw.matmul(ps[:, fo * 2:fo * 2 + 2], w[:, k, :], C.cT[:, k, :], start=(k == 0), stop=(k == 7))
    fw.tt("vector", raw[:, :, :], ps[:, 0:96].rearrange("p (j n) -> p j n", n=2), bm[:, :].unsq(2).bc([128, 48, 2]), ALU.add)
    for (A, B, G, g, base) in ((C.mA_m, C.mB_m, C.mG_m, g1, 0), (C.mA_f, C.mB_f, C.mG_f, g2, 24)):
        fw.ts("vector", A[:, :, :], raw[:, base + 8:base + 16, :], 1.0, None, op0=ALU.add)
        fw.tt("vector", A[:, :, :], A[:, :, :], g[:, :].unsq(2).bc([128, 8, 2]), ALU.mult)
        fw.copy("vector", B[:, :, :], raw[:, base:base + 8, :])
        fw.copy("vector", G[:, :, :], raw[:, base + 16:base + 24, :])
    fw.release(m)


def load_win(fw, C, l, Win):
    wv = C.w_in[l].rearrange("(k p) f -> p k f", p=128)
    segs = [(R_QKV, 0, 1536), (R_Z, 1536, 512), (R_U, 2064, 512), (R_QA, 2576, 384), (R_KVA, 2960, 256), (R_GATE, 3248, 3072)]
    fw.memset("gpsimd", Win[:, :, R_MA:R_MA + 256], 0.0)
    for (dc, sc, wd) in segs:
        for o in range(0, wd, 256):
            w = min(256, wd - o)
            cast_load(fw, C, Win[:, :, dc + o:dc + o + w], wv[:, :, sc + o:sc + o + w], [128, 8, w])
    small = [(R_MA + 0, 2056, 8), (R_MA + 32, 2048, 8), (R_MA + 64, 3216, 32), (R_MB + 64, 3232, 16), (R_MB + 80, 3216, 16)]
    for (dc, sc, wd) in small:
        cast_load(fw, C, Win[:, :, dc:dc + wd], wv[:, :, sc:sc + wd], [128, 8, wd])


def stage_inproj(fw, C, l):
    m = fw.mark()
    Win = fw.alloc("Win", [128, 8, PW], BF16)
    m2 = fw.mark()
    C.stg = [fw.alloc("stg%d" % i, [128, 2048]) for i in range(2)]
    C.stg_i = 0
    load_win(fw, C, l, Win)
    fw.release(m2)
    xt = [fw.alloc("xt%d" % i, [128, 8, 512]) for i in range(2)]
    sq = fw.alloc("sq", [128, 8, 512], BF16)
    h = [fw.alloc("h%d" % i, [128, 8, 512], BF16) for i in range(2)]
    rstd = fw.alloc("rstd", [128, 512])
    tmp = [fw.alloc("tmp%d" % i, [128, 512]) for i in range(2)]
    so = [fw.alloc("so%d" % i, [128, 4, 512]) for i in range(3)]
    xTv = C.xT.v.rearrange("(c p) t -> p c t", p=128)
    pv = C.projT.v.rearrange("(j p) t -> p j t", p=128)
    soi = 0
    for t in range(NT // 512):
        n = 0 if t < 2 else 1
        x = xt[t % 2]
        hh = h[t % 2]
        fw.dma("sync", x[:, :, :], xTv[:, :, t * 512:(t + 1) * 512])
        rms_rstd(fw, C, [x[:, c, :] for c in range(8)], D, 512, sq, rstd, tmp[0])
        for c in range(8):
            tm = tmp[c % 2]
            fw.tt(("vector", "gpsimd")[c % 2], tm[:, :], x[:, c, :], rstd[:, :], ALU.mult)
            fw.act(hh[:, c, :], tm[:, :], AF.Identity, scale=C.mA_m[:, c, n:n + 1], bias=C.mB_m[:, c, n:n + 1])
        for fo in range(NFO):
            ps = fw.ps()
            for k in range(8):
                fw.matmul(ps[:, :], Win[:, k, fo * 128:(fo + 1) * 128], hh[:, k, :], start=(k == 0), stop=(k == 7))
            s = so[soi % 3]
            fw.copy(fw.evac(), s[:, fo % 4, :], ps[:, :])
            if fo % 4 == 3 or fo == NFO - 1:
                f0 = fo - fo % 4
                nn = fo % 4 + 1
                fw.dma("gpsimd", pv[:, f0:f0 + nn, t * 512:(t + 1) * 512], s[:, 0:nn, :])
                soi += 1
    fw.release(m)


TS5 = 16
NCH = NT // TS5
TWO_PI = 2.0 * math.pi


def load_T(fw, C, dst, src2d, J, tmpb):
    fw.dma("sync", tmpb[0:J, :], src2d)
    ps = fw.ps()
    fw.transpose(ps[:, 0:J], tmpb[0:J, :], C.ident[0:J, 0:J])
    fw.copy("vector", dst, ps[:, 0:J])


def cmul(fw, eng, outr, outi, ar, ai, br, bi, t1, t2, nai=None):
    fw.tt(eng, t1, ar, br, ALU.mult)
    fw.tt(eng, t2, ai, bi, ALU.mult)
    fw.tt(eng, outr, t1, t2, ALU.subtract)
    fw.tt(eng, t1, ar, bi, ALU.mult)
    fw.tt(eng, t2, ai, br, ALU.mult)
    fw.tt(eng, outi, t1, t2, ALU.add)


def stage_s5(fw, C, l):
    m = fw.mark()
    pv = C.projT.v.rearrange("(j p) t -> p j t", p=128)
    T = TS5
    U = fw.alloc("U", [128, 4, T, NCH], BF16)
    BD = fw.alloc("BD", [128, 31, 4, 128], BF16)
    HB = fw.alloc("HB", [128, 2, 2, 16, NCH], BF16)
    Er = fw.alloc("Er", [128, 2, 17, 16])
    Ei = fw.alloc("Ei", [128, 2, 17, 16])
    Cbd = fw.alloc("Cbd", [128, 2, 16, 32])
    Wglu = fw.alloc("Wglu", [128, 4, 512], BF16)
    dsk = fw.alloc("dsk", [128, 4])
    bgl = fw.alloc("bgl", [128, 4])
    fin = fw.alloc("fin", [128, 2, 2, NPS, 16])
    m2 = fw.mark()
    HA = fw.alloc("HA", [128, 2, 16, NCH + 1])
    HBk = fw.alloc("HBk", [128, 2, 16, NCH + 1])
    Bb = fw.alloc("Bb", [128, 2, 3, 16, 32])
    C.stg = [fw.alloc("stg%d" % i, [128, 2048]) for i in range(1)]
    C.stg_i = 0
    tb = [fw.alloc("tb%d" % i, [128, 128]) for i in range(4)]
    m3 = fw.mark()
    wv = C.w_glu[l].rearrange("(k p) f -> p k f", p=128)
    cast_load(fw, C, Wglu[:, :, :], wv, [128, 4, 512])
    load_colvec(fw, C, dsk[:, :], C.s5_d[l], 4, tb[0])
    load_colvec(fw, C, bgl[:, :], C.b_glu[l], 4, tb[1])
    lam = fw.alloc("lam", [128, 2, 2, 16])
    dtb = fw.alloc("dtb", [128, 2, 16])
    small = fw.alloc("small", [16, 2])
    for d in range(2):
        load_T(fw, C, lam[:, d, 0, :], C.s5_lam_re[l, d].rearrange("(j g) p -> j (g p)", g=2), 16, tb[2])
        load_T(fw, C, lam[:, d, 1, :], C.s5_lam_im[l, d].rearrange("(j g) p -> j (g p)", g=2), 16, tb[3])
        fw.dma("sync", small[:, :], C.s5_log_dt[l, d].rearrange("(j g) -> j g", g=2))
        fw.copy("vector", tb[2][0:16, :].rearrange("j (g p) -> j g p", g=2), small[:, :].unsq(2).bc([16, 2, 64]))
        ps = fw.ps()
        fw.transpose(ps[:, 0:16], tb[2][0:16, :], C.ident[0:16, 0:16])
        fw.act(dtb[:, d, :], ps[:, 0:16], AF.Exp)
    rho = fw.alloc("rho", [128, 2, 16])
    th = fw.alloc("th", [128, 2, 16])
    fw.tt("vector", rho[:, :, :], lam[:, :, 0, :], dtb[:, :, :], ALU.mult)
    fw.tt("vector", th[:, :, :], lam[:, :, 1, :], dtb[:, :, :], ALU.mult)
    ang = fw.alloc("ang", [128, 17, 16])
    kf = fw.alloc("kf", [128, 17, 16])
    ki = fw.alloc("ki", [128, 17, 16], I32)
    msk = fw.alloc("msk", [128, 17, 16])
    mag = fw.alloc("mag", [128, 17, 16])
    sn = fw.alloc("sn", [128, 17, 16])
    cs = fw.alloc("cs", [128, 17, 16])
    svb = C.svec[:, :].unsq(2).bc([128, 17, 16])
    for d in range(2):
        fw.tt("vector", ang[:, :, :], th[:, d, :].unsq(1).bc([128, 17, 16]), svb, ALU.mult)
        fw.ts("vector", kf[:, :, :], ang[:, :, :], 1.0 / TWO_PI, None, op0=ALU.mult)
        fw.copy("vector", ki[:, :, :], kf[:, :, :])
        fw.copy("vector", kf[:, :, :], ki[:, :, :])
        fw.stt("vector", ang[:, :, :], kf[:, :, :], -TWO_PI, ang[:, :, :], ALU.mult, ALU.add)
        fw.ts("vector", msk[:, :, :], ang[:, :, :], math.pi, None, op0=ALU.is_gt)
        fw.stt("vector", ang[:, :, :], msk[:, :, :], -TWO_PI, ang[:, :, :], ALU.mult, ALU.add)
        fw.ts("vector", msk[:, :, :], ang[:, :, :], -math.pi, None, op0=ALU.is_lt)
        fw.stt("vector", ang[:, :, :], msk[:, :, :], TWO_PI, ang[:, :, :], ALU.mult, ALU.add)
        fw.act(sn[:, :, :], ang[:, :, :], AF.Sin)
        fw.ts("vector", msk[:, :, :], ang[:, :, :], -1.0, None, op0=ALU.mult)
        fw.tt("vector", msk[:, :, :], msk[:, :, :], ang[:, :, :], ALU.max)
        fw.act(cs[:, :, :], msk[:, :, :], AF.Sin, scale=-1.0, bias=C.halfpi[:, 0:1])
        fw.tt("vector", mag[:, :, :], rho[:, d, :].unsq(1).bc([128, 17, 16]), svb, ALU.mult)
        fw.act(mag[:, :, :], mag[:, :, :], AF.Exp)
        fw.tt("vector", Er[:, d, :, :], mag[:, :, :], cs[:, :, :], ALU.mult)
        fw.tt("vector", Ei[:, d, :, :], mag[:, :, :], sn[:, :, :], ALU.mult)
    cf = fw.alloc("cf", [128, 2, 2, 16])
    t1 = fw.alloc("t1", [128, 2, 16])
    t2 = fw.alloc("t2", [128, 2, 16])
    den = fw.alloc("den", [128, 2, 16])
    nr = fw.alloc("nr", [128, 2, 16])
    fw.tt("vector", t1[:, :, :], lam[:, :, 0, :], lam[:, :, 0, :], ALU.mult)
    fw.tt("vector", t2[:, :, :], lam[:, :, 1, :], lam[:, :, 1, :], ALU.mult)
    fw.tt("vector", den[:, :, :], t1[:, :, :], t2[:, :, :], ALU.add)
    fw.recip(den[:, :, :], den[:, :, :])
    fw.ts("vector", nr[:, :, :], Er[:, :, 1, :], -1.0, None, op0=ALU.add)
    fw.tt("vector", t1[:, :, :], nr[:, :, :], lam[:, :, 0, :], ALU.mult)
    fw.tt("vector", t2[:, :, :], Ei[:, :, 1, :], lam[:, :, 1, :], ALU.mult)
    fw.tt("vector", t1[:, :, :], t1[:, :, :], t2[:, :, :], ALU.add)
    fw.tt("vector", cf[:, :, 0, :], t1[:, :, :], den[:, :, :], ALU.mult)
    fw.tt("vector", t1[:, :, :], Ei[:, :, 1, :], lam[:, :, 0, :], ALU.mult)
    fw.tt("vector", t2[:, :, :], nr[:, :, :], lam[:, :, 1, :], ALU.mult)
    fw.tt("vector", t1[:, :, :], t1[:, :, :], t2[:, :, :], ALU.subtract)
    fw.tt("vector", cf[:, :, 1, :], t1[:, :, :], den[:, :, :], ALU.mult)
    Bn = fw.alloc("Bn", [128, 2, 16, 16])
    fw.dma("sync", Bn[:, 0, :, :], C.s5_b_re[l].rearrange("(j g) p c -> (g p) j c", g=2))
    fw.dma("sync", Bn[:, 1, :, :], C.s5_b_im[l].rearrange("(j g) p c -> (g p) j c", g=2))
    bb = fw.alloc("bb", [128, 2, 16, 16])
    u1 = fw.alloc("u1", [128, 16, 16])
    u2 = fw.alloc("u2", [128, 16, 16])
    mS = C.maskS[:, :].unsq(1).unsq(3).bc([128, 16, 2, 16])
    for d in range(2):
        cr = cf[:, d, 0, :].unsq(2).bc([128, 16, 16])
        ci = cf[:, d, 1, :].unsq(2).bc([128, 16, 16])
        cmul(fw, "vector", bb[:, 0, :, :], bb[:, 1, :, :], cr, ci, Bn[:, 0, :, :], Bn[:, 1, :, :], u1[:, :, :], u2[:, :, :])
        for ri in range(2):
            fw.tt("vector", Bb[:, d, ri, :, :].rearrange("p j (g c) -> p j g c", g=2), bb[:, ri, :, :].unsq(2).bc([128, 16, 2, 16]), mS, ALU.mult)
        fw.ts("vector", Bb[:, d, 2, :, :], Bb[:, d, 1, :, :], -1.0, None, op0=ALU.mult)
    cn = fw.alloc("cn", [128, 64])
    cx = fw.alloc("cx", [128, 2, 64])
    for ri, src in enumerate((C.s5_c_re, C.s5_c_im)):
        cv = src[l].rearrange("g c p -> (g c) p")
        for gh in range(4):
            fw.dma("sync", cn[:, :], cv[gh * 128:(gh + 1) * 128, :])
            fw.tt("vector", cx[:, :, :], cn[:, :].unsq(1).bc([128, 2, 64]), C.maskR[:, :].unsq(2).bc([128, 2, 64]), ALU.mult)
            ps = fw.ps()
            fw.transpose(ps[:, 0:128], cx[:, :, :].rearrange("p g q -> p (g q)"), C.ident[:, :])
            fw.copy("vector", Cbd[:, ri, gh * 4:(gh + 1) * 4, :].rearrange("p j x -> p (j x)"), ps[:, 0:128])
    uf = fw.alloc("uf", [128, 512])
    for gh in range(4):
        for t in range(NT // 512):
            fw.dma("sync", uf[:, :], pv[:, R_U // 128 + gh, t * 512:(t + 1) * 512])
            fw.copy(fw.any2(), U[:, gh, :, t * 32:(t + 1) * 32], uf[:, :].rearrange("p (k i) -> p i k", i=T))
    Gt = fw.alloc("Gt", [128, 2, 2, 16, 32])
    Gb = fw.alloc("Gb", [128, 2, 2, 16, 32], BF16)
    Bbb = fw.alloc("Bbb", [128, 2, 3, 16, 32], BF16)
    fw.copy("vector", Bbb[:, :, :, :, :], Bb[:, :, :, :, :])
    g1 = fw.alloc("g1", [128, 16, 32])
    g2 = fw.alloc("g2", [128, 16, 32])
    blk = fw.alloc("blk", [128, 64])
    fw.memset("gpsimd", BD[:, :, :, :], 0.0)
    mQ = C.maskQ[:, :].unsq(2).bc([128, 4, 32])

    def make_G(dst, d, s, j0, j1, t1_, t2_, neg_im):
        er = Er[:, d, s, j0:j1].unsq(2).bc([128, j1 - j0, 32])
        ei = Ei[:, d, s, j0:j1].unsq(2).bc([128, j1 - j0, 32])
        cmul(fw, "vector", dst[0], dst[1], Cbd[:, 0, j0:j1, :], Cbd[:, 1, j0:j1, :], er, ei, t1_, t2_)

    for dd in range(16):
        for d in range(2):
            make_G((Gt[:, d, 0, :, :], Gt[:, d, 1, :, :]), d, dd, 0, 16, g1[:, :, :], g2[:, :, :], False)
        fw.copy("gpsimd", Gb[:, :, :, :, :], Gt[:, :, :, :, :])
        for gh in range(4):
            ps = fw.ps()
            for jl in (3, 0, 1, 2):
                j = gh * 4 + jl
                if jl == 3:
                    o_ = ps[64:128, :]
                    lh = lambda d, bsel: Bbb[:, d, bsel, j - 1:j + 1, :].rearrange("p a b -> p (a b)")
                else:
                    o_ = ps[32 * jl:32 * jl + 32, :]
                    lh = lambda d, bsel: Bbb[:, d, bsel, j, :]
                if dd == 0:
                    seq = [(0, 0, 0), (0, 2, 1), (1, 0, 0), (1, 2, 1)]
                    for n_, (d, bsel, ri) in enumerate(seq):
                        fw.matmul(o_[:, 0:32], lh(d, bsel), Gb[:, d, ri, j, :], start=(n_ == 0), stop=(n_ == 3))
                else:
                    for d in range(2):
                        fw.matmul(o_[:, 32 * d:32 * d + 32], lh(d, 0), Gb[:, d, 0, j, :], start=True, stop=False)
                        fw.matmul(o_[:, 32 * d:32 * d + 32], lh(d, 2), Gb[:, d, 1, j, :], start=False, stop=True)
            fw.copy("vector", blk[:, :], ps[:, 0:64])
            if dd == 0:
                fw.tt("vector", BD[:, 15, gh, :].rearrange("p (a b) -> p a b", b=32), blk[:, 0:32].unsq(1).bc([128, 4, 32]), mQ, ALU.mult)
            else:
                fw.tt("vector", BD[:, 15 + dd, gh, :].rearrange("p (a b) -> p a b", b=32), blk[:, 0:32].unsq(1).bc([128, 4, 32]), mQ, ALU.mult)
                fw.tt("vector", BD[:, 15 - dd, gh, :].rearrange("p (a b) -> p a b", b=32), blk[:, 32:64].unsq(1).bc([128, 4, 32]), mQ, ALU.mult)
    fw.release(m3)
    LX = fw.alloc("LX", [128, 16, 2, 128], BF16)
    LX3 = fw.alloc("LX3", [128, 16, 2, 128], BF16)
    NCHN = 4
    wt = [fw.alloc("wt%d" % i, [128, 2, 4, 32]) for i in range(NCHN)]
    w1 = [fw.alloc("w1_%d" % i, [128, 4, 32]) for i in range(NCHN)]
    w2 = [fw.alloc("w2_%d" % i, [128, 4, 32]) for i in range(NCHN)]

    def rr(gens):
        gens = list(gens)
        while gens:
            for g in list(gens):
                try:
                    next(g)
                except StopIteration:
                    gens.remove(g)

    def cmul_g(eng, outr, outi, ar, ai, br, bi, t1, t2):
        fw.tt(eng, t1, ar, br, ALU.mult)
        fw.tt(eng, t2, ai, bi, ALU.mult)
        yield
        fw.tt(eng, outr, t1, t2, ALU.subtract)
        yield
        fw.tt(eng, t1, ar, bi, ALU.mult)
        fw.tt(eng, t2, ai, br, ALU.mult)
        yield
        fw.tt(eng, outi, t1, t2, ALU.add)
        yield

    for d in range(2):
        Hdst = HA if d == 0 else HBk
        slot0 = 1 if d == 0 else 0
        for gh in range(4):
            def chain(cn):
                for i in range(cn, 16, NCHN):
                    s_ = 15 - i if d == 0 else i
                    w = wt[cn]
                    er = Er[:, d, s_, gh * 4:(gh + 1) * 4].unsq(2).bc([128, 4, 32])
                    ei = Ei[:, d, s_, gh * 4:(gh + 1) * 4].unsq(2).bc([128, 4, 32])
                    yield from cmul_g(("vector", "gpsimd")[cn % 2], w[:, 0, :, :], w[:, 1, :, :], Bb[:, d, 0, gh * 4:(gh + 1) * 4, :], Bb[:, d, 1, gh * 4:(gh + 1) * 4, :], er, ei, w1[cn][:, :, :], w2[cn][:, :, :])
                    ps = fw.ps()
                    for ri in range(2):
                        fw.transpose(ps[:, ri * 128:(ri + 1) * 128], w[:, ri, :, :].rearrange("p a b -> p (a b)"), C.ident[:, :])
                    yield
                    fw.copy(("vector", "scalar")[cn % 2], LX[:, i, :, :], ps[:, 0:256].rearrange("p (r x) -> p r x", r=2))
                    yield
                    fw.ts("vector", LX3[64:128, i, :, :], LX[64:128, i, :, :], C.rm3[64:128, 0:1], None, op0=ALU.mult)
                    yield
            rr([chain(cn) for cn in range(NCHN)])
            for jl in range(4):
                for ri in range(2):
                    ps = fw.ps()
                    for i in range(16):
                        if jl == 3:
                            fw.matmul(ps[:, 0:NCH], LX3[64:128, i, ri, :], U[64:128, gh, i, :], start=(i == 0), stop=(i == 15))
                        else:
                            fw.matmul(ps[:, 0:NCH], LX[32 * jl:32 * jl + 32, i, ri, :], U[32 * jl:32 * jl + 32, gh, i, :], start=(i == 0), stop=(i == 15))
                    fw.copy(fw.evac(), Hdst[:, ri, gh * 4 + jl, slot0:slot0 + NCH], ps[:, 0:NCH])
    h0 = fw.alloc("h0", [128, 2, 2, 16])
    for d in range(2):
        load_T(fw, C, h0[:, d, 0, :], C.s5re[l, d].rearrange("(j g) p -> j (g p)", g=2), 16, tb[0])
        load_T(fw, C, h0[:, d, 1, :], C.s5im[l, d].rearrange("(j g) p -> j (g p)", g=2), 16, tb[1])
    sa = [fw.alloc("sa%d" % i, [128, 2, 16]) for i in range(2)]
    sb = [fw.alloc("sb%d" % i, [128, 2, 16]) for i in range(2)]
    mu3 = fw.alloc("mu3", [128, 2, 3, 16])
    for d in range(2):
        fw.copy("vector", mu3[:, d, 0, :], Er[:, d, 16, :])
        fw.copy("vector", mu3[:, d, 1, :], Ei[:, d, 16, :])
        fw.ts("vector", mu3[:, d, 2, :], Ei[:, d, 16, :], -1.0, None, op0=ALU.mult)
    for d in range(2):
        eng = ("vector", "gpsimd")[d]
        H = HA if d == 0 else HBk
        a, b = sa[d], sb[d]
        mur = mu3[:, d, 0, :].unsq(1).bc([128, 2, 16])
        order = list(enumerate(SEQS)) if d == 0 else list(enumerate(SEQS))[::-1]
        for si, (off, L, smp) in order:
            k0, k1 = off // T, (off + L) // T
            if d == 0:
                init, steps = k0, [(k, k + 1) for k in range(k0, k1)]
            else:
                init, steps = k1, [(k, k - 1) for k in range(k1, k0, -1)]
            if smp:
                fw.copy(eng, H[:, :, :, init], h0[:, d, :, :])
            else:
                fw.memset(eng, H[:, :, :, init], 0.0)
            for n_, (src, dst) in enumerate(steps):
                last = (n_ == len(steps) - 1)
                fw.tt(eng, a[:, :, :], H[:, :, :, src], mur, ALU.mult)
                fw.tt(eng, b[:, 0, :], H[:, 1, :, src], mu3[:, d, 2, :], ALU.mult)
                fw.tt(eng, b[:, 1, :], H[:, 0, :, src], mu3[:, d, 1, :], ALU.mult)
                fw.tt(eng, a[:, :, :], a[:, :, :], b[:, :, :], ALU.add)
                if last:
                    if not smp:
                        fw.tt(eng, fin[:, d, :, si, :], a[:, :, :], H[:, :, :, dst], ALU.add)
                else:
                    fw.tt(eng, H[:, :, :, dst], a[:, :, :], H[:, :, :, dst], ALU.add)
    fo_ = fw.alloc("fo", [16, 128])
    for d in range(2):
        for ri, dst in enumerate((C.o_s5re, C.o_s5im)):
            for si in range(NPS):
                ps = fw.ps()
                fw.transpose(ps[0:16, 0:128], fin[:, d, ri, si, :], C.ident[:, :])
                fw.copy("vector", fo_[:, :], ps[0:16, 0:128])
                fw.dma("gpsimd", dst[si, l, d].rearrange("(j g) p -> j (g p)", g=2), fo_[:, :])
    fw.copy("vector", HB[:, 0, :, :, :], HA[:, :, :, 0:NCH])
    fw.copy("gpsimd", HB[:, 1, :, :, :], HBk[:, :, :, 1:NCH + 1])
    fw.release(m2)
    yg = fw.alloc("yg", [128, 4, NT], BF16)
    Y = fw.alloc("Y", [128, NT])
    uf = fw.alloc("uf2", [128, NT])
    Gm = [fw.alloc("Gm%d" % i, [128, 2, 2, 4, 32]) for i in range(2)]
    Gmb = [fw.alloc("Gmb%d" % i, [128, 2, 2, 5, 32], BF16) for i in range(2)]
    for g_ in Gmb:
        fw.memset("vector", g_[:, :, :, :, :], 0.0)
    g1 = fw.alloc("g1m", [128, 4, 32])
    g2 = fw.alloc("g2m", [128, 4, 32])
    y2 = fw.alloc("y2", [128, NT])
    y3 = fw.alloc("y3", [128, NT])
    for gh in range(4):
        fw.dma("sync", uf[:, :], pv[:, R_U // 128 + gh, :])
        for j in range(16):
            G_, Gb_ = Gm[j % 2], Gmb[j % 2]
            for d in range(2):
                s_ = j + 1 if d == 0 else 16 - j
                er = Er[:, d, s_, gh * 4:(gh + 1) * 4].unsq(2).bc([128, 4, 32])
                ei = Ei[:, d, s_, gh * 4:(gh + 1) * 4].unsq(2).bc([128, 4, 32])
                cmul(fw, ("vector", "gpsimd")[d], G_[:, d, 0, :, :], G_[:, d, 1, :, :], Cbd[:, 0, gh * 4:(gh + 1) * 4, :], Cbd[:, 1, gh * 4:(gh + 1) * 4, :], er, ei, g1[:, :, :] if d == 0 else y2[:, 0:128].rearrange("p (a b) -> p a b", b=32), g2[:, :, :] if d == 0 else y3[:, 0:128].rearrange("p (a b) -> p a b", b=32))
                fw.ts(("vector", "gpsimd")[d], G_[:, d, 1, :, :], G_[:, d, 1, :, :], -1.0, None, op0=ALU.mult)
            fw.copy("vector", Gb_[:, :, :, 0:3, :], G_[:, :, :, 0:3, :])
            fw.copy("vector", Gb_[:, :, :, 4, :], G_[:, :, :, 3, :])
            ps = fw.ps()
            for i in range(16):
                fw.matmul(ps[:, 0:NCH], BD[:, 15 + j - i, gh, :], U[:, gh, i, :], start=(i == 0), stop=False)
            for jl in range(4):
                n_ = 0
                for d in range(2):
                    for ri in range(2):
                        n_ += 1
                        if jl == 3:
                            fw.matmul(ps[64:128, 0:NCH], Gb_[:, d, ri, 3:5, :].rearrange("p a b -> p (a b)"), HB[:, d, ri, gh * 4 + jl, :], start=False, stop=(n_ == 4))
                        else:
                            fw.matmul(ps[32 * jl:32 * jl + 32, 0:NCH], Gb_[:, d, ri, jl, :], HB[:, d, ri, gh * 4 + jl, :], start=False, stop=(n_ == 4 and jl != 2))
            fw.copy(fw.evac(), Y[:, :].rearrange("p (k i) -> p i k", i=T)[:, j, :], ps[:, 0:NCH])
        fw.stt("vector", Y[:, :], uf[:, :], dsk[:, gh:gh + 1], Y[:, :], ALU.mult, ALU.add)
        fw.tt("gpsimd", y2[:, :], Y[:, :], Y[:, :], ALU.mult)
        fw.ts("vector", y2[:, :], y2[:, :], 0.044715, 1.0, op0=ALU.mult, op1=ALU.add)
        fw.tt("gpsimd", y2[:, :], y2[:, :], Y[:, :], ALU.mult)
        fw.act(y3[:, :], y2[:, :], AF.Sigmoid, scale=2.0 * 0.7978845608028654)
        fw.tt("vector", yg[:, gh, :], Y[:, :], y3[:, :], ALU.mult)
    ob = [fw.alloc("ob%d" % i, [128, 4, 512], BF16) for i in range(2)]
    sg = [fw.alloc("sgl%d" % i, [128, 512]) for i in range(2)]
    bT = C.brT.v[1].rearrange("(c p) t -> p c t", p=128)
    for t in range(NT // 512):
        o = ob[t % 2]
        for oc_ in range(4):
            ps = fw.ps()
            for k in range(4):
                fw.matmul(ps[:, :], Wglu[:, k, oc_ * 128:(oc_ + 1) * 128], yg[:, k, t * 512:(t + 1) * 512], start=(k == 0), stop=(k == 3))
            sgt = sg[oc_ % 2]
            fw.act(sgt[:, :], ps[:, :], AF.Sigmoid, bias=bgl[:, oc_:oc_ + 1])
            fw.tt(fw.any2(), o[:, oc_, :], yg[:, oc_, t * 512:(t + 1) * 512], sgt[:, :], ALU.mult)
        fw.dma("gpsimd", bT[:, :, t * 512:(t + 1) * 512], o[:, :, :])
    fw.release(m)


NB = NT // 128
GSC = 128 ** -0.5
import os
GDN_STOP = int(os.environ.get('GDN_STOP', '9'))
BULK_STOP = int(os.environ.get('BULK_STOP', '9'))


def stage_gdn(fw, C, l):
    m = fw.mark()
    pv = C.projT.v.rearrange("(j p) t -> p j t", p=128)
    qkT = fw.alloc("qkT", [128, 12, NT], BF16)
    osum = fw.alloc("osum", [128, 4, NT])
    Rall = fw.alloc("Rall", [128, NT])
    Col = fw.alloc("Col", [128, NB, 32])
    gout = fw.alloc("gout", [128, 1])
    masks = fw.alloc("gmasks", [128, 4, 128])
    sel = fw.alloc("gsel", [128, 16, 128])
    fw.memset("vector", Rall[:, :], 0.0)
    fw.dma("sync", masks[:, :, :], C.c_gmask)
    fw.dma("sync", sel[:, :, :], C.c_gsel)
    fw.dma("sync", gout[:, :], C.g_delta_out[l].rearrange("(p o) -> p o", o=1))
    m2 = fw.mark()
    cw = fw.alloc("cw", [128, 5, 12])
    tb = [fw.alloc("tb%d" % i, [128, 128]) for i in range(5)]
    for i in range(5):
        load_colvec(fw, C, cw[:, i, :], C.conv_qkv[l, i], 12, tb[i])
    xc = [fw.alloc("xc%d" % i, [128, NT]) for i in range(1)]
    acc = [fw.alloc("acc%d" % i, [128, NT]) for i in range(1)]
    sq = fw.alloc("sq", [128, 1, 512], BF16)
    rstd = fw.alloc("rstd", [128, 512])
    tmp = fw.alloc("tmp", [128, 512])
    for c in range(12):
        x, a = xc[0], acc[0]
        fw.dma("sync", x[:, :], pv[:, c, :])
        fw.act(a[:, :], x[:, :], AF.Identity, scale=cw[:, 2, c:c + 1])
        for i in (0, 1, 3, 4):
            sh = i - 2
            for (off, L, smp) in SEQS:
                lo, hi = max(0, -sh), min(L, L - sh)
                eng = ("vector", "gpsimd")[(i + (off // LP)) % 2]
                fw.stt(eng, a[:, off + lo:off + hi], x[:, off + lo + sh:off + hi + sh], cw[:, i, c:c + 1], a[:, off + lo:off + hi], ALU.mult, ALU.add)
        fw.act(a[:, :], a[:, :], AF.Silu)
        if c < 8:
            for t in range(NT // 512):
                tok = slice(t * 512, (t + 1) * 512)
                ps = fw.ps()
                fw.act(sq[:, 0, :], a[:, tok], AF.Square)
                fw.matmul(ps[:, :], C.ones_bf[:, :], sq[:, 0, :])
                fw.act(tmp[:, :], ps[:, :], AF.Sqrt, scale=1.0, bias=C.eps_col[:, 0:1])
                fw.recip(rstd[:, :], tmp[:, :])
                if c < 4:
                    fw.stt("vector", qkT[:, c, tok], a[:, tok], GSC, rstd[:, :], ALU.mult, ALU.mult)
                else:
                    fw.tt("vector", qkT[:, c, tok], a[:, tok], rstd[:, :], ALU.mult)
        else:
            fw.copy("gpsimd", qkT[:, c, :], a[:, :])
    fw.release(m2)
    if GDN_STOP <= 1:
        fw.release(m)
        return
    m2 = fw.mark()
    Rall2 = fw.alloc("Rall2", [128, NT])
    al = fw.alloc("al", [128, NT])
    P = fw.alloc("P", [128, NT])
    tot = fw.alloc("tot", [128, NT // 64])
    rst = fw.alloc("rst", [128, NT])
    colp = fw.alloc("colp", [128, 4])
    fw.dma("sync", al[0:8, :], C.projT[R_MA:R_MA + 8, :])
    fw.dma("sync", Rall[32:40, :], C.projT[R_MA + 32:R_MA + 40, :])
    fw.dma("sync", rst[0:8, :], C.c_rst)
    fw.dma("sync", colp[0:8, 0:1], C.dt_bias[l].rearrange("d (h o) -> (d h) o", o=1))
    fw.dma("sync", colp[0:8, 1:2], C.a_log[l].rearrange("d (h o) -> (d h) o", o=1))
    fw.dma("sync", colp[0:8, 2:3], C.c_mdir)
    fw.memset("vector", colp[0:8, 3:4], 1.0)
    fw.act(colp[0:8, 1:2], colp[0:8, 1:2], AF.Exp)
    fw.ts("vector", colp[0:8, 1:2], colp[0:8, 1:2], -1.0, None, op0=ALU.mult)
    fw.act(al[0:8, :], al[0:8, :], AF.Exp, bias=colp[0:8, 0:1])
    fw.act(al[0:8, :], al[0:8, :], AF.Ln, bias=colp[0:8, 3:4])
    fw.ts("vector", al[0:8, :], al[0:8, :], colp[0:8, 1:2], None, op0=ALU.mult)
    fw.act(Rall[32:40, :], Rall[32:40, :], AF.Sigmoid)
    fw.scan(P[0:8, :], rst[0:8, :], al[0:8, :], 0.0)
    P3 = P[0:8, :].rearrange("r (k c) -> r k c", c=64)
    fw.copy("vector", tot[0:8, :], P3[:, :, 63])
    totb = tot[0:8, :].unsq(2).bc([8, NT // 64, 64])
    S3 = Rall2[0:8, :].rearrange("r (k c) -> r k c", c=64)
    a3 = al[0:8, :].rearrange("r (k c) -> r k c", c=64)
    G3 = Rall[0:8, :].rearrange("r (k c) -> r k c", c=64)
    fw.tt("vector", S3, totb, P3, ALU.subtract)
    fw.tt("vector", S3, S3, a3, ALU.add)
    fw.tt("vector", S3, S3, P3, ALU.subtract)
    fw.stt("vector", Rall[0:8, :], Rall2[0:8, :], colp[0:8, 2:3], P[0:8, :], ALU.mult, ALU.add)
    fw.tt("vector", S3, totb, G3, ALU.subtract)
    fw.act(Rall2[0:8, :], Rall2[0:8, :], AF.Exp)
    for b in range(NB):
        ps = fw.ps()
        fw.transpose(ps[:, 0:128], Rall[:, b * 128:(b + 1) * 128], C.ident[:, :])
        fw.transpose(ps[:, 128:256], Rall2[:, b * 128:(b + 1) * 128], C.ident[:, :])
        fw.copy("vector", Col[:, b, 0:8], ps[:, 0:8])
        fw.copy("vector", Col[:, b, 8:16], ps[:, 32:40])
        fw.copy("vector", Col[:, b, 24:32], ps[:, 128:136])
        fw.act(Col[:, b, 16:24], ps[:, 0:8], AF.Exp)
        fw.tt("vector", Col[:, b, 16:24], Col[:, b, 16:24], Col[:, b, 8:16], ALU.mult)
        fw.ts("vector", Col[:, b, 16:24], Col[:, b, 16:24], -1.0, None, op0=ALU.mult)
    fw.release(m2)
    NU = 2
    def mk(nm, dt=F32):
        return [[fw.alloc("%s_%d_%d" % (nm, h, u), [128, 128], dt) for u in range(NU)] for h in range(4)]
    TTb, QKb, qtb, ktb, bvb = mk("TT", F32), mk("QK", BF16), mk("qt", BF16), mk("kt", BF16), mk("bv", F32)
    ED = fw.alloc("ED", [128, 4, NU, 2])
    def mkh(nm, n, dt=F32):
        return [[fw.alloc("%s_%d_%d" % (nm, h, i), [128, 128], dt) for i in range(n)] for h in range(4)]
    E_h, ET_h, t_h, eg_h, Af_h, Bf_h, Pf_h = mkh("E", 1), mkh("ET", 1), mkh("t", 4), mkh("eg", 1), mkh("Af", 2), mkh("Bf", 2), mkh("Pf", 2)
    Sst = [fw.alloc("S%d" % h, [128, 128]) for h in range(4)]
    Sbf = [fw.alloc("Sb%d" % h, [128, 128], BF16) for h in range(4)]
    rhsb = [fw.alloc("rhs%d" % h, [128, 128]) for h in range(4)]
    vnb = [fw.alloc("vn%d" % h, [128, 128], BF16) for h in range(4)]

    def bulk(d, h, b, u):
        r = d * 4 + h
        blk = slice(b * 128, (b + 1) * 128)
        mS, mT, mI = (masks[:, 0, :], masks[:, 1, :], masks[:, 3, :]) if d == 0 else (masks[:, 1, :], masks[:, 0, :], masks[:, 2, :])
        gcol, bcol, dlcol = Col[:, b, r:r + 1], Col[:, b, 8 + r:9 + r], Col[:, b, 24 + r:25 + r]
        E, ET, eg, t_, Af, Bf, Pf = E_h[h][0], ET_h[h][0], eg_h[h][0], t_h[h], Af_h[h], Bf_h[h], Pf_h[h]
        bank = fw.psb[h]
        fw.matmul(bank[:, 0:128], sel[:, r, :], Rall[:, blk])
        fw.matmul(bank[:, 128:256], sel[:, 8 + r, :], Rall[:, blk])
        yield
        fw.ts("vector", E[:, :], bank[:, 0:128], gcol, 0.0, op0=ALU.subtract, op1=ALU.max)
        fw.ts("vector", ET[:, :], bank[:, 0:128], gcol, 0.0, op0=ALU.subtract, op1=ALU.min)
        yield
        fw.act(eg[:, :], bank[:, 0:128], AF.Exp)
        yield
        fw.tt("vector", t_[3][:, :], bank[:, 128:256], mT, ALU.mult)
        fw.matmul(bank[:, 256:384], qkT[:, 4 + h, blk], qkT[:, 4 + h, blk])
        fw.matmul(bank[:, 384:512], qkT[:, 4 + h, blk], qkT[:, h, blk])
        yield
        fw.act(E[:, :], E[:, :], AF.Exp, scale=-1.0)
        fw.act(ET[:, :], ET[:, :], AF.Exp)
        for hf_ in range(2):
            c_ = 64 * hf_ + (63 if d == 0 else 0)
            fw.copy("gpsimd", ED[:, h, u, hf_:hf_ + 1], eg[:, c_:c_ + 1])
        fw.tt("gpsimd", qtb[h][u][:, :], qkT[:, h, blk], eg[:, :], ALU.mult)
        yield
        fw.tt("vector", t_[0][:, :], bank[:, 256:384], E[:, :], ALU.mult)
        fw.tt("vector", t_[1][:, :], bank[:, 256:384], ET[:, :], ALU.mult)
        fw.tt("vector", E[:, :], bank[:, 384:512], ET[:, :], ALU.mult)
        yield
        fw.stt("vector", Af[0][:, :], t_[0][:, :], bcol, mS, ALU.mult, ALU.mult)
        fw.tt("gpsimd", Bf[0][:, :], t_[1][:, :], t_[3][:, :], ALU.mult)
        fw.tt("gpsimd", QKb[h][u][:, :], E[:, :], mI, ALU.mult)
        ptb = bank[:, :].bitcast(BF16)
        fw.transpose(ptb[:, 0:128], qkT[:, 4 + h, blk], C.ident_bf[:, :])
        fw.transpose(ptb[:, 128:256], qkT[:, 8 + h, blk], C.ident_bf[:, :])
        yield
        fw.tt("gpsimd", Pf[0][:, :], C.ident[:, :], Bf[0][:, :], ALU.subtract)
        fw.ts("vector", ktb[h][u][:, :], ptb[:, 0:128], dlcol, None, op0=ALU.mult)
        fw.ts("vector", bvb[h][u][:, :], ptb[:, 128:256], bcol, None, op0=ALU.mult)
        yield
        cur = 0
        pcur = 0
        for lev in range(5):
            nxt = 1 - cur
            fw.matmul(bank[:, 0:128], Bf[cur][:, :], Af[cur][:, :])
            if lev < 4:
                fw.matmul(bank[:, 128:256], Af[cur][:, :], Bf[cur][:, :])
            yield
            fw.copy("scalar", Af[nxt][:, :], bank[:, 0:128])
            if lev < 4:
                fw.copy("vector", Bf[nxt][:, :], bank[:, 128:256])
            yield
            fw.matmul(bank[:, 256:384], Af[nxt][:, :], Pf[pcur][:, :])
            yield
            if lev < 4:
                fw.tt("vector", Pf[1 - pcur][:, :], bank[:, 256:384], Pf[pcur][:, :], ALU.add)
            else:
                fw.tt("vector", TTb[h][u][:, :], bank[:, 256:384], Pf[pcur][:, :], ALU.add)
            yield
            cur = nxt
            pcur = 1 - pcur

    def recur(d, h, b, u, first_dir):
        r = d * 4 + h
        bank = fw.psb[4 + h]
        for hf in ((0, 1) if d == 0 else (1, 0)):
            lo = 64 * hf
            rows = slice(lo, lo + 64)
            tok = slice(b * 128 + lo, b * 128 + lo + 64)
            nbeg = Col[rows, b, 16 + r:17 + r]
            fw.matmul(bank[rows, 0:128], qkT[:, 4 + h, tok], Sbf[h][:, :])
            yield
            fw.stt("vector", rhsb[h][rows, :], bank[rows, 0:128], nbeg, bvb[h][u][rows, :], ALU.mult, ALU.add)
            yield
            fw.matmul(bank[rows, 128:256], TTb[h][u][rows, lo:lo + 64], rhsb[h][rows, :])
            yield
            fw.copy("scalar", vnb[h][rows, :], bank[rows, 128:256])
            yield
            fw.matmul(bank[:, 256:320], Sbf[h][:, :], qtb[h][u][:, lo:lo + 64], start=True, stop=False)
            fw.matmul(bank[:, 256:320], vnb[h][rows, :], QKb[h][u][rows, lo:lo + 64], start=False, stop=True)
            fw.matmul(bank[:, 384:512], ktb[h][u][rows, :], vnb[h][rows, :])
            yield
            fw.stt("vector", Sst[h][:, :], Sst[h][:, :], ED[:, h, u, hf:hf + 1], bank[:, 384:512], ALU.mult, ALU.add)
            if first_dir:
                fw.copy("scalar", osum[:, h, tok], bank[:, 256:320])
            else:
                fw.tt("vector", osum[:, h, tok], osum[:, h, tok], bank[:, 256:320], ALU.add)
            yield
            fw.copy("gpsimd", Sbf[h][:, :], Sst[h][:, :])
            yield

    def run_rr(gens):
        gens = list(gens)
        while gens:
            for g in list(gens):
                try:
                    next(g)
                except StopIteration:
                    gens.remove(g)

    jobs = []
    for si, (off, L, smp) in enumerate(SEQS):
        b0, b1 = off // 128, (off + L) // 128
        for d in range(2):
            blocks = list(range(b0, b1)) if d == 0 else list(range(b1 - 1, b0 - 1, -1))
            for n_, b in enumerate(blocks):
                jobs.append((si, smp, d, b, n_ == 0, n_ == len(blocks) - 1))
    run_rr([bulk(jobs[0][2], h, jobs[0][3], 0) for h in range(4)])
    for n_, (si, smp, d, b, first, last) in enumerate(jobs):
        u = n_ % NU
        if first:
            for h in range(4):
                if smp:
                    fw.dma("sync", Sst[h][:, :], C.sd[l, d, h])
                else:
                    fw.memset("vector", Sst[h][:, :], 0.0)
                fw.copy("gpsimd", Sbf[h][:, :], Sst[h][:, :])
        gens = [recur(d, h, b, u, d == 0) for h in range(4)]
        if n_ + 1 < len(jobs):
            nj_ = jobs[n_ + 1]
            gens += [bulk(nj_[2], h, nj_[3], (n_ + 1) % NU) for h in range(4)]
        run_rr(gens)
        if last and not smp:
            for h in range(4):
                fw.dma("gpsimd", C.o_sd[si, l, d, h], Sst[h][:, :])
    fw.release(m)
    m = fw.mark()
    qkT = fw.alloc("qkT", [128, 12, NT], BF16)
    osum = fw.alloc("osum", [128, 4, NT])
    Rall = fw.alloc("Rall", [128, NT])
    Col = fw.alloc("Col", [128, NB, 32])
    gout = fw.alloc("gout", [128, 1])
    zt = fw.alloc("zt", [128, NT])
    sq = fw.alloc("sq2", [128, 1, 512], BF16)
    rstd = fw.alloc("rstd2", [128, 512])
    tmp = fw.alloc("tmp2", [128, 512])
    oa = [fw.alloc("oa%d" % i, [128, 512], BF16) for i in range(2)]
    bT = C.brT.v[0].rearrange("(c p) t -> p c t", p=128)
    for h in range(4):
        fw.dma("sync", zt[:, :], pv[:, R_Z // 128 + h, :])
        fw.act(zt[:, :], zt[:, :], AF.Silu)
        for t in range(NT // 512):
            tok = slice(t * 512, (t + 1) * 512)
            ps = fw.ps()
            fw.act(sq[:, 0, :], osum[:, h, tok], AF.Square)
            fw.matmul(ps[:, :], C.ones_bf[:, :], sq[:, 0, :])
            fw.act(tmp[:, :], ps[:, :], AF.Sqrt, scale=1.0 / 128, bias=C.eps_col[:, 0:1])
            fw.recip(rstd[:, :], tmp[:, :])
            fw.stt("vector", tmp[:, :], osum[:, h, tok], gout[:, 0:1], rstd[:, :], ALU.mult, ALU.mult)
            o = oa[t % 2]
            fw.tt("gpsimd", o[:, :], tmp[:, :], zt[:, tok], ALU.mult)
            fw.dma("gpsimd", bT[:, h, tok], o[:, :])
    fw.release(m)


QSCALE = 96 ** -0.5
H_C = 8


def stage_mla(fw, C, l):
    m = fw.mark()
    pv = C.projT.v.rearrange("(j p) t -> p j t", p=128)
    Wqb = fw.alloc("Wqb", [128, 3, 768], BF16)
    Wqs = fw.alloc("Wqs", [128, 3, 256], BF16)
    Wkn = fw.alloc("Wkn", [128, 2, 512], BF16)
    Wv = fw.alloc("Wv", [128, 2, 512], BF16)
    gq = fw.alloc("gq", [128, 3])
    gkv = fw.alloc("gkv", [128, 2])
    rope = fw.alloc("rope", [128, 4, LS])
    m2 = fw.mark()
    C.stg = [fw.alloc("stg%d" % i, [128, 2304]) for i in range(2)]
    C.stg_i = 0
    tb = [fw.alloc("tb%d" % i, [128, 128]) for i in range(2)]
    load_colvec(fw, C, gq[:, :], C.g_q_a[l], 3, tb[0])
    load_colvec(fw, C, gkv[:, :], C.g_kv_a[l], 2, tb[1])
    for i in range(4):
        fw.dma("sync", rope[0:32, i, :], C.c_rope[i])
    wv = C.w_q_b[l].rearrange("(k p) f -> p k f", p=128)
    cast_load(fw, C, Wqb[:, :, :], wv, [128, 3, 768])
    st = C.stg[C.stg_i % 2]
    C.stg_i += 1
    sv = st[:, 0:768].rearrange("p (k f) -> p k f", f=256)
    for h in range(8):
        fw.dma("sync", sv[:, :, h * 32:h * 32 + 16], wv[:, :, h * 96 + 80:h * 96 + 96])
        fw.dma("sync", sv[:, :, h * 32 + 16:h * 32 + 32], wv[:, :, h * 96 + 64:h * 96 + 80])
    fw.copy("vector", Wqs[:, :, :], sv)
    wv = C.w_kv_b[l].rearrange("(k p) (h x) -> p k h x", p=128, x=128)
    st = C.stg[C.stg_i % 2]
    C.stg_i += 1
    sv = st[:, 0:2048].rearrange("p (k h x) -> p k h x", h=8, x=128)
    for k in range(2):
        fw.dma("sync", sv[:, k, :, :], wv[:, k, :, :])
    for k in range(2):
        fw.copy("vector", Wkn[:, k, :].rearrange("p (h x) -> p h x", x=64), sv[:, k, :, 0:64])
        fw.copy("gpsimd", Wv[:, k, :].rearrange("p (h x) -> p h x", x=64), sv[:, k, :, 64:128])
    fw.release(m2)
    NKMAX = LS + 256
    QTh = [fw.alloc("QT%d" % h, [128, LS], BF16) for h in range(8)]
    KT = fw.alloc("KT", [128, 8, NKMAX], BF16)
    Vaug = fw.alloc("Vaug", [128, NKMAX // 128, 8, 65], BF16)
    ckvb = fw.alloc("ckvb", [128, 2, NKMAX], BF16)
    qn = fw.alloc("qn", [128, 3, 512], BF16)
    krot = fw.alloc("krot", [128, NKMAX], BF16)
    qa = fw.alloc("qa", [128, 3, 512])
    sq = fw.alloc("sq", [128, 3, 512], BF16)
    rstd = fw.alloc("rstd", [128, 512])
    tmp = [fw.alloc("tmp%d" % i, [128, 512]) for i in range(3)]
    ckvf = fw.alloc("ckvf", [128, 2, 512])
    krA = fw.alloc("krA", [128, 512])
    krB = fw.alloc("krB", [128, 512])
    otok = [fw.alloc("otok%d" % i, [128, 288]) for i in range(2)]
    ctxt = fw.alloc("ctxt", [128, 288])
    pt = [fw.alloc("pt%d" % i, [128, 512], BF16) for i in range(3)]
    mx = fw.alloc("mx", [128, 2, 32])
    rl = fw.alloc("rl", [128, 2, 4])
    oc = fw.alloc("oc", [128, 4, 512])
    ocT = [fw.alloc("ocT%d" % i, [128, 4, 512], BF16) for i in range(2)]
    fw.memset("vector", KT[32:64, :, :], 0.0)
    fw.memset("vector", KT[32:33, :, :], 1.0)
    fw.memset("vector", Vaug[:, :, :, 64:65], 1.0)
    oti = 0
    for si, (off, L, smp) in enumerate(SEQS):
        koff = 256 if smp else 0
        nk = L + koff
        N = min(512, L)
        for h in range(8):
            fw.memset(fw.any2(), QTh[h][32:64, 0:L], 0.0)
        for t0 in range(0, L, N):
            tok = slice(off + t0, off + t0 + N)
            fw.dma("sync", qa[:, :, 0:N], pv[:, R_QA // 128:R_QA // 128 + 3, tok])
            rms_rstd(fw, C, [qa[:, c, 0:N] for c in range(3)], 384, N, sq, rstd, tmp[0])
            for c in range(3):
                fw.stt("vector", qn[:, c, 0:N], qa[:, c, 0:N], gq[:, c:c + 1], rstd[:, 0:N], ALU.mult, ALU.mult)
            for h in range(8):
                psA = fw.ps()
                for k in range(3):
                    fw.matmul(psA[64:128, 0:N], Wqb[:, k, h * 96:h * 96 + 64], qn[:, k, 0:N], start=(k == 0), stop=(k == 2))
                for k in range(3):
                    fw.matmul(psA[0:32, 0:N], Wqb[:, k, h * 96 + 64:h * 96 + 96], qn[:, k, 0:N], start=(k == 0), stop=(k == 2))
                fw.act(QTh[h][64:128, t0:t0 + N], psA[64:128, 0:N], AF.Copy, scale=QSCALE)
                if smp:
                    psB = fw.ps()
                    for k in range(3):
                        fw.matmul(psB[0:32, 0:N], Wqs[:, k, h * 32:h * 32 + 32], qn[:, k, 0:N], start=(k == 0), stop=(k == 2))
                    fw.tt("vector", tmp[1][0:32, 0:N], psA[0:32, 0:N], rope[0:32, 2, t0:t0 + N], ALU.mult)
                    fw.tt("vector", tmp[2][0:32, 0:N], psB[0:32, 0:N], rope[0:32, 3, t0:t0 + N], ALU.mult)
                    fw.tt("gpsimd", QTh[h][0:32, t0:t0 + N], tmp[1][0:32, 0:N], tmp[2][0:32, 0:N], ALU.add)
                else:
                    fw.act(QTh[h][0:32, t0:t0 + N], psA[0:32, 0:N], AF.Copy, scale=QSCALE)
            fw.dma("sync", qa[:, 0:2, 0:N], pv[:, R_KVA // 128:R_KVA // 128 + 2, tok])
            rms_rstd(fw, C, [qa[:, c, 0:N] for c in range(2)], 256, N, sq, rstd, tmp[0])
            for c in range(2):
                fw.stt("vector", ckvf[:, c, 0:N], qa[:, c, 0:N], gkv[:, c:c + 1], rstd[:, 0:N], ALU.mult, ALU.mult)
                fw.copy("gpsimd", ckvb[:, c, koff + t0:koff + t0 + N], ckvf[:, c, 0:N])
            fw.dma("sync", krA[0:32, 0:N], C.projT[R_MA + 64:R_MA + 96, tok])
            if smp:
                fw.dma("sync", krB[0:32, 0:N], C.projT[R_MB + 64:R_MB + 96, tok])
                fw.tt("vector", tmp[1][0:32, 0:N], krA[0:32, 0:N], rope[0:32, 0, t0:t0 + N], ALU.mult)
                fw.tt("vector", tmp[2][0:32, 0:N], krB[0:32, 0:N], rope[0:32, 1, t0:t0 + N], ALU.mult)
                fw.tt("gpsimd", krot[0:32, koff + t0:koff + t0 + N], tmp[1][0:32, 0:N], tmp[2][0:32, 0:N], ALU.add)
            else:
                fw.copy("gpsimd", krot[0:32, t0:t0 + N], krA[0:32, 0:N])
                for tb_ in range(N // 128):
                    ot = otok[oti % 2]
                    oti += 1
                    ps = fw.ps()
                    for c in range(2):
                        fw.transpose(ps[:, c * 128:(c + 1) * 128], ckvf[:, c, tb_ * 128:(tb_ + 1) * 128], C.ident[:, :])
                    fw.transpose(ps[:, 256:288], krA[0:32, tb_ * 128:(tb_ + 1) * 128], C.ident[0:32, 0:32])
                    fw.copy(fw.evac(), ot[:, :], ps[:, 0:288])
                    fw.dma("gpsimd", C.o_ckv[si, l, t0 + tb_ * 128:t0 + (tb_ + 1) * 128, :], ot[:, 0:256])
                    fw.dma("gpsimd", C.o_kr[si, l, t0 + tb_ * 128:t0 + (tb_ + 1) * 128, :], ot[:, 256:288])
        if smp:
            for tb_ in range(2):
                fw.dma("sync", ctxt[:, 0:256], C.cckv[l, tb_ * 128:(tb_ + 1) * 128, :])
                fw.dma("sync", ctxt[:, 256:288], C.ckr[l, tb_ * 128:(tb_ + 1) * 128, :])
                ps = fw.ps()
                for c in range(2):
                    fw.transpose(ps[:, c * 128:(c + 1) * 128], ctxt[:, c * 128:(c + 1) * 128], C.ident[:, :])
                    fw.copy(fw.evac(), ckvb[:, c, tb_ * 128:(tb_ + 1) * 128], ps[:, c * 128:(c + 1) * 128])
                fw.transpose(ps[0:32, 256:384], ctxt[:, 256:288], C.ident[:, :])
                fw.copy("vector", krot[0:32, tb_ * 128:(tb_ + 1) * 128], ps[0:32, 256:384])
        for h in range(8):
            fw.copy(fw.any2(), KT[0:32, h, 0:nk], krot[0:32, 0:nk])
            for k0 in range(0, nk, 512):
                n = min(512, nk - k0)
                ps = fw.ps()
                for k in range(2):
                    fw.matmul(ps[64:128, 0:n], Wkn[:, k, h * 64:(h + 1) * 64], ckvb[:, k, k0:k0 + n], start=(k == 0), stop=(k == 1))
                fw.copy(fw.evac(), KT[64:128, h, k0:k0 + n], ps[64:128, 0:n])
        for kc in range(nk // 128):
            ps = fw.ps()
            for k in range(2):
                fw.matmul(ps[:, :], ckvb[:, k, kc * 128:(kc + 1) * 128], Wv[:, k, :], start=(k == 0), stop=(k == 1))
            fw.copy(fw.evac(), Vaug[:, kc, :, 0:64], ps[:, :].rearrange("p (h x) -> p h x", x=64))
        nj = N // 128
        nkb = (nk + 511) // 512
        nkc = nk // 128
        for t0 in range(0, L, N):
            fw.ps_set = [4, 5, 6, 7]

            def gen_max(h):
                for j in range(nj):
                    q0 = t0 + j * 128
                    for kb in range(nkb):
                        n = min(512, nk - kb * 512)
                        ps = fw.ps()
                        fw.matmul(ps[:, 0:n], QTh[h][:, q0:q0 + 128], KT[:, h, kb * 512:kb * 512 + n])
                        fw.rmax(mx[:, h % 2, j * 8 + kb:j * 8 + kb + 1], ps[:, 0:n])
                        yield
                    if nkb > 1:
                        fw.rmax(mx[:, h % 2, j * 8 + 7:j * 8 + 8], mx[:, h % 2, j * 8:j * 8 + nkb])
                        mcol = mx[:, h % 2, j * 8 + 7:j * 8 + 8]
                    else:
                        mcol = mx[:, h % 2, j * 8:j * 8 + 1]
                    ps = fw.ps()
                    fw.matmul(ps[32:33, 0:128], mcol, C.ident[:, :])
                    fw.act(QTh[h][32:33, q0:q0 + 128], ps[32:33, 0:128], AF.Copy, scale=-1.0)
                    yield

            def gen_main(h):
                acc = [fw.psb[j] for j in range(nj)]
                for kc in range(nkc):
                    ST = fw.ps()
                    fw.matmul(ST[:, 0:N], KT[:, h, kc * 128:(kc + 1) * 128], QTh[h][:, t0:t0 + N])
                    p = pt[kc % 3]
                    fw.act(p[:, 0:N], ST[:, 0:N], AF.Exp)
                    for j in range(nj):
                        fw.matmul(acc[j][:, 0:65], p[:, j * 128:(j + 1) * 128], Vaug[:, kc, h, :], start=(kc == 0), stop=(kc == nkc - 1))
                    yield
                for j in range(nj):
                    fw.recip(rl[:, h % 2, j:j + 1], acc[j][:, 64:65])
                    fw.ts("vector", oc[:, j, h * 64:(h + 1) * 64], acc[j][:, 0:64], rl[:, h % 2, j:j + 1], None, op0=ALU.mult)
                yield

            def rr(gens):
                gens = list(gens)
                while gens:
                    for g in list(gens):
                        try:
                            next(g)
                        except StopIteration:
                            gens.remove(g)

            rr([gen_max(0)])
            for h in range(8):
                rr([gen_main(h)] + ([gen_max(h + 1)] if h < 7 else []))
            fw.ps_set = list(range(8))
            o = ocT[(t0 // N) % 2]
            for j in range(nj):
                ps = fw.ps()
                for c in range(4):
                    fw.transpose(ps[:, c * 128:(c + 1) * 128], oc[:, j, c * 128:(c + 1) * 128], C.ident[:, :])
                fw.copy(fw.evac(), o[:, :, j * 128:(j + 1) * 128], ps[:, :].rearrange("p (c t) -> p c t", t=128))
            fw.dma("gpsimd", C.brT.v[2].rearrange("(c p) t -> p c t", p=128)[:, :, off + t0:off + t0 + N], o[:, :, 0:N])
    fw.release(m)


def stage_merge(fw, C, l):
    m = fw.mark()
    Wbr = fw.alloc("Wbr", [128, 12, 1024], BF16)
    Wout = fw.alloc("Wout", [128, 8, 1024], BF16)
    m2 = fw.mark()
    C.stg = [fw.alloc("stg%d" % i, [128, 2048]) for i in range(2)]
    C.stg_i = 0
    for br in range(3):
        wv = C.w_branch[l, br].rearrange("(k p) f -> p k f", p=128)
        for o in range(0, 1024, 512):
            cast_load(fw, C, Wbr[:, br * 4:(br + 1) * 4, o:o + 512], wv[:, :, o:o + 512], [128, 4, 512])
    wv = C.w_out[l].rearrange("(k p) f -> p k f", p=128)
    for o in range(0, 1024, 256):
        cast_load(fw, C, Wout[:, :, o:o + 256], wv[:, :, o:o + 256], [128, 8, 256])
    fw.release(m2)
    xt = [fw.alloc("xt%d" % i, [128, 8, 512]) for i in range(2)]
    xo = [fw.alloc("xo%d" % i, [128, 8, 512]) for i in range(1)]
    brt = [fw.alloc("brt%d" % i, [128, 12, 512], BF16) for i in range(2)]
    gl = [fw.alloc("gl%d" % i, [128, 8, 512]) for i in range(2)]
    merged = fw.alloc("merged", [128, 8, 512])
    mergedb = fw.alloc("mergedb", [128, 8, 512], BF16)
    sg = [fw.alloc("sg%d" % i, [128, 512]) for i in range(2)]
    tp = [fw.alloc("tp%d" % i, [128, 512]) for i in range(2)]
    xTv = C.xT.v.rearrange("(c p) t -> p c t", p=128)
    pv = C.projT.v.rearrange("(j p) t -> p j t", p=128)
    bv = C.brT.v.rearrange("b (k p) t -> b p k t", p=128)
    gi = 0
    for t in range(NT // 512):
        n = 0 if t < 2 else 1
        tok = slice(t * 512, (t + 1) * 512)
        x = xt[t % 2]
        o = xo[0]
        bt = brt[t % 2]
        fw.dma("sync", x[:, :, :], xTv[:, :, tok])
        for br in range(3):
            fw.dma("sync", bt[:, br * 4:(br + 1) * 4, :], bv[br, :, :, tok])
        for br in range(3):
            g = gl[gi % 2]
            gi += 1
            fw.dma("sync", g[:, :, :], pv[:, R_GATE // 128 + br * 8:R_GATE // 128 + br * 8 + 8, tok])
            for dc in range(8):
                ps = fw.ps()
                for k in range(4):
                    fw.matmul(ps[:, :], Wbr[:, br * 4 + k, dc * 128:(dc + 1) * 128], bt[:, br * 4 + k, :], start=(k == 0), stop=(k == 3))
                s = sg[dc % 2]
                fw.act(s[:, :], g[:, dc, :], AF.Sigmoid)
                if br == 0:
                    fw.tt("vector", merged[:, dc, :], ps[:, :], s[:, :], ALU.mult)
                else:
                    tq = tp[dc % 2]
                    fw.tt("vector", tq[:, :], ps[:, :], s[:, :], ALU.mult)
                    if br == 1:
                        fw.tt("gpsimd", merged[:, dc, :], merged[:, dc, :], tq[:, :], ALU.add)
                    else:
                        fw.tt("gpsimd", mergedb[:, dc, :], merged[:, dc, :], tq[:, :], ALU.add)
        for ec in range(8):
            ps = fw.ps()
            for d in range(8):
                fw.matmul(ps[:, :], Wout[:, d, ec * 128:(ec + 1) * 128], mergedb[:, d, :], start=(d == 0), stop=(d == 7))
            fw.stt("vector", o[:, ec, :], ps[:, :], C.mG_m[:, ec, n:n + 1], x[:, ec, :], ALU.mult, ALU.add)
        fw.dma("gpsimd", xTv[:, :, tok], o[:, :, :])
    fw.release(m)


def stage_ffn(fw, C, l):
    m = fw.mark()
    Wup = fw.alloc("Wup", [128, 8, 2 * D_FF], BF16)
    Wdn = fw.alloc("Wdn", [128, 22, 1024], BF16)
    cw = fw.alloc("cw", [128, 3, 44])
    cb = fw.alloc("cb", [128, 44])
    m2 = fw.mark()
    C.stg = [fw.alloc("stg%d" % i, [128, 2048]) for i in range(2)]
    C.stg_i = 0
    wv = C.w_ffn_up[l].rearrange("(k p) f -> p k f", p=128)
    for o in range(0, 2 * D_FF, 256):
        cast_load(fw, C, Wup[:, :, o:o + 256], wv[:, :, o:o + 256], [128, 8, 256])
    wv = C.w_ffn_down[l].rearrange("(j p) e -> p j e", p=128)
    for j in range(0, 22, 2):
        cast_load(fw, C, Wdn[:, j:j + 2, :], wv[:, j:j + 2, :], [128, 2, 1024])
    tb = [fw.alloc("tb%d" % i, [128, 128]) for i in range(4)]
    for i in range(3):
        load_colvec(fw, C, cw[:, i, :], C.conv_ffn[l, i], 44, tb[i])
    load_colvec(fw, C, cb[:, :], C.b_conv_ffn[l], 44, tb[3])
    fw.release(m2)
    W = 258
    xh = [fw.alloc("xh%d" % i, [128, 8, W]) for i in range(2)]
    sq = fw.alloc("sq", [128, 8, W], BF16)
    h = [fw.alloc("h%d" % i, [128, 8, W], BF16) for i in range(2)]
    rstd = fw.alloc("rstd", [128, W])
    tmp = [fw.alloc("tmp%d" % i, [128, W]) for i in range(2)]
    actt = fw.alloc("actt", [128, 22, 256], BF16)
    ga = [fw.alloc("ga%d" % i, [128, 256]) for i in range(2)]
    gb = [fw.alloc("gb%d" % i, [128, 256]) for i in range(2)]
    va = [fw.alloc("va%d" % i, [128, 256]) for i in range(2)]
    vb = [fw.alloc("vb%d" % i, [128, 256]) for i in range(2)]
    xo = [fw.alloc("xo%d" % i, [128, 8, 256]) for i in range(2)]
    xTv = C.xT.v.rearrange("(c p) t -> p c t", p=128)
    ti = 0
    for (off, L, n) in SEQS:
        for t0 in range(0, L, 256):
            x = xh[ti % 2]
            hh = h[ti % 2]
            o = xo[ti % 2]
            ti += 1
            lo = 1 if t0 == 0 else 0
            hi = W - 1 if t0 + 256 >= L else W
            if lo == 1:
                fw.memset("gpsimd", x[:, :, 0:1], 0.0)
            else:
                fw.copy("gpsimd", x[:, :, 0:1], xprev[:, :, W - 2:W - 1])
            if hi == W - 1:
                fw.memset("gpsimd", x[:, :, W - 1:W], 0.0)
            fw.dma("sync", x[:, :, 1:hi], xTv[:, :, off + t0:off + t0 - 1 + hi])
            xprev = x
            rms_rstd(fw, C, [x[:, c, :] for c in range(8)], D, W, sq, rstd, tmp[0])
            for c in range(8):
                tm = tmp[c % 2]
                fw.tt(("vector", "gpsimd")[c % 2], tm[:, :], x[:, c, :], rstd[:, :], ALU.mult)
                fw.act(hh[:, c, :], tm[:, :], AF.Identity, scale=C.mA_f[:, c, n:n + 1], bias=C.mB_f[:, c, n:n + 1])
            if lo == 1:
                fw.memset("gpsimd", hh[:, :, 0:1], 0.0)
            if hi == W - 1:
                fw.memset("gpsimd", hh[:, :, W - 1:W], 0.0)
            for j in range(22):
                res = []
                for (which, ta, tb) in ((0, ga[j % 2], gb[j % 2]), (1, va[j % 2], vb[j % 2])):
                    ch = which * 22 + j
                    ps = fw.ps()
                    for k in range(8):
                        fw.matmul(ps[:, 0:W], Wup[:, k, ch * 128:(ch + 1) * 128], hh[:, k, :], start=(k == 0), stop=(k == 7))
                    fw.act(ta[:, :], ps[:, 1:257], AF.Identity, scale=cw[:, 1, ch:ch + 1], bias=cb[:, ch:ch + 1])
                    fw.stt("vector", tb[:, :], ps[:, 0:256], cw[:, 0, ch:ch + 1], ta[:, :], ALU.mult, ALU.add)
                    fw.stt("vector", ta[:, :], ps[:, 2:258], cw[:, 2, ch:ch + 1], tb[:, :], ALU.mult, ALU.add)
                    res.append(ta)
                fw.act(gb[j % 2][:, :], res[0][:, :], AF.Silu)
                fw.tt("gpsimd", actt[:, j, :], gb[j % 2][:, :], res[1][:, :], ALU.mult)
            for ec in range(8):
                ps = fw.ps()
                for j in range(22):
                    fw.matmul(ps[:, 0:256], Wdn[:, j, ec * 128:(ec + 1) * 128], actt[:, j, :], start=(j == 0), stop=(j == 21))
                fw.stt("vector", o[:, ec, :], ps[:, 0:256], C.mG_f[:, ec, n:n + 1], x[:, ec, 1:257], ALU.mult, ALU.add)
            fw.dma("gpsimd", xTv[:, :, off + t0:off + t0 + 256], o[:, :, :])
    fw.release(m)


def stage_final(fw, C):
    m = fw.mark()
    gB = fw.alloc("gB", [128, 1024])
    fw.dma("sync", gB[:, :], C.g_final.unsq(0).pbc(128).rearrange("p a f -> p (a f)"))
    xt = [fw.alloc("xt%d" % i, [128, 8, 128]) for i in range(2)]
    yt = [fw.alloc("yt%d" % i, [128, 1024]) for i in range(2)]
    junk = fw.alloc("junk", [128, 512])
    ss = [fw.alloc("ss%d" % i, [128, 4]) for i in range(2)]
    xTv = C.xT.v.rearrange("(c p) t -> p c t", p=128)
    for tt in range(NT // 128):
        x = xt[tt % 2]
        y = yt[tt % 2]
        s = ss[tt % 2]
        fw.dma("sync", x[:, :, :], xTv[:, :, tt * 128:(tt + 1) * 128])
        pss = []
        for half in range(2):
            ps = fw.ps()
            pss.append(ps)
            for c in range(4):
                fw.transpose(ps[:, c * 128:(c + 1) * 128], x[:, half * 4 + c, :], C.ident[:, :])
            fw.act(junk[:, :], ps[:, :], AF.Square, accum_out=s[:, half:half + 1])
        fw.tt("vector", s[:, 2:3], s[:, 0:1], s[:, 1:2], ALU.add)
        fw.act(s[:, 3:4], s[:, 2:3], AF.Sqrt, scale=1.0 / D, bias=C.eps_col[:, 0:1])
        fw.recip(s[:, 2:3], s[:, 3:4])
        for half in range(2):
            fw.stt("vector", y[:, half * 512:(half + 1) * 512], pss[half][:, :], s[:, 2:3], gB[:, half * 512:(half + 1) * 512], ALU.mult, ALU.mult)
        fw.dma("gpsimd", C.y_tok[tt * 128:(tt + 1) * 128, :], y[:, :])
    fw.release(m)


WEIGHT_NAMES = ["w_mod", "b_mod", "g_norm_mix", "g_norm_ffn", "w_in", "conv_qkv", "a_log", "dt_bias", "g_delta_out",
                "s5_lam_re", "s5_lam_im", "s5_log_dt", "s5_b_re", "s5_b_im", "s5_c_re", "s5_c_im", "s5_d", "w_glu", "b_glu",
                "g_q_a", "w_q_b", "g_kv_a", "w_kv_b", "w_branch", "w_out", "w_ffn_up", "conv_ffn", "b_conv_ffn", "w_ffn_down",
                "g_final"]


def build(shapes, depth=DEPTH, dbg=False, skip_mixers=False, stages=("mla", "s5", "gdn", "dense")):
    nc = bass.Bass("TRN2", target_bir_lowering=False)
    fw = FW(nc)
    C = Ctx()
    for name, shp in shapes.items():
        setattr(C, name, fw.dram(name, shp, F32, kind="ExternalInput").v)
    kind = "ExternalOutput" if dbg else "Internal"
    C.y_tok = fw.dram("y_tok", [NT, D], F32, kind="ExternalOutput").v
    C.o_sd = fw.dram("o_sd", [NPS, DEPTH, 2, 4, 128, 128], F32, kind="ExternalOutput").v
    C.o_s5re = fw.dram("o_s5re", [NPS, DEPTH, 2, 32, 64], F32, kind="ExternalOutput").v
    C.o_s5im = fw.dram("o_s5im", [NPS, DEPTH, 2, 32, 64], F32, kind="ExternalOutput").v
    C.o_ckv = fw.dram("o_ckv", [NPS, DEPTH, LP, 256], F32, kind="ExternalOutput").v
    C.o_kr = fw.dram("o_kr", [NPS, DEPTH, LP, 32], F32, kind="ExternalOutput").v
    C.xT = fw.dram("xT", [D, NT], F32, kind=kind)
    C.projT = fw.dram("projT", [PW, NT], F32, kind=kind)
    C.brT = fw.dram("brT", [3, 512, NT], BF16, kind=kind)
    if dbg:
        C.xmid = fw.dram("xmid", [D, NT], F32, kind=kind)
    C.ident = fw.alloc("ident", [128, 128])
    C.ones_bf = fw.alloc("ones_bf", [128, 128], BF16)
    C.eps_col = fw.alloc("eps_col", [128, 1])
    C.cT = fw.alloc("cT", [128, 8, 2])
    for nm in ("mA_m", "mB_m", "mG_m", "mA_f", "mB_f", "mG_f"):
        setattr(C, nm, fw.alloc(nm, [128, 8, 2]))
    fw.dma("sync", C.ident[:, :], C.c_ident)
    C.halfpi = fw.alloc("halfpi", [128, 1])
    fw.memset("vector", C.halfpi[:, :], math.pi / 2)
    C.svec = fw.alloc("svec", [128, 17])
    C.maskR = fw.alloc("maskR", [128, 2])
    C.maskS = fw.alloc("maskS", [128, 2])
    C.maskQ = fw.alloc("maskQ", [128, 4])
    fw.dma("sync", C.svec[:, :], C.c_svec)
    fw.dma("sync", C.maskR[:, :], C.c_maskR)
    fw.dma("sync", C.maskS[:, :], C.c_maskS)
    fw.dma("sync", C.maskQ[:, :], C.c_maskQ)
    C.ident_bf = fw.alloc("ident_bf", [128, 128], BF16)
    fw.copy("vector", C.ident_bf[:, :], C.ident[:, :])
    C.rm3 = fw.alloc("rm3", [128, 1])
    fw.dma("sync", C.rm3[:, :], C.c_rm3)
    fw.memset("vector", C.ones_bf[:, :], 1.0)
    fw.memset("vector", C.eps_col[:, :], EPS)
    tb0 = fw.alloc("tb0", [128, 128])
    tb1 = fw.alloc("tb1", [128, 8])
    for n in range(2):
        load_colvec(fw, C, tb1[:, :], C.cc[n], 8, tb0)
        fw.copy("vector", C.cT[:, :, n], tb1[:, :])
    fw.act(C.cT[:, :, :], C.cT[:, :, :], AF.Silu)
    if skip_mixers:
        m = fw.mark()
        z = fw.alloc("z", [128, 4, 512], BF16)
        zf = fw.alloc("zf", [128, 4, 512])
        bv = C.brT.v.rearrange("b (k p) t -> b p k t", p=128)
        dv = C.dbg_br.rearrange("b (k p) t -> b p k t", p=128)
        for br in range(3):
            for t in range(NT // 512):
                fw.dma("sync", zf[:, :, :], dv[br, :, :, t * 512:(t + 1) * 512])
                fw.copy("vector", z[:, :, :], zf[:, :, :])
                fw.dma("gpsimd", bv[br, :, :, t * 512:(t + 1) * 512], z[:, :, :])
        fw.release(m)
    stage_in(fw, C)
    for l in range(depth):
        stage_mods(fw, C, l)
        stage_inproj(fw, C, l)
        if "mla" in stages:
            stage_mla(fw, C, l)
        if "s5" in stages:
            stage_s5(fw, C, l)
        if "gdn" in stages:
            stage_gdn(fw, C, l)
        if "dense" in stages:
            stage_merge(fw, C, l)
            if dbg and l == 0:
                fw.dma("sync", C.xmid.v, C.xT.v)
            stage_ffn(fw, C, l)
    stage_final(fw, C)
    fw.emit()
    return nc, fw


def host_consts():
    t = np.arange(LS)
    row = (t // 64).astype(np.float32)
    col = (t % 64).astype(np.float32)
    inv = (1.0 / (10000.0 ** (np.arange(8, dtype=np.float32) / 8))).astype(np.float32)
    ang = np.concatenate([row[:, None] * inv, col[:, None] * inv], axis=-1).astype(np.float32)
    cos, sin = np.cos(ang).astype(np.float32).T, np.sin(ang).astype(np.float32).T
    cs1 = np.concatenate([cos, cos], 0)
    cs2 = np.concatenate([-sin, sin], 0)
    rope = np.stack([cs1, cs2, cs1 * np.float32(QSCALE), cs2 * np.float32(QSCALE)]).astype(np.float32)
    p = np.arange(128)
    svec = np.tile(np.arange(17, dtype=np.float32)[None, :], (128, 1))
    maskR = (((p // 16) % 2)[:, None] == np.arange(2)[None, :]).astype(np.float32)
    maskS = ((p // 64)[:, None] == np.arange(2)[None, :]).astype(np.float32)
    maskQ = ((p // 32)[:, None] == np.arange(4)[None, :]).astype(np.float32)
    rm3 = (p >= 96).astype(np.float32)[:, None]
    cc_, ee_ = np.meshgrid(p, p, indexing="ij")
    same = (cc_ // 64) == (ee_ // 64)
    gmask = np.stack([same & (cc_ > ee_), same & (cc_ < ee_), same & (cc_ >= ee_), same & (cc_ <= ee_)], axis=1).astype(np.float32)
    gsel = np.zeros((128, 16, 128), np.float32)
    for r_ in range(8):
        gsel[r_, r_, :] = 1.0
        gsel[32 + r_, 8 + r_, :] = 1.0
    rst = np.ones((8, NT), np.float32)
    rst[:, ::64] = 0.0
    mdir = (np.arange(8) >= 4).astype(np.float32)[:, None]
    return {"c_ident": np.eye(128, dtype=np.float32), "c_rope": rope, "c_svec": svec, "c_maskR": maskR,
            "c_maskS": maskS, "c_maskQ": maskQ, "c_rm3": rm3, "c_gmask": gmask, "c_gsel": gsel, "c_rst": rst,
            "c_mdir": mdir}


def make_in_maps(inputs):
    f = lambda a: np.ascontiguousarray(np.asarray(a, dtype=np.float32))
    W = {k: f(inputs[k]) for k in WEIGHT_NAMES}
    consts = host_consts()
    xp = f(inputs["x_prompt"])
    xs = f(inputs["x_sample"])
    maps = []
    for c in range(8):
        d = dict(W)
        d.update(consts)
        d["x_tok"] = np.concatenate([xp[4 * c:4 * c + 4].reshape(NPS * LP, D), xs[c]], axis=0)
        d["cc"] = np.stack([f(inputs["c_ctx"]), f(inputs["c"])[c]], axis=0)
        d["sd"] = f(inputs["state_delta"])[c]
        d["s5re"] = f(inputs["state_s5_re"])[c]
        d["s5im"] = f(inputs["state_s5_im"])[c]
        d["cckv"] = f(inputs["cache_ckv"])[c]
        d["ckr"] = f(inputs["cache_krope"])[c]
        maps.append(d)
    return maps


def kernel(**inputs):
    maps = make_in_maps(inputs)
    shapes = {k: list(v.shape) for k, v in maps[0].items()}
    nc, fw = build(shapes)
    res = run_bass_kernel_spmd(nc, maps, core_ids=list(range(8)))
    R = res.results
    y_prompt = np.stack([R[c]["y_tok"][:NPS * LP].reshape(NPS, LP, D) for c in range(8)]).reshape(32, LP, D)
    y_sample = np.stack([R[c]["y_tok"][NPS * LP:] for c in range(8)])
    cat = lambda k: np.concatenate([np.asarray(R[c][k]) for c in range(8)], axis=0).astype(np.float32)
    return (y_prompt.astype(np.float32), y_sample.astype(np.float32), cat("o_sd"), cat("o_s5re"), cat("o_s5im"), cat("o_ckv"), cat("o_kr"))
```

```python
import math
import numpy as np
import concourse.bass as bass
import concourse.mybir as mybir
from concourse.bass_utils import run_bass_kernel_spmd

F32 = mybir.dt.float32
BF16 = mybir.dt.bfloat16
I32 = mybir.dt.int32
AF = mybir.ActivationFunctionType
ALU = mybir.AluOpType
AX = mybir.AxisListType

DEPTH = 4
NPS, LP, LS = 4, 256, 2048
NT = NPS * LP + LS
SEQS = [(i * LP, LP, 0) for i in range(NPS)] + [(NPS * LP, LS, 1)]
D = 1024
EPS = 1e-6
D_FF = 2816
NFO = 51
PW = NFO * 128
R_QKV, R_Z, R_U, R_QA, R_KVA, R_GATE, R_MA, R_MB = 0, 1536, 2048, 2560, 2944, 3200, 6272, 6400

DMA_QUEUES = ("sync", "gpsimd", "scalar")
COMPUTE = ("tensor", "vector", "scalar", "gpsimd")
N_DMA_SEMS = 12
ARENA_WORDS = 52600


def dsize(dt):
    return 2 if dt == BF16 else 4


class Buf:
    __slots__ = ("name", "w", "r", "t", "psum", "g")

    def __init__(self, name, t=None, psum=False):
        self.name = name
        self.w = {}
        self.r = {}
        self.g = []
        self.t = t
        self.psum = psum

    def __getitem__(self, idx):
        return V(self, self.t[idx])

    @property
    def v(self):
        return V(self, self.t)


class V:
    __slots__ = ("buf", "ap")

    def __init__(self, buf, ap):
        self.buf = buf
        self.ap = ap

    def __getitem__(self, idx):
        return V(self.buf, self.ap[idx])

    def rearrange(self, pat, **kw):
        return V(self.buf, self.ap.rearrange(pat, **kw))

    def bitcast(self, dt):
        return V(self.buf, self.ap.bitcast(dt))

    def bc(self, shape):
        return V(self.buf, self.ap.broadcast_to(list(shape)))

    def unsq(self, ax):
        return V(self.buf, self.ap.unsqueeze(ax))

    def pbc(self, n):
        return V(self.buf, self.ap.partition_broadcast(n))

    @property
    def shape(self):
        return self.ap.shape


class Op:
    __slots__ = ("eng", "fn", "deps", "dma", "idx", "sig", "sem", "val", "prev_val", "slot")

    def __init__(self, eng, fn, dma):
        self.eng = eng
        self.fn = fn
        self.dma = dma
        self.deps = set()
        self.sig = False
        self.sem = None
        self.val = 0
        self.prev_val = 0


def _ap(x):
    return x.ap if isinstance(x, V) else x


class FW:
    def __init__(self, nc):
        self.nc = nc
        self.ops = []
        self.streams = {e: [] for e in ("tensor", "vector", "scalar", "gpsimd", "sync")}
        self.arena = nc.alloc_sbuf_tensor("arena", [128, ARENA_WORDS], F32)
        self.top = 0
        self.psb = [Buf("psb%d" % i, nc.alloc_psum_tensor("psb%d" % i, [128, 512], F32), psum=True) for i in range(8)]
        self.psi = 0
        self.ps_set = list(range(8))
        self.dma_since = []
        self.pending = {e: [] for e in self.streams}
        self.rr = 0
        self.drr = {q: 0 for q in DMA_QUEUES}

    def alloc(self, name, shape, dtype=F32):
        P = shape[0]
        free = 1
        for s in shape[1:]:
            free *= s
        words = (free * dsize(dtype) + 3) // 4
        words = (words + 7) // 8 * 8
        off = self.top
        self.top += words
        assert self.top <= ARENA_WORDS, "SBUF arena overflow %s %d" % (name, self.top)
        ap = self.arena[0:P, off:off + words]
        if dtype != F32:
            ap = ap.bitcast(dtype)
        ap = ap[:, 0:free]
        if len(shape) >= 3:
            names = ["d%d" % i for i in range(len(shape) - 1)]
            pat = "p (%s) -> p %s" % (" ".join(names), " ".join(names))
            kw = {names[i]: shape[i + 1] for i in range(1, len(names))}
            ap = ap.rearrange(pat, **kw)
        return Buf(name, ap)

    def mark(self):
        return self.top

    def release(self, m):
        self.top = m
        self.barrier()

    def ps(self):
        st = self.ps_set
        b = self.psb[st[self.psi % len(st)]]
        self.psi += 1
        return b

    def dram(self, name, shape, dtype, kind="Internal"):
        t = self.nc.dram_tensor(name, list(shape), dtype, kind=kind)
        return Buf(name, t.ap())

    def barrier(self):
        bar = []
        for e, st in self.streams.items():
            for o in reversed(st):
                if not o.dma:
                    bar.append(o.idx)
                    break
        bar.extend(self.dma_since)
        self.dma_since = []
        for e in self.pending:
            self.pending[e] = list(bar)

    def op(self, eng, fn, reads=(), writes=(), dma=False):
        o = Op(eng, fn, dma)
        o.idx = len(self.ops)
        self.ops.append(o)
        self.streams[eng].append(o)
        if dma:
            o.slot = self.drr[eng] % N_DMA_SEMS
            self.drr[eng] += 1
            key = (eng, o.slot)
        else:
            key = eng
        if self.pending[eng]:
            o.deps.update(self.pending[eng])
            self.pending[eng] = []
        for b in reads:
            o.deps.update(b.w.values())
        for b in writes:
            if b.r:
                b.g = list(b.r.values()) + list(b.w.values())
                b.w = {}
                b.r = {}
            o.deps.update(b.g)
            if b.psum:
                o.deps.update(b.w.values())
            elif key in b.w:
                o.deps.add(b.w[key])
        o.deps.discard(o.idx)
        for b in writes:
            b.w[key] = o.idx
        for b in reads:
            if b not in writes:
                b.r[key] = o.idx
        if dma:
            self.dma_since.append(o.idx)
        return o.idx

    def _rw(self, outs, ins):
        w = [x.buf for x in outs if isinstance(x, V)]
        r = [x.buf for x in ins if isinstance(x, V)]
        w = w + [b for b in r if b.psum]
        return r, w

    def dma(self, q, out, in_, slow=False):
        r, w = self._rw([out], [in_])
        o, i = _ap(out), _ap(in_)
        if slow:
            return self.op(q, lambda e: e.dma_start(out=o, in_=i, allow_slow_non_contiguous=True), r, w, dma=True)
        return self.op(q, lambda e: e.dma_start(out=o, in_=i), r, w, dma=True)

    def matmul(self, out, lhsT, rhs, start=True, stop=True, **kw):
        r, w = self._rw([out], [lhsT, rhs])
        if not start:
            r = r + [out.buf]
        o, a, b = _ap(out), _ap(lhsT), _ap(rhs)
        return self.op("tensor", lambda e: e.matmul(o, lhsT=a, rhs=b, start=start, stop=stop, **kw), r, w)

    def transpose(self, out, in_, ident):
        r, w = self._rw([out], [in_, ident])
        o, a, b = _ap(out), _ap(in_), _ap(ident)
        return self.op("tensor", lambda e: e.transpose(o, a, b), r, w)

    def act(self, out, in_, func, bias=None, scale=None, accum_out=None, eng="scalar"):
        r, w = self._rw([out, accum_out], [in_, bias, scale])
        kw = {}
        if bias is not None:
            kw["bias"] = _ap(bias)
        if scale is not None:
            kw["scale"] = _ap(scale)
        if accum_out is not None:
            kw["accum_out"] = _ap(accum_out)
        o, i = _ap(out), _ap(in_)
        return self.op("scalar", lambda e: e.activation(out=o, in_=i, func=func, **kw), r, w)

    def tt(self, eng, out, in0, in1, op):
        r, w = self._rw([out], [in0, in1])
        o, a, b = _ap(out), _ap(in0), _ap(in1)
        return self.op(eng, lambda e: e.tensor_tensor(out=o, in0=a, in1=b, op=op), r, w)

    def ts(self, eng, out, in0, s1, s2=None, op0=ALU.mult, op1=None):
        eng = "vector"
        r, w = self._rw([out], [in0, s1, s2])
        o, a, x1, x2 = _ap(out), _ap(in0), _ap(s1), _ap(s2)
        if op1 is None:
            return self.op(eng, lambda e: e.tensor_scalar(out=o, in0=a, scalar1=x1, scalar2=None, op0=op0), r, w)
        return self.op(eng, lambda e: e.tensor_scalar(out=o, in0=a, scalar1=x1, scalar2=x2, op0=op0, op1=op1), r, w)

    def stt(self, eng, out, in0, scalar, in1, op0, op1):
        eng = "vector"
        r, w = self._rw([out], [in0, scalar, in1])
        o, a, s, b = _ap(out), _ap(in0), _ap(scalar), _ap(in1)
        return self.op(eng, lambda e: e.scalar_tensor_tensor(out=o, in0=a, scalar=s, in1=b, op0=op0, op1=op1), r, w)

    def copy(self, eng, out, in_):
        r, w = self._rw([out], [in_])
        o, a = _ap(out), _ap(in_)
        if eng == "scalar":
            return self.op(eng, lambda e: e.activation(out=o, in_=a, func=AF.Copy), r, w)
        return self.op(eng, lambda e: e.tensor_copy(out=o, in_=a), r, w)

    def memset(self, eng, out, val):
        r, w = self._rw([out], [])
        r = [out.buf]
        o = _ap(out)
        i = self.op(eng, lambda e: e.memset(o, val), r, w)
        out.buf.r[("ms", eng)] = i
        return i

    def recip(self, out, in_, eng="vector"):
        r, w = self._rw([out], [in_])
        o, a = _ap(out), _ap(in_)
        return self.op(eng, lambda e: e.reciprocal(out=o, in_=a), r, w)

    def rmax(self, out, in_, eng="vector"):
        r, w = self._rw([out], [in_])
        o, a = _ap(out), _ap(in_)
        return self.op(eng, lambda e: e.reduce_max(out=o, in_=a, axis=AX.X), r, w)

    def scan(self, out, d0, d1, initial, op0=ALU.mult, op1=ALU.add):
        r, w = self._rw([out], [d0, d1, initial])
        o, a, b, i = _ap(out), _ap(d0), _ap(d1), _ap(initial)
        return self.op("vector", lambda e: e.tensor_tensor_scan(out=o, data0=a, data1=b, initial=i, op0=op0, op1=op1), r, w)

    def any2(self):
        self.rr += 1
        return ("vector", "gpsimd")[self.rr % 2]

    def any3(self):
        self.rr += 1
        return ("vector", "gpsimd", "scalar")[self.rr % 3]

    def evac(self):
        self.rr += 1
        return ("vector", "scalar")[self.rr % 2]

    def emit(self):
        nc = self.nc
        ops = self.ops
        for o in ops:
            for d in list(o.deps):
                do = ops[d]
                if (not do.dma) and (not o.dma) and do.eng == o.eng and o.eng == "tensor":
                    o.deps.discard(d)
                    continue
                do.sig = True
        sems = {e: nc.alloc_semaphore("s_" + e) for e in COMPUTE}
        dsems = {q: [nc.alloc_semaphore("d_%s_%d" % (q, i)) for i in range(N_DMA_SEMS)] for q in DMA_QUEUES}
        cnt = {e: 0 for e in COMPUTE}
        dcnt = {q: [0] * N_DMA_SEMS for q in DMA_QUEUES}
        for o in ops:
            if o.dma:
                j = o.slot
                o.sem = dsems[o.eng][j]
                o.prev_val = 16 * dcnt[o.eng][j]
                dcnt[o.eng][j] += 1
                o.val = 16 * dcnt[o.eng][j]
            elif o.sig:
                cnt[o.eng] += 1
                o.sem = sems[o.eng]
                o.val = cnt[o.eng]

        def run_stream(ename, eng, final=False):
            seen = {}
            for o in self.streams[ename]:
                waits = {}
                for d in o.deps:
                    do = ops[d]
                    k = id(do.sem)
                    if seen.get(k, 0) >= do.val:
                        continue
                    if k not in waits or waits[k][1] < do.val:
                        waits[k] = (do.sem, do.val)
                if o.dma and o.prev_val > 0:
                    k = id(o.sem)
                    if seen.get(k, 0) < o.prev_val and (k not in waits or waits[k][1] < o.prev_val):
                        waits[k] = (o.sem, o.prev_val)
                for k, (s, v) in waits.items():
                    eng.wait_ge(s, v)
                    seen[k] = v
                ins = o.fn(eng)
                if o.dma:
                    ins.then_inc(o.sem, 16)
                elif o.sig:
                    ins.then_inc(o.sem, 1)
            if final:
                for q in DMA_QUEUES:
                    for j in range(N_DMA_SEMS):
                        if dcnt[q][j] > 0:
                            eng.wait_ge(dsems[q][j], 16 * dcnt[q][j])
                for e in COMPUTE:
                    if cnt[e] > 0:
                        eng.wait_ge(sems[e], cnt[e])

        with nc.Block() as block:
            @block.sync
            def _(e):
                run_stream("sync", e, final=True)

            @block.tensor
            def _(e):
                run_stream("tensor", e)

            @block.vector
            def _(e):
                run_stream("vector", e)

            @block.scalar
            def _(e):
                run_stream("scalar", e)

            @block.gpsimd
            def _(e):
                run_stream("gpsimd", e)


class Ctx:
    pass


def split_cols(buf, n, w):
    return [Buf("%s_%d" % (buf.name, i), buf.t[:, :, i * w:(i + 1) * w]) for i in range(n)]


def cast_load(fw, C, dst, src, shape, q=None):
    st = C.stg[C.stg_i % len(C.stg)]
    C.stg_i += 1
    n = 1
    for s in shape[1:]:
        n *= s
    sv = st[:, 0:n]
    if len(shape) == 3:
        sv = sv.rearrange("p (a b) -> p a b", b=shape[2])
    fw.dma(("sync", "gpsimd")[C.stg_i % 2] if q is None else q, sv, src)
    fw.copy(fw.any3(), dst, sv)


def load_colvec(fw, C, dst, src_vec, J, tmpb):
    fw.dma("sync", tmpb[0:J, :], src_vec.rearrange("(j p) -> j p", p=128))
    ps = fw.ps()
    fw.transpose(ps[:, 0:J], tmpb[0:J, :], C.ident[0:J, 0:J])
    fw.copy("vector", dst, ps[:, 0:J])


def rms_rstd(fw, C, chunks, nfeat, N, sq, rstd, tmp):
    ps = fw.ps()
    nchunk = len(chunks)
    for c, xc in enumerate(chunks):
        fw.act(sq[:, c, 0:N], xc, AF.Square)
    for c in range(nchunk):
        fw.matmul(ps[:, 0:N], C.ones_bf[:, :], sq[:, c, 0:N], start=(c == 0), stop=(c == nchunk - 1))
    fw.act(tmp[:, 0:N], ps[:, 0:N], AF.Sqrt, scale=1.0 / nfeat, bias=C.eps_col[:, 0:1])
    fw.recip(rstd[:, 0:N], tmp[:, 0:N])
    return rstd


def stage_in(fw, C):
    m = fw.mark()
    xin = [fw.alloc("xin%d" % i, [128, 1024]) for i in range(2)]
    xo = [fw.alloc("xo%d" % i, [128, 8, 128]) for i in range(2)]
    xTv = C.xT.v.rearrange("(c p) t -> p c t", p=128)
    for tt in range(NT // 128):
        a = xin[tt % 2]
        o = xo[tt % 2]
        fw.dma("sync", a[:, :], C.x_tok[tt * 128:(tt + 1) * 128, :])
        for half in range(2):
            ps = fw.ps()
            for c in range(4):
                fw.transpose(ps[:, c * 128:(c + 1) * 128], a[:, (half * 4 + c) * 128:(half * 4 + c + 1) * 128], C.ident[:, :])
            fw.copy(fw.evac(), o[:, half * 4:(half + 1) * 4, :], ps[:, :].rearrange("p (c t) -> p c t", t=128))
        fw.dma("gpsimd", xTv[:, :, tt * 128:(tt + 1) * 128], o[:, :, :])
    fw.release(m)


def stage_mods(fw, C, l):
    m = fw.mark()
    wt = [fw.alloc("wmod%d" % i, [128, 8, 128]) for i in range(3)]
    raw = fw.alloc("modraw", [128, 48, 2])
    bm = fw.alloc("bmod", [128, 48])
    g1 = fw.alloc("gmix", [128, 8])
    g2 = fw.alloc("gffn", [128, 8])
    wv = C.w_mod[l].rearrange("(k p) f -> p k f", p=128)
    tb = [fw.alloc("tb%d" % i, [128, 128]) for i in range(3)]
    load_colvec(fw, C, bm[:, :], C.b_mod[l], 48, tb[0])
    load_colvec(fw, C, g1[:, :], C.g_norm_mix[l], 8, tb[1])
    load_colvec(fw, C, g2[:, :], C.g_norm_ffn[l], 8, tb[2])
    ps = fw.ps()
    for fo in range(48):
        w = wt[fo % 3]
        fw.dma(("sync", "gpsimd")[fo % 2], w[:, :, :], wv[:, :, fo * 128:(fo + 1) * 128])
        for k in range(8):
            fw.matmul(ps[:, fo * 2:fo * 2 + 2], w[:, k, :], C.cT[:, k, :], start=(k == 0), stop=(k == 7))
    fw.tt("vector", raw[:, :, :], ps[:, 0:96].rearrange("p (j n) -> p j n", n=2), bm[:, :].unsq(2).bc([128, 48, 2]), ALU.add)
    for (A, B, G, g, base) in ((C.mA_m, C.mB_m, C.mG_m, g1, 0), (C.mA_f, C.mB_f, C.mG_f, g2, 24)):
        fw.ts("vector", A[:, :, :], raw[:, base + 8:base + 16, :], 1.0, None, op0=ALU.add)
        fw.tt("vector", A[:, :, :], A[:, :, :], g[:, :].unsq(2).bc([128, 8, 2]), ALU.mult)
        fw.copy("vector", B[:, :, :], raw[:, base:base + 8, :])
        fw.copy("vector", G[:, :, :], raw[:, base + 16:base + 24, :])
    fw.release(m)


def load_win(fw, C, l, WinB):
    wv = C.w_in[l].rearrange("(k p) f -> p k f", p=128)
    segs = [(R_QKV, 0, 1536), (R_Z, 1536, 512), (R_U, 2064, 512), (R_QA, 2576, 384), (R_KVA, 2960, 256), (R_GATE, 3248, 3072)]
    fw.memset("gpsimd", WinB[R_MA // 128][:, :, :], 0.0)
    fw.memset("gpsimd", WinB[R_MB // 128][:, :, :], 0.0)
    for (dc, sc, wd) in segs:
        for o in range(0, wd, 128):
            cast_load(fw, C, WinB[(dc + o) // 128][:, :, :], wv[:, :, sc + o:sc + o + 128], [128, 8, 128])
    small = [(R_MA + 0, 2056, 8), (R_MA + 32, 2048, 8), (R_MA + 64, 3216, 32), (R_MB + 64, 3232, 16), (R_MB + 80, 3216, 16)]
    for (dc, sc, wd) in small:
        cast_load(fw, C, WinB[dc // 128][:, :, dc % 128:dc % 128 + wd], wv[:, :, sc:sc + wd], [128, 8, wd])


def stage_inproj(fw, C, l):
    m = fw.mark()
    Win = fw.alloc("Win", [128, 8, PW], BF16)
    m2 = fw.mark()
    C.stg = [fw.alloc("stg%d" % i, [128, 1024]) for i in range(4)]
    C.stg_i = 0
    WinB = split_cols(Win, NFO, 128)
    load_win(fw, C, l, WinB)
    fw.release(m2)
    xt = [fw.alloc("xt%d" % i, [128, 8, 512]) for i in range(2)]
    sq = fw.alloc("sq", [128, 8, 512], BF16)
    h = [fw.alloc("h%d" % i, [128, 8, 512], BF16) for i in range(2)]
    rstd = fw.alloc("rstd", [128, 512])
    tmp = [fw.alloc("tmp%d" % i, [128, 512]) for i in range(2)]
    so = [fw.alloc("so%d" % i, [128, 4, 512]) for i in range(3)]
    xTv = C.xT.v.rearrange("(c p) t -> p c t", p=128)
    pv = C.projT.v.rearrange("(j p) t -> p j t", p=128)
    soi = 0
    for t in range(NT // 512):
        n = 0 if t < 2 else 1
        x = xt[t % 2]
        hh = h[t % 2]
        fw.dma("sync", x[:, :, :], xTv[:, :, t * 512:(t + 1) * 512])
        rms_rstd(fw, C, [x[:, c, :] for c in range(8)], D, 512, sq, rstd, tmp[0])
        for c in range(8):
            tm = tmp[c % 2]
            fw.tt(("vector", "gpsimd")[c % 2], tm[:, :], x[:, c, :], rstd[:, :], ALU.mult)
            fw.act(hh[:, c, :], tm[:, :], AF.Identity, scale=C.mA_m[:, c, n:n + 1], bias=C.mB_m[:, c, n:n + 1])
        for fo in range(NFO):
            ps = fw.ps()
            for k in range(8):
                fw.matmul(ps[:, :], WinB[fo][:, k, :], hh[:, k, :], start=(k == 0), stop=(k == 7))
            s = so[soi % 3]
            fw.copy(fw.evac(), s[:, fo % 4, :], ps[:, :])
            if fo % 4 == 3 or fo == NFO - 1:
                f0 = fo - fo % 4
                nn = fo % 4 + 1
                fw.dma("gpsimd", pv[:, f0:f0 + nn, t * 512:(t + 1) * 512], s[:, 0:nn, :])
                soi += 1
    fw.release(m)


TS5 = 16
NCH = NT // TS5
TWO_PI = 2.0 * math.pi


def load_T(fw, C, dst, src2d, J, tmpb):
    fw.dma("sync", tmpb[0:J, :], src2d)
    ps = fw.ps()
    fw.transpose(ps[:, 0:J], tmpb[0:J, :], C.ident[0:J, 0:J])
    fw.copy("vector", dst, ps[:, 0:J])


def cmul(fw, eng, outr, outi, ar, ai, br, bi, t1, t2, nai=None):
    fw.tt(eng, t1, ar, br, ALU.mult)
    fw.tt(eng, t2, ai, bi, ALU.mult)
    fw.tt(eng, outr, t1, t2, ALU.subtract)
    fw.tt(eng, t1, ar, bi, ALU.mult)
    fw.tt(eng, t2, ai, br, ALU.mult)
    fw.tt(eng, outi, t1, t2, ALU.add)


def stage_s5(fw, C, l):
    m = fw.mark()
    pv = C.projT.v.rearrange("(j p) t -> p j t", p=128)
    T = TS5
    U = fw.alloc("U", [128, 4, T, NCH], BF16)
    BD = fw.alloc("BD", [128, 31, 4, 128], BF16)
    HB = fw.alloc("HB", [128, 2, 2, 16, NCH], BF16)
    Er = fw.alloc("Er", [128, 2, 17, 16])
    Ei = fw.alloc("Ei", [128, 2, 17, 16])
    Cbd = fw.alloc("Cbd", [128, 2, 16, 32])
    Wglu = fw.alloc("Wglu", [128, 4, 512], BF16)
    dsk = fw.alloc("dsk", [128, 4])
    bgl = fw.alloc("bgl", [128, 4])
    fin = fw.alloc("fin", [128, 2, 2, NPS, 16])
    m2 = fw.mark()
    HA = fw.alloc("HA", [128, 2, 16, NCH + 1])
    HBk = fw.alloc("HBk", [128, 2, 16, NCH + 1])
    Bb = fw.alloc("Bb", [128, 2, 3, 16, 32])
    C.stg = [fw.alloc("stg%d" % i, [128, 2048]) for i in range(1)]
    C.stg_i = 0
    tb = [fw.alloc("tb%d" % i, [128, 128]) for i in range(4)]
    m3 = fw.mark()
    wv = C.w_glu[l].rearrange("(k p) f -> p k f", p=128)
    cast_load(fw, C, Wglu[:, :, :], wv, [128, 4, 512])
    load_colvec(fw, C, dsk[:, :], C.s5_d[l], 4, tb[0])
    load_colvec(fw, C, bgl[:, :], C.b_glu[l], 4, tb[1])
    lam = fw.alloc("lam", [128, 2, 2, 16])
    dtb = fw.alloc("dtb", [128, 2, 16])
    small = fw.alloc("small", [16, 2])
    for d in range(2):
        load_T(fw, C, lam[:, d, 0, :], C.s5_lam_re[l, d].rearrange("(j g) p -> j (g p)", g=2), 16, tb[2])
        load_T(fw, C, lam[:, d, 1, :], C.s5_lam_im[l, d].rearrange("(j g) p -> j (g p)", g=2), 16, tb[3])
        fw.dma("sync", small[:, :], C.s5_log_dt[l, d].rearrange("(j g) -> j g", g=2))
        fw.copy("vector", tb[2][0:16, :].rearrange("j (g p) -> j g p", g=2), small[:, :].unsq(2).bc([16, 2, 64]))
        ps = fw.ps()
        fw.transpose(ps[:, 0:16], tb[2][0:16, :], C.ident[0:16, 0:16])
        fw.act(dtb[:, d, :], ps[:, 0:16], AF.Exp)
    rho = fw.alloc("rho", [128, 2, 16])
    th = fw.alloc("th", [128, 2, 16])
    fw.tt("vector", rho[:, :, :], lam[:, :, 0, :], dtb[:, :, :], ALU.mult)
    fw.tt("vector", th[:, :, :], lam[:, :, 1, :], dtb[:, :, :], ALU.mult)
    ang = fw.alloc("ang", [128, 17, 16])
    kf = fw.alloc("kf", [128, 17, 16])
    ki = fw.alloc("ki", [128, 17, 16], I32)
    msk = fw.alloc("msk", [128, 17, 16])
    mag = fw.alloc("mag", [128, 17, 16])
    sn = fw.alloc("sn", [128, 17, 16])
    cs = fw.alloc("cs", [128, 17, 16])
    svb = C.svec[:, :].unsq(2).bc([128, 17, 16])
    for d in range(2):
        fw.tt("vector", ang[:, :, :], th[:, d, :].unsq(1).bc([128, 17, 16]), svb, ALU.mult)
        fw.ts("vector", kf[:, :, :], ang[:, :, :], 1.0 / TWO_PI, None, op0=ALU.mult)
        fw.copy("vector", ki[:, :, :], kf[:, :, :])
        fw.copy("vector", kf[:, :, :], ki[:, :, :])
        fw.stt("vector", ang[:, :, :], kf[:, :, :], -TWO_PI, ang[:, :, :], ALU.mult, ALU.add)
        fw.ts("vector", msk[:, :, :], ang[:, :, :], math.pi, None, op0=ALU.is_gt)
        fw.stt("vector", ang[:, :, :], msk[:, :, :], -TWO_PI, ang[:, :, :], ALU.mult, ALU.add)
        fw.ts("vector", msk[:, :, :], ang[:, :, :], -math.pi, None, op0=ALU.is_lt)
        fw.stt("vector", ang[:, :, :], msk[:, :, :], TWO_PI, ang[:, :, :], ALU.mult, ALU.add)
        fw.act(sn[:, :, :], ang[:, :, :], AF.Sin)
        fw.ts("vector", msk[:, :, :], ang[:, :, :], -1.0, None, op0=ALU.mult)
        fw.tt("vector", msk[:, :, :], msk[:, :, :], ang[:, :, :], ALU.max)
        fw.act(cs[:, :, :], msk[:, :, :], AF.Sin, scale=-1.0, bias=C.halfpi[:, 0:1])
        fw.tt("vector", mag[:, :, :], rho[:, d, :].unsq(1).bc([128, 17, 16]), svb, ALU.mult)
        fw.act(mag[:, :, :], mag[:, :, :], AF.Exp)
        fw.tt("vector", Er[:, d, :, :], mag[:, :, :], cs[:, :, :], ALU.mult)
        fw.tt("vector", Ei[:, d, :, :], mag[:, :, :], sn[:, :, :], ALU.mult)
    cf = fw.alloc("cf", [128, 2, 2, 16])
    t1 = fw.alloc("t1", [128, 2, 16])
    t2 = fw.alloc("t2", [128, 2, 16])
    den = fw.alloc("den", [128, 2, 16])
    nr = fw.alloc("nr", [128, 2, 16])
    fw.tt("vector", t1[:, :, :], lam[:, :, 0, :], lam[:, :, 0, :], ALU.mult)
    fw.tt("vector", t2[:, :, :], lam[:, :, 1, :], lam[:, :, 1, :], ALU.mult)
    fw.tt("vector", den[:, :, :], t1[:, :, :], t2[:, :, :], ALU.add)
    fw.recip(den[:, :, :], den[:, :, :])
    fw.ts("vector", nr[:, :, :], Er[:, :, 1, :], -1.0, None, op0=ALU.add)
    fw.tt("vector", t1[:, :, :], nr[:, :, :], lam[:, :, 0, :], ALU.mult)
    fw.tt("vector", t2[:, :, :], Ei[:, :, 1, :], lam[:, :, 1, :], ALU.mult)
    fw.tt("vector", t1[:, :, :], t1[:, :, :], t2[:, :, :], ALU.add)
    fw.tt("vector", cf[:, :, 0, :], t1[:, :, :], den[:, :, :], ALU.mult)
    fw.tt("vector", t1[:, :, :], Ei[:, :, 1, :], lam[:, :, 0, :], ALU.mult)
    fw.tt("vector", t2[:, :, :], nr[:, :, :], lam[:, :, 1, :], ALU.mult)
    fw.tt("vector", t1[:, :, :], t1[:, :, :], t2[:, :, :], ALU.subtract)
    fw.tt("vector", cf[:, :, 1, :], t1[:, :, :], den[:, :, :], ALU.mult)
    Bn = fw.alloc("Bn", [128, 2, 16, 16])
    fw.dma("sync", Bn[:, 0, :, :], C.s5_b_re[l].rearrange("(j g) p c -> (g p) j c", g=2))
    fw.dma("sync", Bn[:, 1, :, :], C.s5_b_im[l].rearrange("(j g) p c -> (g p) j c", g=2))
    bb = fw.alloc("bb", [128, 2, 16, 16])
    u1 = fw.alloc("u1", [128, 16, 16])
    u2 = fw.alloc("u2", [128, 16, 16])
    mS = C.maskS[:, :].unsq(1).unsq(3).bc([128, 16, 2, 16])
    for d in range(2):
        cr = cf[:, d, 0, :].unsq(2).bc([128, 16, 16])
        ci = cf[:, d, 1, :].unsq(2).bc([128, 16, 16])
        cmul(fw, "vector", bb[:, 0, :, :], bb[:, 1, :, :], cr, ci, Bn[:, 0, :, :], Bn[:, 1, :, :], u1[:, :, :], u2[:, :, :])
        for ri in range(2):
            fw.tt("vector", Bb[:, d, ri, :, :].rearrange("p j (g c) -> p j g c", g=2), bb[:, ri, :, :].unsq(2).bc([128, 16, 2, 16]), mS, ALU.mult)
        fw.ts("vector", Bb[:, d, 2, :, :], Bb[:, d, 1, :, :], -1.0, None, op0=ALU.mult)
    cn = fw.alloc("cn", [128, 64])
    cx = fw.alloc("cx", [128, 2, 64])
    for ri, src in enumerate((C.s5_c_re, C.s5_c_im)):
        cv = src[l].rearrange("g c p -> (g c) p")
        for gh in range(4):
            fw.dma("sync", cn[:, :], cv[gh * 128:(gh + 1) * 128, :])
            fw.tt("vector", cx[:, :, :], cn[:, :].unsq(1).bc([128, 2, 64]), C.maskR[:, :].unsq(2).bc([128, 2, 64]), ALU.mult)
            ps = fw.ps()
            fw.transpose(ps[:, 0:128], cx[:, :, :].rearrange("p g q -> p (g q)"), C.ident[:, :])
            fw.copy("vector", Cbd[:, ri, gh * 4:(gh + 1) * 4, :].rearrange("p j x -> p (j x)"), ps[:, 0:128])
    uf = fw.alloc("uf", [128, 512])
    for gh in range(4):
        for t in range(NT // 512):
            fw.dma("sync", uf[:, :], pv[:, R_U // 128 + gh, t * 512:(t + 1) * 512])
            fw.copy(fw.any2(), U[:, gh, :, t * 32:(t + 1) * 32], uf[:, :].rearrange("p (k i) -> p i k", i=T))
    Gt = fw.alloc("Gt", [128, 2, 2, 16, 32])
    Gb = fw.alloc("Gb", [128, 2, 2, 16, 32], BF16)
    Bbb = fw.alloc("Bbb", [128, 2, 3, 16, 32], BF16)
    fw.copy("vector", Bbb[:, :, :, :, :], Bb[:, :, :, :, :])
    g1 = fw.alloc("g1", [128, 16, 32])
    g2 = fw.alloc("g2", [128, 16, 32])
    blk = fw.alloc("blk", [128, 64])
    fw.memset("gpsimd", BD[:, :, :, :], 0.0)
    mQ = C.maskQ[:, :].unsq(2).bc([128, 4, 32])

    def make_G(dst, d, s, j0, j1, t1_, t2_, neg_im):
        er = Er[:, d, s, j0:j1].unsq(2).bc([128, j1 - j0, 32])
        ei = Ei[:, d, s, j0:j1].unsq(2).bc([128, j1 - j0, 32])
        cmul(fw, "vector", dst[0], dst[1], Cbd[:, 0, j0:j1, :], Cbd[:, 1, j0:j1, :], er, ei, t1_, t2_)

    for dd in range(16):
        for d in range(2):
            make_G((Gt[:, d, 0, :, :], Gt[:, d, 1, :, :]), d, dd, 0, 16, g1[:, :, :], g2[:, :, :], False)
        fw.copy("gpsimd", Gb[:, :, :, :, :], Gt[:, :, :, :, :])
        for gh in range(4):
            ps = fw.ps()
            for jl in (3, 0, 1, 2):
                j = gh * 4 + jl
                if jl == 3:
                    o_ = ps[64:128, :]
                    lh = lambda d, bsel: Bbb[:, d, bsel, j - 1:j + 1, :].rearrange("p a b -> p (a b)")
                else:
                    o_ = ps[32 * jl:32 * jl + 32, :]
                    lh = lambda d, bsel: Bbb[:, d, bsel, j, :]
                if dd == 0:
                    seq = [(0, 0, 0), (0, 2, 1), (1, 0, 0), (1, 2, 1)]
                    for n_, (d, bsel, ri) in enumerate(seq):
                        fw.matmul(o_[:, 0:32], lh(d, bsel), Gb[:, d, ri, j, :], start=(n_ == 0), stop=(n_ == 3))
                else:
                    for d in range(2):
                        fw.matmul(o_[:, 32 * d:32 * d + 32], lh(d, 0), Gb[:, d, 0, j, :], start=True, stop=False)
                        fw.matmul(o_[:, 32 * d:32 * d + 32], lh(d, 2), Gb[:, d, 1, j, :], start=False, stop=True)
            fw.copy("vector", blk[:, :], ps[:, 0:64])
            if dd == 0:
                fw.tt("vector", BD[:, 15, gh, :].rearrange("p (a b) -> p a b", b=32), blk[:, 0:32].unsq(1).bc([128, 4, 32]), mQ, ALU.mult)
            else:
                fw.tt("vector", BD[:, 15 + dd, gh, :].rearrange("p (a b) -> p a b", b=32), blk[:, 0:32].unsq(1).bc([128, 4, 32]), mQ, ALU.mult)
                fw.tt("vector", BD[:, 15 - dd, gh, :].rearrange("p (a b) -> p a b", b=32), blk[:, 32:64].unsq(1).bc([128, 4, 32]), mQ, ALU.mult)
    fw.release(m3)
    LX = fw.alloc("LX", [128, 16, 2, 128], BF16)
    LX3 = fw.alloc("LX3", [128, 16, 2, 128], BF16)
    NCHN = 4
    wt = [fw.alloc("wt%d" % i, [128, 2, 4, 32]) for i in range(NCHN)]
    w1 = [fw.alloc("w1_%d" % i, [128, 4, 32]) for i in range(NCHN)]
    w2 = [fw.alloc("w2_%d" % i, [128, 4, 32]) for i in range(NCHN)]

    def rr(gens):
        gens = list(gens)
        while gens:
            for g in list(gens):
                try:
                    next(g)
                except StopIteration:
                    gens.remove(g)

    def cmul_g(eng, outr, outi, ar, ai, br, bi, t1, t2):
        fw.tt(eng, t1, ar, br, ALU.mult)
        fw.tt(eng, t2, ai, bi, ALU.mult)
        yield
        fw.tt(eng, outr, t1, t2, ALU.subtract)
        yield
        fw.tt(eng, t1, ar, bi, ALU.mult)
        fw.tt(eng, t2, ai, br, ALU.mult)
        yield
        fw.tt(eng, outi, t1, t2, ALU.add)
        yield

    for d in range(2):
        Hdst = HA if d == 0 else HBk
        slot0 = 1 if d == 0 else 0
        for gh in range(4):
            def chain(cn):
                for i in range(cn, 16, NCHN):
                    s_ = 15 - i if d == 0 else i
                    w = wt[cn]
                    er = Er[:, d, s_, gh * 4:(gh + 1) * 4].unsq(2).bc([128, 4, 32])
                    ei = Ei[:, d, s_, gh * 4:(gh + 1) * 4].unsq(2).bc([128, 4, 32])
                    yield from cmul_g(("vector", "gpsimd")[cn % 2], w[:, 0, :, :], w[:, 1, :, :], Bb[:, d, 0, gh * 4:(gh + 1) * 4, :], Bb[:, d, 1, gh * 4:(gh + 1) * 4, :], er, ei, w1[cn][:, :, :], w2[cn][:, :, :])
                    ps = fw.ps()
                    for ri in range(2):
                        fw.transpose(ps[:, ri * 128:(ri + 1) * 128], w[:, ri, :, :].rearrange("p a b -> p (a b)"), C.ident[:, :])
                    yield
                    fw.copy(("vector", "scalar")[cn % 2], LX[:, i, :, :], ps[:, 0:256].rearrange("p (r x) -> p r x", r=2))
                    yield
                    fw.ts("vector", LX3[64:128, i, :, :], LX[64:128, i, :, :], C.rm3[64:128, 0:1], None, op0=ALU.mult)
                    yield
            rr([chain(cn) for cn in range(NCHN)])
            for jl in range(4):
                for ri in range(2):
                    ps = fw.ps()
                    for i in range(16):
                        if jl == 3:
                            fw.matmul(ps[:, 0:NCH], LX3[64:128, i, ri, :], U[64:128, gh, i, :], start=(i == 0), stop=(i == 15))
                        else:
                            fw.matmul(ps[:, 0:NCH], LX[32 * jl:32 * jl + 32, i, ri, :], U[32 * jl:32 * jl + 32, gh, i, :], start=(i == 0), stop=(i == 15))
                    fw.copy(fw.evac(), Hdst[:, ri, gh * 4 + jl, slot0:slot0 + NCH], ps[:, 0:NCH])
    h0 = fw.alloc("h0", [128, 2, 2, 16])
    for d in range(2):
        load_T(fw, C, h0[:, d, 0, :], C.s5re[l, d].rearrange("(j g) p -> j (g p)", g=2), 16, tb[0])
        load_T(fw, C, h0[:, d, 1, :], C.s5im[l, d].rearrange("(j g) p -> j (g p)", g=2), 16, tb[1])
    sa = [fw.alloc("sa%d" % i, [128, 2, 16]) for i in range(2)]
    sb = [fw.alloc("sb%d" % i, [128, 2, 16]) for i in range(2)]
    mu3 = fw.alloc("mu3", [128, 2, 3, 16])
    for d in range(2):
        fw.copy("vector", mu3[:, d, 0, :], Er[:, d, 16, :])
        fw.copy("vector", mu3[:, d, 1, :], Ei[:, d, 16, :])
        fw.ts("vector", mu3[:, d, 2, :], Ei[:, d, 16, :], -1.0, None, op0=ALU.mult)
    for d in range(2):
        eng = ("vector", "gpsimd")[d]
        H = HA if d == 0 else HBk
        a, b = sa[d], sb[d]
        mur = mu3[:, d, 0, :].unsq(1).bc([128, 2, 16])
        order = list(enumerate(SEQS)) if d == 0 else list(enumerate(SEQS))[::-1]
        for si, (off, L, smp) in order:
            k0, k1 = off // T, (off + L) // T
            if d == 0:
                init, steps = k0, [(k, k + 1) for k in range(k0, k1)]
            else:
                init, steps = k1, [(k, k - 1) for k in range(k1, k0, -1)]
            if smp:
                fw.copy(eng, H[:, :, :, init], h0[:, d, :, :])
            else:
                fw.memset(eng, H[:, :, :, init], 0.0)
            for n_, (src, dst) in enumerate(steps):
                last = (n_ == len(steps) - 1)
                fw.tt(eng, a[:, :, :], H[:, :, :, src], mur, ALU.mult)
                fw.tt(eng, b[:, 0, :], H[:, 1, :, src], mu3[:, d, 2, :], ALU.mult)
                fw.tt(eng, b[:, 1, :], H[:, 0, :, src], mu3[:, d, 1, :], ALU.mult)
                fw.tt(eng, a[:, :, :], a[:, :, :], b[:, :, :], ALU.add)
                if last:
                    if not smp:
                        fw.tt(eng, fin[:, d, :, si, :], a[:, :, :], H[:, :, :, dst], ALU.add)
                else:
                    fw.tt(eng, H[:, :, :, dst], a[:, :, :], H[:, :, :, dst], ALU.add)
    fo_ = fw.alloc("fo", [16, 128])
    for d in range(2):
        for ri, dst in enumerate((C.o_s5re, C.o_s5im)):
            for si in range(NPS):
                ps = fw.ps()
                fw.transpose(ps[0:16, 0:128], fin[:, d, ri, si, :], C.ident[:, :])
                fw.copy("vector", fo_[:, :], ps[0:16, 0:128])
                fw.dma("gpsimd", dst[si, l, d].rearrange("(j g) p -> j (g p)", g=2), fo_[:, :])
    fw.copy("vector", HB[:, 0, :, :, :], HA[:, :, :, 0:NCH])
    fw.copy("gpsimd", HB[:, 1, :, :, :], HBk[:, :, :, 1:NCH + 1])
    fw.release(m2)
    yg = fw.alloc("yg", [128, 4, NT], BF16)
    Y = fw.alloc("Y", [128, NT])
    uf = fw.alloc("uf2", [128, NT])
    Gm = [fw.alloc("Gm%d" % i, [128, 2, 2, 4, 32]) for i in range(2)]
    Gmb = [fw.alloc("Gmb%d" % i, [128, 2, 2, 5, 32], BF16) for i in range(2)]
    for g_ in Gmb:
        fw.memset("vector", g_[:, :, :, :, :], 0.0)
    g1 = fw.alloc("g1m", [128, 4, 32])
    g2 = fw.alloc("g2m", [128, 4, 32])
    y2 = fw.alloc("y2", [128, NT])
    y3 = fw.alloc("y3", [128, NT])
    for gh in range(4):
        fw.dma("sync", uf[:, :], pv[:, R_U // 128 + gh, :])
        for j in range(16):
            G_, Gb_ = Gm[j % 2], Gmb[j % 2]
            for d in range(2):
                s_ = j + 1 if d == 0 else 16 - j
                er = Er[:, d, s_, gh * 4:(gh + 1) * 4].unsq(2).bc([128, 4, 32])
                ei = Ei[:, d, s_, gh * 4:(gh + 1) * 4].unsq(2).bc([128, 4, 32])
                cmul(fw, ("vector", "gpsimd")[d], G_[:, d, 0, :, :], G_[:, d, 1, :, :], Cbd[:, 0, gh * 4:(gh + 1) * 4, :], Cbd[:, 1, gh * 4:(gh + 1) * 4, :], er, ei, g1[:, :, :] if d == 0 else y2[:, 0:128].rearrange("p (a b) -> p a b", b=32), g2[:, :, :] if d == 0 else y3[:, 0:128].rearrange("p (a b) -> p a b", b=32))
                fw.ts(("vector", "gpsimd")[d], G_[:, d, 1, :, :], G_[:, d, 1, :, :], -1.0, None, op0=ALU.mult)
            fw.copy("vector", Gb_[:, :, :, 0:3, :], G_[:, :, :, 0:3, :])
            fw.copy("vector", Gb_[:, :, :, 4, :], G_[:, :, :, 3, :])
            ps = fw.ps()
            for i in range(16):
                fw.matmul(ps[:, 0:NCH], BD[:, 15 + j - i, gh, :], U[:, gh, i, :], start=(i == 0), stop=False)
            for jl in range(4):
                n_ = 0
                for d in range(2):
                    for ri in range(2):
                        n_ += 1
                        if jl == 3:
                            fw.matmul(ps[64:128, 0:NCH], Gb_[:, d, ri, 3:5, :].rearrange("p a b -> p (a b)"), HB[:, d, ri, gh * 4 + jl, :], start=False, stop=(n_ == 4))
                        else:
                            fw.matmul(ps[32 * jl:32 * jl + 32, 0:NCH], Gb_[:, d, ri, jl, :], HB[:, d, ri, gh * 4 + jl, :], start=False, stop=(n_ == 4 and jl != 2))
            fw.copy(fw.evac(), Y[:, :].rearrange("p (k i) -> p i k", i=T)[:, j, :], ps[:, 0:NCH])
        fw.stt("vector", Y[:, :], uf[:, :], dsk[:, gh:gh + 1], Y[:, :], ALU.mult, ALU.add)
        fw.tt("gpsimd", y2[:, :], Y[:, :], Y[:, :], ALU.mult)
        fw.ts("vector", y2[:, :], y2[:, :], 0.044715, 1.0, op0=ALU.mult, op1=ALU.add)
        fw.tt("gpsimd", y2[:, :], y2[:, :], Y[:, :], ALU.mult)
        fw.act(y3[:, :], y2[:, :], AF.Sigmoid, scale=2.0 * 0.7978845608028654)
        fw.tt("vector", yg[:, gh, :], Y[:, :], y3[:, :], ALU.mult)
    ob = [fw.alloc("ob%d" % i, [128, 4, 512], BF16) for i in range(2)]
    sg = [fw.alloc("sgl%d" % i, [128, 512]) for i in range(2)]
    bT = C.brT.v[1].rearrange("(c p) t -> p c t", p=128)
    for t in range(NT // 512):
        o = ob[t % 2]
        for oc_ in range(4):
            ps = fw.ps()
            for k in range(4):
                fw.matmul(ps[:, :], Wglu[:, k, oc_ * 128:(oc_ + 1) * 128], yg[:, k, t * 512:(t + 1) * 512], start=(k == 0), stop=(k == 3))
            sgt = sg[oc_ % 2]
            fw.act(sgt[:, :], ps[:, :], AF.Sigmoid, bias=bgl[:, oc_:oc_ + 1])
            fw.tt(fw.any2(), o[:, oc_, :], yg[:, oc_, t * 512:(t + 1) * 512], sgt[:, :], ALU.mult)
        fw.dma("gpsimd", bT[:, :, t * 512:(t + 1) * 512], o[:, :, :])
    fw.release(m)


NB = NT // 128
GSC = 128 ** -0.5
import os
GDN_STOP = int(os.environ.get('GDN_STOP', '9'))
BULK_STOP = int(os.environ.get('BULK_STOP', '9'))


def stage_gdn(fw, C, l):
    m = fw.mark()
    pv = C.projT.v.rearrange("(j p) t -> p j t", p=128)
    qkT = fw.alloc("qkT", [128, 12, NT], BF16)
    osum = fw.alloc("osum", [128, 4, NT])
    Rall = fw.alloc("Rall", [128, NT])
    Col = fw.alloc("Col", [128, NB, 32])
    gout = fw.alloc("gout", [128, 1])
    masks = fw.alloc("gmasks", [128, 4, 128])
    sel = fw.alloc("gsel", [128, 16, 128])
    fw.memset("vector", Rall[:, :], 0.0)
    fw.dma("sync", masks[:, :, :], C.c_gmask)
    fw.dma("sync", sel[:, :, :], C.c_gsel)
    fw.dma("sync", gout[:, :], C.g_delta_out[l].rearrange("(p o) -> p o", o=1))
    m2 = fw.mark()
    cw = fw.alloc("cw", [128, 5, 12])
    tb = [fw.alloc("tb%d" % i, [128, 128]) for i in range(5)]
    for i in range(5):
        load_colvec(fw, C, cw[:, i, :], C.conv_qkv[l, i], 12, tb[i])
    xc = [fw.alloc("xc%d" % i, [128, NT]) for i in range(1)]
    acc = [fw.alloc("acc%d" % i, [128, NT]) for i in range(1)]
    sq = fw.alloc("sq", [128, 1, 512], BF16)
    rstd = fw.alloc("rstd", [128, 512])
    tmp = fw.alloc("tmp", [128, 512])
    for c in range(12):
        x, a = xc[0], acc[0]
        fw.dma("sync", x[:, :], pv[:, c, :])
        fw.act(a[:, :], x[:, :], AF.Identity, scale=cw[:, 2, c:c + 1])
        for i in (0, 1, 3, 4):
            sh = i - 2
            for (off, L, smp) in SEQS:
                lo, hi = max(0, -sh), min(L, L - sh)
                eng = ("vector", "gpsimd")[(i + (off // LP)) % 2]
                fw.stt(eng, a[:, off + lo:off + hi], x[:, off + lo + sh:off + hi + sh], cw[:, i, c:c + 1], a[:, off + lo:off + hi], ALU.mult, ALU.add)
        fw.act(a[:, :], a[:, :], AF.Silu)
        if c < 8:
            for t in range(NT // 512):
                tok = slice(t * 512, (t + 1) * 512)
                ps = fw.ps()
                fw.act(sq[:, 0, :], a[:, tok], AF.Square)
                fw.matmul(ps[:, :], C.ones_bf[:, :], sq[:, 0, :])
                fw.act(tmp[:, :], ps[:, :], AF.Sqrt, scale=1.0, bias=C.eps_col[:, 0:1])
                fw.recip(rstd[:, :], tmp[:, :])
                if c < 4:
                    fw.stt("vector", qkT[:, c, tok], a[:, tok], GSC, rstd[:, :], ALU.mult, ALU.mult)
                else:
                    fw.tt("vector", qkT[:, c, tok], a[:, tok], rstd[:, :], ALU.mult)
        else:
            fw.copy("gpsimd", qkT[:, c, :], a[:, :])
    fw.release(m2)
    if GDN_STOP <= 1:
        fw.release(m)
        return
    m2 = fw.mark()
    Rall2 = fw.alloc("Rall2", [128, NT])
    al = fw.alloc("al", [128, NT])
    P = fw.alloc("P", [128, NT])
    tot = fw.alloc("tot", [128, NT // 64])
    rst = fw.alloc("rst", [128, NT])
    colp = fw.alloc("colp", [128, 4])
    fw.dma("sync", al[0:8, :], C.projT[R_MA:R_MA + 8, :])
    fw.dma("sync", Rall[32:40, :], C.projT[R_MA + 32:R_MA + 40, :])
    fw.dma("sync", rst[0:8, :], C.c_rst)
    fw.dma("sync", colp[0:8, 0:1], C.dt_bias[l].rearrange("d (h o) -> (d h) o", o=1))
    fw.dma("sync", colp[0:8, 1:2], C.a_log[l].rearrange("d (h o) -> (d h) o", o=1))
    fw.dma("sync", colp[0:8, 2:3], C.c_mdir)
    fw.memset("vector", colp[0:8, 3:4], 1.0)
    fw.act(colp[0:8, 1:2], colp[0:8, 1:2], AF.Exp)
    fw.ts("vector", colp[0:8, 1:2], colp[0:8, 1:2], -1.0, None, op0=ALU.mult)
    fw.act(al[0:8, :], al[0:8, :], AF.Exp, bias=colp[0:8, 0:1])
    fw.act(al[0:8, :], al[0:8, :], AF.Ln, bias=colp[0:8, 3:4])
    fw.ts("vector", al[0:8, :], al[0:8, :], colp[0:8, 1:2], None, op0=ALU.mult)
    fw.act(Rall[32:40, :], Rall[32:40, :], AF.Sigmoid)
    fw.scan(P[0:8, :], rst[0:8, :], al[0:8, :], 0.0)
    P3 = P[0:8, :].rearrange("r (k c) -> r k c", c=64)
    fw.copy("vector", tot[0:8, :], P3[:, :, 63])
    totb = tot[0:8, :].unsq(2).bc([8, NT // 64, 64])
    S3 = Rall2[0:8, :].rearrange("r (k c) -> r k c", c=64)
    a3 = al[0:8, :].rearrange("r (k c) -> r k c", c=64)
    G3 = Rall[0:8, :].rearrange("r (k c) -> r k c", c=64)
    fw.tt("vector", S3, totb, P3, ALU.subtract)
    fw.tt("vector", S3, S3, a3, ALU.add)
    fw.tt("vector", S3, S3, P3, ALU.subtract)
    fw.stt("vector", Rall[0:8, :], Rall2[0:8, :], colp[0:8, 2:3], P[0:8, :], ALU.mult, ALU.add)
    fw.tt("vector", S3, totb, G3, ALU.subtract)
    fw.act(Rall2[0:8, :], Rall2[0:8, :], AF.Exp)
    for b in range(NB):
        ps = fw.ps()
        fw.transpose(ps[:, 0:128], Rall[:, b * 128:(b + 1) * 128], C.ident[:, :])
        fw.transpose(ps[:, 128:256], Rall2[:, b * 128:(b + 1) * 128], C.ident[:, :])
        fw.copy("vector", Col[:, b, 0:8], ps[:, 0:8])
        fw.copy("vector", Col[:, b, 8:16], ps[:, 32:40])
        fw.copy("vector", Col[:, b, 24:32], ps[:, 128:136])
        fw.act(Col[:, b, 16:24], ps[:, 0:8], AF.Exp)
        fw.tt("vector", Col[:, b, 16:24], Col[:, b, 16:24], Col[:, b, 8:16], ALU.mult)
        fw.ts("vector", Col[:, b, 16:24], Col[:, b, 16:24], -1.0, None, op0=ALU.mult)
    fw.release(m2)
    NU = 2
    def mk(nm, dt=F32):
        return [[fw.alloc("%s_%d_%d" % (nm, h, u), [128, 128], dt) for u in range(NU)] for h in range(4)]
    TTb, QKb, qtb, ktb, bvb = mk("TT", F32), mk("QK", BF16), mk("qt", BF16), mk("kt", BF16), mk("bv", F32)
    ED = fw.alloc("ED", [128, 4, NU, 2])
    def mkh(nm, n, dt=F32):
        return [[fw.alloc("%s_%d_%d" % (nm, h, i), [128, 128], dt) for i in range(n)] for h in range(4)]
    E_h, ET_h, t_h, eg_h, Af_h, Bf_h, Pf_h = mkh("E", 1), mkh("ET", 1), mkh("t", 4), mkh("eg", 1), mkh("Af", 2), mkh("Bf", 2), mkh("Pf", 2)
    Sst = [fw.alloc("S%d" % h, [128, 128]) for h in range(4)]
    Sbf = [fw.alloc("Sb%d" % h, [128, 128], BF16) for h in range(4)]
    rhsb = [fw.alloc("rhs%d" % h, [128, 128]) for h in range(4)]
    vnb = [fw.alloc("vn%d" % h, [128, 128], BF16) for h in range(4)]

    def bulk(d, h, b, u):
        r = d * 4 + h
        blk = slice(b * 128, (b + 1) * 128)
        mS, mT, mI = (masks[:, 0, :], masks[:, 1, :], masks[:, 3, :]) if d == 0 else (masks[:, 1, :], masks[:, 0, :], masks[:, 2, :])
        gcol, bcol, dlcol = Col[:, b, r:r + 1], Col[:, b, 8 + r:9 + r], Col[:, b, 24 + r:25 + r]
        E, ET, eg, t_, Af, Bf, Pf = E_h[h][0], ET_h[h][0], eg_h[h][0], t_h[h], Af_h[h], Bf_h[h], Pf_h[h]
        bank = fw.psb[h]
        fw.matmul(bank[:, 0:128], sel[:, r, :], Rall[:, blk])
        fw.matmul(bank[:, 128:256], sel[:, 8 + r, :], Rall[:, blk])
        yield
        fw.ts("vector", E[:, :], bank[:, 0:128], gcol, 0.0, op0=ALU.subtract, op1=ALU.max)
        fw.ts("vector", ET[:, :], bank[:, 0:128], gcol, 0.0, op0=ALU.subtract, op1=ALU.min)
        yield
        fw.act(eg[:, :], bank[:, 0:128], AF.Exp)
        yield
        fw.tt("vector", t_[3][:, :], bank[:, 128:256], mT, ALU.mult)
        fw.matmul(bank[:, 256:384], qkT[:, 4 + h, blk], qkT[:, 4 + h, blk])
        fw.matmul(bank[:, 384:512], qkT[:, 4 + h, blk], qkT[:, h, blk])
        yield
        fw.act(E[:, :], E[:, :], AF.Exp, scale=-1.0)
        fw.act(ET[:, :], ET[:, :], AF.Exp)
        for hf_ in range(2):
            c_ = 64 * hf_ + (63 if d == 0 else 0)
            fw.copy("gpsimd", ED[:, h, u, hf_:hf_ + 1], eg[:, c_:c_ + 1])
        fw.tt("gpsimd", qtb[h][u][:, :], qkT[:, h, blk], eg[:, :], ALU.mult)
        yield
        fw.tt("vector", t_[0][:, :], bank[:, 256:384], E[:, :], ALU.mult)
        fw.tt("vector", t_[1][:, :], bank[:, 256:384], ET[:, :], ALU.mult)
        fw.tt("vector", E[:, :], bank[:, 384:512], ET[:, :], ALU.mult)
        yield
        fw.stt("vector", Af[0][:, :], t_[0][:, :], bcol, mS, ALU.mult, ALU.mult)
        fw.tt("gpsimd", Bf[0][:, :], t_[1][:, :], t_[3][:, :], ALU.mult)
        fw.tt("gpsimd", QKb[h][u][:, :], E[:, :], mI, ALU.mult)
        ptb = bank[:, :].bitcast(BF16)
        fw.transpose(ptb[:, 0:128], qkT[:, 4 + h, blk], C.ident_bf[:, :])
        fw.transpose(ptb[:, 128:256], qkT[:, 8 + h, blk], C.ident_bf[:, :])
        yield
        fw.tt("gpsimd", Pf[0][:, :], C.ident[:, :], Bf[0][:, :], ALU.subtract)
        fw.ts("vector", ktb[h][u][:, :], ptb[:, 0:128], dlcol, None, op0=ALU.mult)
        fw.ts("vector", bvb[h][u][:, :], ptb[:, 128:256], bcol, None, op0=ALU.mult)
        yield
        cur = 0
        pcur = 0
        for lev in range(5):
            nxt = 1 - cur
            fw.matmul(bank[:, 0:128], Bf[cur][:, :], Af[cur][:, :])
            if lev < 4:
                fw.matmul(bank[:, 128:256], Af[cur][:, :], Bf[cur][:, :])
            yield
            fw.copy("scalar", Af[nxt][:, :], bank[:, 0:128])
            if lev < 4:
                fw.copy("vector", Bf[nxt][:, :], bank[:, 128:256])
            yield
            fw.matmul(bank[:, 256:384], Af[nxt][:, :], Pf[pcur][:, :])
            yield
            if lev < 4:
                fw.tt("vector", Pf[1 - pcur][:, :], bank[:, 256:384], Pf[pcur][:, :], ALU.add)
            else:
                fw.tt("vector", TTb[h][u][:, :], bank[:, 256:384], Pf[pcur][:, :], ALU.add)
            yield
            cur = nxt
            pcur = 1 - pcur

    def recur(d, h, b, u, first_dir):
        r = d * 4 + h
        bank = fw.psb[4 + h]
        for hf in ((0, 1) if d == 0 else (1, 0)):
            lo = 64 * hf
            rows = slice(lo, lo + 64)
            tok = slice(b * 128 + lo, b * 128 + lo + 64)
            nbeg = Col[rows, b, 16 + r:17 + r]
            fw.matmul(bank[rows, 0:128], qkT[:, 4 + h, tok], Sbf[h][:, :])
            yield
            fw.stt("vector", rhsb[h][rows, :], bank[rows, 0:128], nbeg, bvb[h][u][rows, :], ALU.mult, ALU.add)
            yield
            fw.matmul(bank[rows, 128:256], TTb[h][u][rows, lo:lo + 64], rhsb[h][rows, :])
            yield
            fw.copy("scalar", vnb[h][rows, :], bank[rows, 128:256])
            yield
            fw.matmul(bank[:, 256:320], Sbf[h][:, :], qtb[h][u][:, lo:lo + 64], start=True, stop=False)
            fw.matmul(bank[:, 256:320], vnb[h][rows, :], QKb[h][u][rows, lo:lo + 64], start=False, stop=True)
            fw.matmul(bank[:, 384:512], ktb[h][u][rows, :], vnb[h][rows, :])
            yield
            fw.stt("vector", Sst[h][:, :], Sst[h][:, :], ED[:, h, u, hf:hf + 1], bank[:, 384:512], ALU.mult, ALU.add)
            if first_dir:
                fw.copy("scalar", osum[:, h, tok], bank[:, 256:320])
            else:
                fw.tt("vector", osum[:, h, tok], osum[:, h, tok], bank[:, 256:320], ALU.add)
            yield
            fw.copy("gpsimd", Sbf[h][:, :], Sst[h][:, :])
            yield

    def run_rr(gens):
        gens = list(gens)
        while gens:
            for g in list(gens):
                try:
                    next(g)
                except StopIteration:
                    gens.remove(g)

    jobs = []
    for si, (off, L, smp) in enumerate(SEQS):
        b0, b1 = off // 128, (off + L) // 128
        for d in range(2):
            blocks = list(range(b0, b1)) if d == 0 else list(range(b1 - 1, b0 - 1, -1))
            for n_, b in enumerate(blocks):
                jobs.append((si, smp, d, b, n_ == 0, n_ == len(blocks) - 1))
    run_rr([bulk(jobs[0][2], h, jobs[0][3], 0) for h in range(4)])
    for n_, (si, smp, d, b, first, last) in enumerate(jobs):
        u = n_ % NU
        if first:
            for h in range(4):
                if smp:
                    fw.dma("sync", Sst[h][:, :], C.sd[l, d, h])
                else:
                    fw.memset("vector", Sst[h][:, :], 0.0)
                fw.copy("gpsimd", Sbf[h][:, :], Sst[h][:, :])
        gens = [recur(d, h, b, u, d == 0) for h in range(4)]
        if n_ + 1 < len(jobs):
            nj_ = jobs[n_ + 1]
            gens += [bulk(nj_[2], h, nj_[3], (n_ + 1) % NU) for h in range(4)]
        run_rr(gens)
        if last and not smp:
            for h in range(4):
                fw.dma("gpsimd", C.o_sd[si, l, d, h], Sst[h][:, :])
    fw.release(m)
    m = fw.mark()
    qkT = fw.alloc("qkT", [128, 12, NT], BF16)
    osum = fw.alloc("osum", [128, 4, NT])
    Rall = fw.alloc("Rall", [128, NT])
    Col = fw.alloc("Col", [128, NB, 32])
    gout = fw.alloc("gout", [128, 1])
    zt = fw.alloc("zt", [128, NT])
    sq = fw.alloc("sq2", [128, 1, 512], BF16)
    rstd = fw.alloc("rstd2", [128, 512])
    tmp = fw.alloc("tmp2", [128, 512])
    oa = [fw.alloc("oa%d" % i, [128, 512], BF16) for i in range(2)]
    bT = C.brT.v[0].rearrange("(c p) t -> p c t", p=128)
    for h in range(4):
        fw.dma("sync", zt[:, :], pv[:, R_Z // 128 + h, :])
        fw.act(zt[:, :], zt[:, :], AF.Silu)
        for t in range(NT // 512):
            tok = slice(t * 512, (t + 1) * 512)
            ps = fw.ps()
            fw.act(sq[:, 0, :], osum[:, h, tok], AF.Square)
            fw.matmul(ps[:, :], C.ones_bf[:, :], sq[:, 0, :])
            fw.act(tmp[:, :], ps[:, :], AF.Sqrt, scale=1.0 / 128, bias=C.eps_col[:, 0:1])
            fw.recip(rstd[:, :], tmp[:, :])
            fw.stt("vector", tmp[:, :], osum[:, h, tok], gout[:, 0:1], rstd[:, :], ALU.mult, ALU.mult)
            o = oa[t % 2]
            fw.tt("gpsimd", o[:, :], tmp[:, :], zt[:, tok], ALU.mult)
            fw.dma("gpsimd", bT[:, h, tok], o[:, :])
    fw.release(m)


QSCALE = 96 ** -0.5
H_C = 8


def stage_mla(fw, C, l):
    m = fw.mark()
    pv = C.projT.v.rearrange("(j p) t -> p j t", p=128)
    Wqb = fw.alloc("Wqb", [128, 3, 768], BF16)
    Wqs = fw.alloc("Wqs", [128, 3, 256], BF16)
    Wkn = fw.alloc("Wkn", [128, 2, 512], BF16)
    Wv = fw.alloc("Wv", [128, 2, 512], BF16)
    gq = fw.alloc("gq", [128, 3])
    gkv = fw.alloc("gkv", [128, 2])
    rope = fw.alloc("rope", [128, 4, LS])
    m2 = fw.mark()
    C.stg = [fw.alloc("stg%d" % i, [128, 2304]) for i in range(2)]
    C.stg_i = 0
    tb = [fw.alloc("tb%d" % i, [128, 128]) for i in range(2)]
    load_colvec(fw, C, gq[:, :], C.g_q_a[l], 3, tb[0])
    load_colvec(fw, C, gkv[:, :], C.g_kv_a[l], 2, tb[1])
    for i in range(4):
        fw.dma("sync", rope[0:32, i, :], C.c_rope[i])
    wv = C.w_q_b[l].rearrange("(k p) f -> p k f", p=128)
    cast_load(fw, C, Wqb[:, :, :], wv, [128, 3, 768])
    st = C.stg[C.stg_i % 2]
    C.stg_i += 1
    sv = st[:, 0:768].rearrange("p (k f) -> p k f", f=256)
    for h in range(8):
        fw.dma("sync", sv[:, :, h * 32:h * 32 + 16], wv[:, :, h * 96 + 80:h * 96 + 96])
        fw.dma("sync", sv[:, :, h * 32 + 16:h * 32 + 32], wv[:, :, h * 96 + 64:h * 96 + 80])
    fw.copy("vector", Wqs[:, :, :], sv)
    wv = C.w_kv_b[l].rearrange("(k p) (h x) -> p k h x", p=128, x=128)
    st = C.stg[C.stg_i % 2]
    C.stg_i += 1
    sv = st[:, 0:2048].rearrange("p (k h x) -> p k h x", h=8, x=128)
    for k in range(2):
        fw.dma("sync", sv[:, k, :, :], wv[:, k, :, :])
    for k in range(2):
        fw.copy("vector", Wkn[:, k, :].rearrange("p (h x) -> p h x", x=64), sv[:, k, :, 0:64])
        fw.copy("gpsimd", Wv[:, k, :].rearrange("p (h x) -> p h x", x=64), sv[:, k, :, 64:128])
    fw.release(m2)
    NKMAX = LS + 256
    QTh = [fw.alloc("QT%d" % h, [128, LS], BF16) for h in range(8)]
    KT = fw.alloc("KT", [128, 8, NKMAX], BF16)
    Vaug = fw.alloc("Vaug", [128, NKMAX // 128, 8, 65], BF16)
    ckvb = fw.alloc("ckvb", [128, 2, NKMAX], BF16)
    qn = fw.alloc("qn", [128, 3, 512], BF16)
    krot = fw.alloc("krot", [128, NKMAX], BF16)
    qa = fw.alloc("qa", [128, 3, 512])
    sq = fw.alloc("sq", [128, 3, 512], BF16)
    rstd = fw.alloc("rstd", [128, 512])
    tmp = [fw.alloc("tmp%d" % i, [128, 512]) for i in range(3)]
    ckvf = fw.alloc("ckvf", [128, 2, 512])
    krA = fw.alloc("krA", [128, 512])
    krB = fw.alloc("krB", [128, 512])
    otok = [fw.alloc("otok%d" % i, [128, 288]) for i in range(2)]
    ctxt = fw.alloc("ctxt", [128, 288])
    pt = [fw.alloc("pt%d" % i, [128, 512], BF16) for i in range(3)]
    mx = fw.alloc("mx", [128, 2, 32])
    rl = fw.alloc("rl", [128, 2, 4])
    oc = fw.alloc("oc", [128, 4, 512])
    ocT = [fw.alloc("ocT%d" % i, [128, 4, 512], BF16) for i in range(2)]
    fw.memset("vector", KT[32:64, :, :], 0.0)
    fw.memset("vector", KT[32:33, :, :], 1.0)
    fw.memset("vector", Vaug[:, :, :, 64:65], 1.0)
    oti = 0
    for si, (off, L, smp) in enumerate(SEQS):
        koff = 256 if smp else 0
        nk = L + koff
        N = min(512, L)
        for h in range(8):
            fw.memset(fw.any2(), QTh[h][32:64, 0:L], 0.0)
        for t0 in range(0, L, N):
            tok = slice(off + t0, off + t0 + N)
            fw.dma("sync", qa[:, :, 0:N], pv[:, R_QA // 128:R_QA // 128 + 3, tok])
            rms_rstd(fw, C, [qa[:, c, 0:N] for c in range(3)], 384, N, sq, rstd, tmp[0])
            for c in range(3):
                fw.stt("vector", qn[:, c, 0:N], qa[:, c, 0:N], gq[:, c:c + 1], rstd[:, 0:N], ALU.mult, ALU.mult)
            for h in range(8):
                psA = fw.ps()
                for k in range(3):
                    fw.matmul(psA[64:128, 0:N], Wqb[:, k, h * 96:h * 96 + 64], qn[:, k, 0:N], start=(k == 0), stop=(k == 2))
                for k in range(3):
                    fw.matmul(psA[0:32, 0:N], Wqb[:, k, h * 96 + 64:h * 96 + 96], qn[:, k, 0:N], start=(k == 0), stop=(k == 2))
                fw.act(QTh[h][64:128, t0:t0 + N], psA[64:128, 0:N], AF.Copy, scale=QSCALE)
                if smp:
                    psB = fw.ps()
                    for k in range(3):
                        fw.matmul(psB[0:32, 0:N], Wqs[:, k, h * 32:h * 32 + 32], qn[:, k, 0:N], start=(k == 0), stop=(k == 2))
                    fw.tt("vector", tmp[1][0:32, 0:N], psA[0:32, 0:N], rope[0:32, 2, t0:t0 + N], ALU.mult)
                    fw.tt("vector", tmp[2][0:32, 0:N], psB[0:32, 0:N], rope[0:32, 3, t0:t0 + N], ALU.mult)
                    fw.tt("gpsimd", QTh[h][0:32, t0:t0 + N], tmp[1][0:32, 0:N], tmp[2][0:32, 0:N], ALU.add)
                else:
                    fw.act(QTh[h][0:32, t0:t0 + N], psA[0:32, 0:N], AF.Copy, scale=QSCALE)
            fw.dma("sync", qa[:, 0:2, 0:N], pv[:, R_KVA // 128:R_KVA // 128 + 2, tok])
            rms_rstd(fw, C, [qa[:, c, 0:N] for c in range(2)], 256, N, sq, rstd, tmp[0])
            for c in range(2):
                fw.stt("vector", ckvf[:, c, 0:N], qa[:, c, 0:N], gkv[:, c:c + 1], rstd[:, 0:N], ALU.mult, ALU.mult)
                fw.copy("gpsimd", ckvb[:, c, koff + t0:koff + t0 + N], ckvf[:, c, 0:N])
            fw.dma("sync", krA[0:32, 0:N], C.projT[R_MA + 64:R_MA + 96, tok])
            if smp:
                fw.dma("sync", krB[0:32, 0:N], C.projT[R_MB + 64:R_MB + 96, tok])
                fw.tt("vector", tmp[1][0:32, 0:N], krA[0:32, 0:N], rope[0:32, 0, t0:t0 + N], ALU.mult)
                fw.tt("vector", tmp[2][0:32, 0:N], krB[0:32, 0:N], rope[0:32, 1, t0:t0 + N], ALU.mult)
                fw.tt("gpsimd", krot[0:32, koff + t0:koff + t0 + N], tmp[1][0:32, 0:N], tmp[2][0:32, 0:N], ALU.add)
            else:
                fw.copy("gpsimd", krot[0:32, t0:t0 + N], krA[0:32, 0:N])
                for tb_ in range(N // 128):
                    ot = otok[oti % 2]
                    oti += 1
                    ps = fw.ps()
                    for c in range(2):
                        fw.transpose(ps[:, c * 128:(c + 1) * 128], ckvf[:, c, tb_ * 128:(tb_ + 1) * 128], C.ident[:, :])
                    fw.transpose(ps[:, 256:288], krA[0:32, tb_ * 128:(tb_ + 1) * 128], C.ident[0:32, 0:32])
                    fw.copy(fw.evac(), ot[:, :], ps[:, 0:288])
                    fw.dma("gpsimd", C.o_ckv[si, l, t0 + tb_ * 128:t0 + (tb_ + 1) * 128, :], ot[:, 0:256])
                    fw.dma("gpsimd", C.o_kr[si, l, t0 + tb_ * 128:t0 + (tb_ + 1) * 128, :], ot[:, 256:288])
        if smp:
            for tb_ in range(2):
                fw.dma("sync", ctxt[:, 0:256], C.cckv[l, tb_ * 128:(tb_ + 1) * 128, :])
                fw.dma("sync", ctxt[:, 256:288], C.ckr[l, tb_ * 128:(tb_ + 1) * 128, :])
                ps = fw.ps()
                for c in range(2):
                    fw.transpose(ps[:, c * 128:(c + 1) * 128], ctxt[:, c * 128:(c + 1) * 128], C.ident[:, :])
                    fw.copy(fw.evac(), ckvb[:, c, tb_ * 128:(tb_ + 1) * 128], ps[:, c * 128:(c + 1) * 128])
                fw.transpose(ps[0:32, 256:384], ctxt[:, 256:288], C.ident[:, :])
                fw.copy("vector", krot[0:32, tb_ * 128:(tb_ + 1) * 128], ps[0:32, 256:384])
        for h in range(8):
            fw.copy(fw.any2(), KT[0:32, h, 0:nk], krot[0:32, 0:nk])
            for k0 in range(0, nk, 512):
                n = min(512, nk - k0)
                ps = fw.ps()
                for k in range(2):
                    fw.matmul(ps[64:128, 0:n], Wkn[:, k, h * 64:(h + 1) * 64], ckvb[:, k, k0:k0 + n], start=(k == 0), stop=(k == 1))
                fw.copy(fw.evac(), KT[64:128, h, k0:k0 + n], ps[64:128, 0:n])
        for kc in range(nk // 128):
            ps = fw.ps()
            for k in range(2):
                fw.matmul(ps[:, :], ckvb[:, k, kc * 128:(kc + 1) * 128], Wv[:, k, :], start=(k == 0), stop=(k == 1))
            fw.copy(fw.evac(), Vaug[:, kc, :, 0:64], ps[:, :].rearrange("p (h x) -> p h x", x=64))
        nj = N // 128
        nkb = (nk + 511) // 512
        nkc = nk // 128
        for t0 in range(0, L, N):
            fw.ps_set = [4, 5, 6, 7]

            def gen_max(h):
                for j in range(nj):
                    q0 = t0 + j * 128
                    for kb in range(nkb):
                        n = min(512, nk - kb * 512)
                        ps = fw.ps()
                        fw.matmul(ps[:, 0:n], QTh[h][:, q0:q0 + 128], KT[:, h, kb * 512:kb * 512 + n])
                        fw.rmax(mx[:, h % 2, j * 8 + kb:j * 8 + kb + 1], ps[:, 0:n])
                        yield
                    if nkb > 1:
                        fw.rmax(mx[:, h % 2, j * 8 + 7:j * 8 + 8], mx[:, h % 2, j * 8:j * 8 + nkb])
                        mcol = mx[:, h % 2, j * 8 + 7:j * 8 + 8]
                    else:
                        mcol = mx[:, h % 2, j * 8:j * 8 + 1]
                    ps = fw.ps()
                    fw.matmul(ps[32:33, 0:128], mcol, C.ident[:, :])
                    fw.act(QTh[h][32:33, q0:q0 + 128], ps[32:33, 0:128], AF.Copy, scale=-1.0)
                    yield

            def gen_main(h):
                acc = [fw.psb[j] for j in range(nj)]
                for kc in range(nkc):
                    ST = fw.ps()
                    fw.matmul(ST[:, 0:N], KT[:, h, kc * 128:(kc + 1) * 128], QTh[h][:, t0:t0 + N])
                    p = pt[kc % 3]
                    fw.act(p[:, 0:N], ST[:, 0:N], AF.Exp)
                    for j in range(nj):
                        fw.matmul(acc[j][:, 0:65], p[:, j * 128:(j + 1) * 128], Vaug[:, kc, h, :], start=(kc == 0), stop=(kc == nkc - 1))
                    yield
                for j in range(nj):
                    fw.recip(rl[:, h % 2, j:j + 1], acc[j][:, 64:65])
                    fw.ts("vector", oc[:, j, h * 64:(h + 1) * 64], acc[j][:, 0:64], rl[:, h % 2, j:j + 1], None, op0=ALU.mult)
                yield

            def rr(gens):
                gens = list(gens)
                while gens:
                    for g in list(gens):
                        try:
                            next(g)
                        except StopIteration:
                            gens.remove(g)

            rr([gen_max(0)])
            for h in range(8):
                rr([gen_main(h)] + ([gen_max(h + 1)] if h < 7 else []))
            fw.ps_set = list(range(8))
            o = ocT[(t0 // N) % 2]
            for j in range(nj):
                ps = fw.ps()
                for c in range(4):
                    fw.transpose(ps[:, c * 128:(c + 1) * 128], oc[:, j, c * 128:(c + 1) * 128], C.ident[:, :])
                fw.copy(fw.evac(), o[:, :, j * 128:(j + 1) * 128], ps[:, :].rearrange("p (c t) -> p c t", t=128))
            fw.dma("gpsimd", C.brT.v[2].rearrange("(c p) t -> p c t", p=128)[:, :, off + t0:off + t0 + N], o[:, :, 0:N])
    fw.release(m)


def stage_merge(fw, C, l):
    m = fw.mark()
    Wbr = fw.alloc("Wbr", [128, 12, 1024], BF16)
    Wout = fw.alloc("Wout", [128, 8, 1024], BF16)
    m2 = fw.mark()
    C.stg = [fw.alloc("stg%d" % i, [128, 2048]) for i in range(2)]
    C.stg_i = 0
    for br in range(3):
        wv = C.w_branch[l, br].rearrange("(k p) f -> p k f", p=128)
        for o in range(0, 1024, 512):
            cast_load(fw, C, Wbr[:, br * 4:(br + 1) * 4, o:o + 512], wv[:, :, o:o + 512], [128, 4, 512])
    wv = C.w_out[l].rearrange("(k p) f -> p k f", p=128)
    for o in range(0, 1024, 256):
        cast_load(fw, C, Wout[:, :, o:o + 256], wv[:, :, o:o + 256], [128, 8, 256])
    fw.release(m2)
    xt = [fw.alloc("xt%d" % i, [128, 8, 512]) for i in range(2)]
    xo = [fw.alloc("xo%d" % i, [128, 8, 512]) for i in range(1)]
    brt = [fw.alloc("brt%d" % i, [128, 12, 512], BF16) for i in range(2)]
    gl = [fw.alloc("gl%d" % i, [128, 8, 512]) for i in range(2)]
    merged = fw.alloc("merged", [128, 8, 512])
    mergedb = fw.alloc("mergedb", [128, 8, 512], BF16)
    sg = [fw.alloc("sg%d" % i, [128, 512]) for i in range(2)]
    tp = [fw.alloc("tp%d" % i, [128, 512]) for i in range(2)]
    xTv = C.xT.v.rearrange("(c p) t -> p c t", p=128)
    pv = C.projT.v.rearrange("(j p) t -> p j t", p=128)
    bv = C.brT.v.rearrange("b (k p) t -> b p k t", p=128)
    gi = 0
    for t in range(NT // 512):
        n = 0 if t < 2 else 1
        tok = slice(t * 512, (t + 1) * 512)
        x = xt[t % 2]
        o = xo[0]
        bt = brt[t % 2]
        fw.dma("sync", x[:, :, :], xTv[:, :, tok])
        for br in range(3):
            fw.dma("sync", bt[:, br * 4:(br + 1) * 4, :], bv[br, :, :, tok])
        for br in range(3):
            g = gl[gi % 2]
            gi += 1
            fw.dma("sync", g[:, :, :], pv[:, R_GATE // 128 + br * 8:R_GATE // 128 + br * 8 + 8, tok])
            for dc in range(8):
                ps = fw.ps()
                for k in range(4):
                    fw.matmul(ps[:, :], Wbr[:, br * 4 + k, dc * 128:(dc + 1) * 128], bt[:, br * 4 + k, :], start=(k == 0), stop=(k == 3))
                s = sg[dc % 2]
                fw.act(s[:, :], g[:, dc, :], AF.Sigmoid)
                if br == 0:
                    fw.tt("vector", merged[:, dc, :], ps[:, :], s[:, :], ALU.mult)
                else:
                    tq = tp[dc % 2]
                    fw.tt("vector", tq[:, :], ps[:, :], s[:, :], ALU.mult)
                    if br == 1:
                        fw.tt("gpsimd", merged[:, dc, :], merged[:, dc, :], tq[:, :], ALU.add)
                    else:
                        fw.tt("gpsimd", mergedb[:, dc, :], merged[:, dc, :], tq[:, :], ALU.add)
        for ec in range(8):
            ps = fw.ps()
            for d in range(8):
                fw.matmul(ps[:, :], Wout[:, d, ec * 128:(ec + 1) * 128], mergedb[:, d, :], start=(d == 0), stop=(d == 7))
            fw.stt("vector", o[:, ec, :], ps[:, :], C.mG_m[:, ec, n:n + 1], x[:, ec, :], ALU.mult, ALU.add)
        fw.dma("gpsimd", xTv[:, :, tok], o[:, :, :])
    fw.release(m)


def stage_ffn(fw, C, l):
    m = fw.mark()
    Wup = fw.alloc("Wup", [128, 8, 2 * D_FF], BF16)
    Wdn = fw.alloc("Wdn", [128, 22, 1024], BF16)
    cw = fw.alloc("cw", [128, 3, 44])
    cb = fw.alloc("cb", [128, 44])
    m2 = fw.mark()
    C.stg = [fw.alloc("stg%d" % i, [128, 1024]) for i in range(4)]
    C.stg_i = 0
    wv = C.w_ffn_up[l].rearrange("(k p) f -> p k f", p=128)
    WupB = split_cols(Wup, 44, 128)
    WdnB = [Buf("Wdn_%d" % j, Wdn.t[:, j, :]) for j in range(22)]
    for j in range(22):
        for ch in (j, 22 + j):
            cast_load(fw, C, WupB[ch][:, :, :], wv[:, :, ch * 128:(ch + 1) * 128], [128, 8, 128])
    wv = C.w_ffn_down[l].rearrange("(j p) e -> p j e", p=128)
    for j in range(22):
        cast_load(fw, C, WdnB[j][:, :], wv[:, j, :], [128, 1024])
    tb = [fw.alloc("tb%d" % i, [128, 128]) for i in range(4)]
    for i in range(3):
        load_colvec(fw, C, cw[:, i, :], C.conv_ffn[l, i], 44, tb[i])
    load_colvec(fw, C, cb[:, :], C.b_conv_ffn[l], 44, tb[3])
    fw.release(m2)
    W = 258
    xh = [fw.alloc("xh%d" % i, [128, 8, W]) for i in range(2)]
    sq = fw.alloc("sq", [128, 8, W], BF16)
    h = [fw.alloc("h%d" % i, [128, 8, W], BF16) for i in range(2)]
    rstd = fw.alloc("rstd", [128, W])
    tmp = [fw.alloc("tmp%d" % i, [128, W]) for i in range(2)]
    actt = fw.alloc("actt", [128, 22, 256], BF16)
    ga = [fw.alloc("ga%d" % i, [128, 256]) for i in range(2)]
    gb = [fw.alloc("gb%d" % i, [128, 256]) for i in range(2)]
    va = [fw.alloc("va%d" % i, [128, 256]) for i in range(2)]
    vb = [fw.alloc("vb%d" % i, [128, 256]) for i in range(2)]
    xo = [fw.alloc("xo%d" % i, [128, 8, 256]) for i in range(2)]
    xTv = C.xT.v.rearrange("(c p) t -> p c t", p=128)
    ti = 0
    for (off, L, n) in SEQS:
        for t0 in range(0, L, 256):
            x = xh[ti % 2]
            hh = h[ti % 2]
            o = xo[ti % 2]
            ti += 1
            lo = 1 if t0 == 0 else 0
            hi = W - 1 if t0 + 256 >= L else W
            if lo == 1:
                fw.memset("gpsimd", x[:, :, 0:1], 0.0)
            else:
                fw.copy("gpsimd", x[:, :, 0:1], xprev[:, :, W - 2:W - 1])
            if hi == W - 1:
                fw.memset("gpsimd", x[:, :, W - 1:W], 0.0)
            fw.dma("sync", x[:, :, 1:hi], xTv[:, :, off + t0:off + t0 - 1 + hi])
            xprev = x
            rms_rstd(fw, C, [x[:, c, :] for c in range(8)], D, W, sq, rstd, tmp[0])
            for c in range(8):
                tm = tmp[c % 2]
                fw.tt(("vector", "gpsimd")[c % 2], tm[:, :], x[:, c, :], rstd[:, :], ALU.mult)
                fw.act(hh[:, c, :], tm[:, :], AF.Identity, scale=C.mA_f[:, c, n:n + 1], bias=C.mB_f[:, c, n:n + 1])
            if lo == 1:
                fw.memset("gpsimd", hh[:, :, 0:1], 0.0)
            if hi == W - 1:
                fw.memset("gpsimd", hh[:, :, W - 1:W], 0.0)
            for j in range(22):
                res = []
                for (which, ta, tb) in ((0, ga[j % 2], gb[j % 2]), (1, va[j % 2], vb[j % 2])):
                    ch = which * 22 + j
                    ps = fw.ps()
                    for k in range(8):
                        fw.matmul(ps[:, 0:W], WupB[ch][:, k, :], hh[:, k, :], start=(k == 0), stop=(k == 7))
                    fw.act(ta[:, :], ps[:, 1:257], AF.Identity, scale=cw[:, 1, ch:ch + 1], bias=cb[:, ch:ch + 1])
                    fw.stt("vector", tb[:, :], ps[:, 0:256], cw[:, 0, ch:ch + 1], ta[:, :], ALU.mult, ALU.add)
                    fw.stt("vector", ta[:, :], ps[:, 2:258], cw[:, 2, ch:ch + 1], tb[:, :], ALU.mult, ALU.add)
                    res.append(ta)
                fw.act(gb[j % 2][:, :], res[0][:, :], AF.Silu)
                fw.tt("gpsimd", actt[:, j, :], gb[j % 2][:, :], res[1][:, :], ALU.mult)
            for ec in range(8):
                ps = fw.ps()
                for j in range(22):
                    fw.matmul(ps[:, 0:256], WdnB[j][:, ec * 128:(ec + 1) * 128], actt[:, j, :], start=(j == 0), stop=(j == 21))
                fw.stt("vector", o[:, ec, :], ps[:, 0:256], C.mG_f[:, ec, n:n + 1], x[:, ec, 1:257], ALU.mult, ALU.add)
            fw.dma("gpsimd", xTv[:, :, off + t0:off + t0 + 256], o[:, :, :])
    fw.release(m)


def stage_final(fw, C):
    m = fw.mark()
    gB = fw.alloc("gB", [128, 1024])
    fw.dma("sync", gB[:, :], C.g_final.unsq(0).pbc(128).rearrange("p a f -> p (a f)"))
    xt = [fw.alloc("xt%d" % i, [128, 8, 128]) for i in range(2)]
    yt = [fw.alloc("yt%d" % i, [128, 1024]) for i in range(2)]
    junk = fw.alloc("junk", [128, 512])
    ss = [fw.alloc("ss%d" % i, [128, 4]) for i in range(2)]
    xTv = C.xT.v.rearrange("(c p) t -> p c t", p=128)
    for tt in range(NT // 128):
        x = xt[tt % 2]
        y = yt[tt % 2]
        s = ss[tt % 2]
        fw.dma("sync", x[:, :, :], xTv[:, :, tt * 128:(tt + 1) * 128])
        pss = []
        for half in range(2):
            ps = fw.ps()
            pss.append(ps)
            for c in range(4):
                fw.transpose(ps[:, c * 128:(c + 1) * 128], x[:, half * 4 + c, :], C.ident[:, :])
            fw.act(junk[:, :], ps[:, :], AF.Square, accum_out=s[:, half:half + 1])
        fw.tt("vector", s[:, 2:3], s[:, 0:1], s[:, 1:2], ALU.add)
        fw.act(s[:, 3:4], s[:, 2:3], AF.Sqrt, scale=1.0 / D, bias=C.eps_col[:, 0:1])
        fw.recip(s[:, 2:3], s[:, 3:4])
        for half in range(2):
            fw.stt("vector", y[:, half * 512:(half + 1) * 512], pss[half][:, :], s[:, 2:3], gB[:, half * 512:(half + 1) * 512], ALU.mult, ALU.mult)
        fw.dma("gpsimd", C.y_tok[tt * 128:(tt + 1) * 128, :], y[:, :])
    fw.release(m)


WEIGHT_NAMES = ["w_mod", "b_mod", "g_norm_mix", "g_norm_ffn", "w_in", "conv_qkv", "a_log", "dt_bias", "g_delta_out",
                "s5_lam_re", "s5_lam_im", "s5_log_dt", "s5_b_re", "s5_b_im", "s5_c_re", "s5_c_im", "s5_d", "w_glu", "b_glu",
                "g_q_a", "w_q_b", "g_kv_a", "w_kv_b", "w_branch", "w_out", "w_ffn_up", "conv_ffn", "b_conv_ffn", "w_ffn_down",
                "g_final"]


def build(shapes, depth=DEPTH, dbg=False, skip_mixers=False, stages=("mla", "s5", "gdn", "dense")):
    nc = bass.Bass("TRN2", target_bir_lowering=False)
    fw = FW(nc)
    C = Ctx()
    for name, shp in shapes.items():
        setattr(C, name, fw.dram(name, shp, F32, kind="ExternalInput").v)
    kind = "ExternalOutput" if dbg else "Internal"
    C.y_tok = fw.dram("y_tok", [NT, D], F32, kind="ExternalOutput").v
    C.o_sd = fw.dram("o_sd", [NPS, DEPTH, 2, 4, 128, 128], F32, kind="ExternalOutput").v
    C.o_s5re = fw.dram("o_s5re", [NPS, DEPTH, 2, 32, 64], F32, kind="ExternalOutput").v
    C.o_s5im = fw.dram("o_s5im", [NPS, DEPTH, 2, 32, 64], F32, kind="ExternalOutput").v
    C.o_ckv = fw.dram("o_ckv", [NPS, DEPTH, LP, 256], F32, kind="ExternalOutput").v
    C.o_kr = fw.dram("o_kr", [NPS, DEPTH, LP, 32], F32, kind="ExternalOutput").v
    C.xT = fw.dram("xT", [D, NT], F32, kind=kind)
    C.projT = fw.dram("projT", [PW, NT], F32, kind=kind)
    C.brT = fw.dram("brT", [3, 512, NT], BF16, kind=kind)
    if dbg:
        C.xmid = fw.dram("xmid", [D, NT], F32, kind=kind)
    C.ident = fw.alloc("ident", [128, 128])
    C.ones_bf = fw.alloc("ones_bf", [128, 128], BF16)
    C.eps_col = fw.alloc("eps_col", [128, 1])
    C.cT = fw.alloc("cT", [128, 8, 2])
    for nm in ("mA_m", "mB_m", "mG_m", "mA_f", "mB_f", "mG_f"):
        setattr(C, nm, fw.alloc(nm, [128, 8, 2]))
    fw.dma("sync", C.ident[:, :], C.c_ident)
    C.halfpi = fw.alloc("halfpi", [128, 1])
    fw.memset("vector", C.halfpi[:, :], math.pi / 2)
    C.svec = fw.alloc("svec", [128, 17])
    C.maskR = fw.alloc("maskR", [128, 2])
    C.maskS = fw.alloc("maskS", [128, 2])
    C.maskQ = fw.alloc("maskQ", [128, 4])
    fw.dma("sync", C.svec[:, :], C.c_svec)
    fw.dma("sync", C.maskR[:, :], C.c_maskR)
    fw.dma("sync", C.maskS[:, :], C.c_maskS)
    fw.dma("sync", C.maskQ[:, :], C.c_maskQ)
    C.ident_bf = fw.alloc("ident_bf", [128, 128], BF16)
    fw.copy("vector", C.ident_bf[:, :], C.ident[:, :])
    C.rm3 = fw.alloc("rm3", [128, 1])
    fw.dma("sync", C.rm3[:, :], C.c_rm3)
    fw.memset("vector", C.ones_bf[:, :], 1.0)
    fw.memset("vector", C.eps_col[:, :], EPS)
    tb0 = fw.alloc("tb0", [128, 128])
    tb1 = fw.alloc("tb1", [128, 8])
    for n in range(2):
        load_colvec(fw, C, tb1[:, :], C.cc[n], 8, tb0)
        fw.copy("vector", C.cT[:, :, n], tb1[:, :])
    fw.act(C.cT[:, :, :], C.cT[:, :, :], AF.Silu)
    if skip_mixers:
        m = fw.mark()
        z = fw.alloc("z", [128, 4, 512], BF16)
        zf = fw.alloc("zf", [128, 4, 512])
        bv = C.brT.v.rearrange("b (k p) t -> b p k t", p=128)
        dv = C.dbg_br.rearrange("b (k p) t -> b p k t", p=128)
        for br in range(3):
            for t in range(NT // 512):
                fw.dma("sync", zf[:, :, :], dv[br, :, :, t * 512:(t + 1) * 512])
                fw.copy("vector", z[:, :, :], zf[:, :, :])
                fw.dma("gpsimd", bv[br, :, :, t * 512:(t + 1) * 512], z[:, :, :])
        fw.release(m)
    stage_in(fw, C)
    for l in range(depth):
        stage_mods(fw, C, l)
        stage_inproj(fw, C, l)
        if "mla" in stages:
            stage_mla(fw, C, l)
        if "s5" in stages:
            stage_s5(fw, C, l)
        if "gdn" in stages:
            stage_gdn(fw, C, l)
        if "dense" in stages:
            stage_merge(fw, C, l)
            if dbg and l == 0:
                fw.dma("sync", C.xmid.v, C.xT.v)
            stage_ffn(fw, C, l)
    stage_final(fw, C)
    fw.emit()
    return nc, fw


def host_consts():
    t = np.arange(LS)
    row = (t // 64).astype(np.float32)
    col = (t % 64).astype(np.float32)
    inv = (1.0 / (10000.0 ** (np.arange(8, dtype=np.float32) / 8))).astype(np.float32)
    ang = np.concatenate([row[:, None] * inv, col[:, None] * inv], axis=-1).astype(np.float32)
    cos, sin = np.cos(ang).astype(np.float32).T, np.sin(ang).astype(np.float32).T
    cs1 = np.concatenate([cos, cos], 0)
    cs2 = np.concatenate([-sin, sin], 0)
    rope = np.stack([cs1, cs2, cs1 * np.float32(QSCALE), cs2 * np.float32(QSCALE)]).astype(np.float32)
    p = np.arange(128)
    svec = np.tile(np.arange(17, dtype=np.float32)[None, :], (128, 1))
    maskR = (((p // 16) % 2)[:, None] == np.arange(2)[None, :]).astype(np.float32)
    maskS = ((p // 64)[:, None] == np.arange(2)[None, :]).astype(np.float32)
    maskQ = ((p // 32)[:, None] == np.arange(4)[None, :]).astype(np.float32)
    rm3 = (p >= 96).astype(np.float32)[:, None]
    cc_, ee_ = np.meshgrid(p, p, indexing="ij")
    same = (cc_ // 64) == (ee_ // 64)
    gmask = np.stack([same & (cc_ > ee_), same & (cc_ < ee_), same & (cc_ >= ee_), same & (cc_ <= ee_)], axis=1).astype(np.float32)
    gsel = np.zeros((128, 16, 128), np.float32)
    for r_ in range(8):
        gsel[r_, r_, :] = 1.0
        gsel[32 + r_, 8 + r_, :] = 1.0
    rst = np.ones((8, NT), np.float32)
    rst[:, ::64] = 0.0
    mdir = (np.arange(8) >= 4).astype(np.float32)[:, None]
    return {"c_ident": np.eye(128, dtype=np.float32), "c_rope": rope, "c_svec": svec, "c_maskR": maskR,
            "c_maskS": maskS, "c_maskQ": maskQ, "c_rm3": rm3, "c_gmask": gmask, "c_gsel": gsel, "c_rst": rst,
            "c_mdir": mdir}


def make_in_maps(inputs):
    f = lambda a: np.ascontiguousarray(np.asarray(a, dtype=np.float32))
    W = {k: f(inputs[k]) for k in WEIGHT_NAMES}
    consts = host_consts()
    xp = f(inputs["x_prompt"])
    xs = f(inputs["x_sample"])
    maps = []
    for c in range(8):
        d = dict(W)
        d.update(consts)
        d["x_tok"] = np.concatenate([xp[4 * c:4 * c + 4].reshape(NPS * LP, D), xs[c]], axis=0)
        d["cc"] = np.stack([f(inputs["c_ctx"]), f(inputs["c"])[c]], axis=0)
        d["sd"] = f(inputs["state_delta"])[c]
        d["s5re"] = f(inputs["state_s5_re"])[c]
        d["s5im"] = f(inputs["state_s5_im"])[c]
        d["cckv"] = f(inputs["cache_ckv"])[c]
        d["ckr"] = f(inputs["cache_krope"])[c]
        maps.append(d)
    return maps


def kernel(**inputs):
    maps = make_in_maps(inputs)
    shapes = {k: list(v.shape) for k, v in maps[0].items()}
    nc, fw = build(shapes)
    res = run_bass_kernel_spmd(nc, maps, core_ids=list(range(8)))
    R = res.results
    y_prompt = np.stack([R[c]["y_tok"][:NPS * LP].reshape(NPS, LP, D) for c in range(8)]).reshape(32, LP, D)
    y_sample = np.stack([R[c]["y_tok"][NPS * LP:] for c in range(8)])
    cat = lambda k: np.concatenate([np.asarray(R[c][k]) for c in range(8)], axis=0).astype(np.float32)
    return (y_prompt.astype(np.float32), y_sample.astype(np.float32), cat("o_sd"), cat("o_s5re"), cat("o_s5im"), cat("o_ckv"), cat("o_kr"))
```
